# Optimizing a Trainium2 kernel written in Bass

```python
import jax, jax.numpy as jnp
from jax import lax
import numpy as np

D_MODEL = 1024
BATCH = 8
SEQ = 4096
DEPTH = 1
DEC_BATCH = 32
DEC_SEQ = 16
PAST_LEN = 1024

CHUNK = 64

D_CONV = 512
CONV_W = 3

RWKV_HEADS = 8
HEAD_DIM = 64
D_RWKV = RWKV_HEADS * HEAD_DIM
LORA_DECAY = 64
LORA_AAA = 64
LORA_GATE = 128
GN_EPS = 64e-5

OFF_RKV = 3 * D_CONV
OFF_GATE = OFF_RKV + 3 * D_RWKV
D_IN = OFF_GATE + 2 * D_MODEL

PEER_HEADS = 8
PEER_KEYS = 128
N_EXPERTS = PEER_KEYS * PEER_KEYS
PEER_QDIM = 256
PEER_HALF = PEER_QDIM // 2
PEER_TOPK = 16
PEER_BLOCK = 128

RMS_EPS = 1e-6

kernel_name = "hybrid_conv_rwkv7_peer_stream_step"


def rms_norm(x, g):
    xf = x.astype(jnp.float32)
    y = xf * lax.rsqrt(jnp.mean(xf * xf, axis=-1, keepdims=True) + RMS_EPS)
    return (y * g.astype(jnp.float32)).astype(x.dtype)


def wkv7_scan(s0, r, w, k, v, a, b):
    def step(S, inp):
        rt, wt, kt, vt, at, bt = inp
        sa = jnp.einsum('bhvk,bhk->bhv', S, at)
        S = S * wt[:, :, None, :] + sa[..., None] * bt[:, :, None, :] + vt[..., None] * kt[:, :, None, :]
        return S, jnp.einsum('bhvk,bhk->bhv', S, rt)
    seq = tuple(jnp.swapaxes(t, 0, 1) for t in (r, w, k, v, a, b))
    s_fin, y = lax.scan(step, s0, seq)
    return jnp.swapaxes(y, 0, 1), s_fin


def hybrid_mixer(xn, conv_state, shift_state, wkv_state, w_in, conv_w, mu_rkv, mu_wag,
                 w0, w1, w2, a0, a1, a2, g1, g2, k_k, k_a, r_k, gn_w, gn_b, w_pa, w_pb, w_o):
    bsz, T, _ = xn.shape
    dt = xn.dtype
    z = xn @ w_in
    zb = z[..., 0:D_CONV]
    zc = z[..., D_CONV:2 * D_CONV]
    zh = z[..., 2 * D_CONV:3 * D_CONV]
    zrkv = z[..., OFF_RKV:OFF_GATE]
    zga = z[..., OFF_GATE:OFF_GATE + D_MODEL]
    zgb = z[..., OFF_GATE + D_MODEL:]

    u = zc * zh
    u_full = jnp.concatenate([conv_state.astype(dt), u], axis=1)
    conv = sum(conv_w[j] * u_full[:, j:j + T] for j in range(CONV_W))
    y_a = zb * conv
    new_conv = u_full[:, T:]

    prev_row = shift_state.astype(dt)
    xprev = jnp.concatenate([prev_row[:, None], xn[:, :-1]], axis=1)
    zrkv_prev = jnp.concatenate([(prev_row @ w_in[:, OFF_RKV:OFF_GATE])[:, None], zrkv[:, :-1]], axis=1)
    zs = zrkv + mu_rkv * (zrkv_prev - zrkv)
    dx = xprev - xn
    xw = xn + dx * mu_wag[0]
    xa = xn + dx * mu_wag[1]
    xg = xn + dx * mu_wag[2]
    w_raw = -jax.nn.softplus(-(w0 + jnp.tanh(xw @ w1) @ w2)) - 0.5
    decay = jnp.exp(-jnp.exp(w_raw.astype(jnp.float32)))
    a = jax.nn.sigmoid((a0 + (xa @ a1) @ a2).astype(jnp.float32))
    g = jax.nn.sigmoid(xg @ g1) @ g2

    hs = (bsz, T, RWKV_HEADS, HEAD_DIM)
    r = zs[..., 0:D_RWKV].astype(jnp.float32).reshape(hs)
    k = zs[..., D_RWKV:2 * D_RWKV].astype(jnp.float32)
    v = zs[..., 2 * D_RWKV:].astype(jnp.float32).reshape(hs)
    kk = (k * k_k.astype(jnp.float32)).reshape(hs)
    kk = kk / jnp.maximum(jnp.sqrt(jnp.sum(kk * kk, axis=-1, keepdims=True)), 1e-12)
    k = (k * (1.0 + (a - 1.0) * k_a.astype(jnp.float32))).reshape(hs)
    a = a.reshape(hs)
    w = decay.reshape(hs)
    y, s_fin = wkv7_scan(wkv_state.astype(jnp.float32), r, w, k, v, -kk, kk * a)
    mean = jnp.mean(y, axis=-1, keepdims=True)
    var = jnp.mean(jnp.square(y - mean), axis=-1, keepdims=True)
    y = ((y - mean) * lax.rsqrt(var + GN_EPS)).reshape(bsz, T, D_RWKV)
    y = y * gn_w.astype(jnp.float32) + gn_b.astype(jnp.float32)
    bonus = jnp.sum(r * k * r_k.astype(jnp.float32), axis=-1, keepdims=True) * v
    y = y + bonus.reshape(bsz, T, D_RWKV)
    y_b = y.astype(dt) * g

    merged = jax.nn.sigmoid(zga) * (y_a @ w_pa) + jax.nn.sigmoid(zgb) * (y_b @ w_pb)
    out = merged @ w_o
    return out, new_conv, xn[:, -1], s_fin.astype(wkv_state.dtype)


def peer_ffn(x, wq, keys, U, V):
    bsz, T, D = x.shape
    n = bsz * T
    nb = -(-n // PEER_BLOCK)
    pad = nb * PEER_BLOCK - n
    xb = jnp.pad(x.reshape(n, D), ((0, pad), (0, 0))).reshape(nb, PEER_BLOCK, D)

    def block(xt):
        q = (xt @ wq).reshape(PEER_BLOCK, PEER_HEADS, 2, PEER_HALF)
        s = jnp.einsum('thpd,hpnd->thpn', q, keys).astype(jnp.float32)
        s_top, i_top = lax.top_k(s, PEER_TOPK)
        cand = (s_top[:, :, 0, :, None] + s_top[:, :, 1, None, :]).reshape(PEER_BLOCK, PEER_HEADS, PEER_TOPK * PEER_TOPK)
        cidx = (i_top[:, :, 0, :, None] * PEER_KEYS + i_top[:, :, 1, None, :]).reshape(PEER_BLOCK, PEER_HEADS, PEER_TOPK * PEER_TOPK)
        sc, pos = lax.top_k(cand, PEER_TOPK)
        eidx = jnp.take_along_axis(cidx, pos, axis=-1)
        gate = jax.nn.softmax(sc, axis=-1)
        u = jnp.take(U, eidx, axis=0)
        act = jax.nn.gelu(jnp.einsum('thkd,td->thk', u, xt).astype(jnp.float32))
        coef = (gate * act).astype(xt.dtype)
        return jnp.einsum('thk,thkd->td', coef, jnp.take(V, eidx, axis=0))

    out = lax.map(block, xb).reshape(nb * PEER_BLOCK, D)[:n]
    return out.reshape(bsz, T, D)


def run_trunk(x, st_conv, st_shift, st_wkv, weights):
    (norm1_g, w_in, conv_w, mu_rkv, mu_wag, w0, w1, w2, a0, a1, a2, g1, g2, k_k, k_a, r_k,
     gn_w, gn_b, w_pa, w_pb, w_o, norm2_g, peer_wq, peer_keys, peer_u, peer_v, norm_f_g) = weights
    convs, shifts, wkvs = [], [], []
    for l in range(DEPTH):
        xn = rms_norm(x, norm1_g[l])
        h, c_new, s_new, w_new = hybrid_mixer(
            xn, st_conv[l], st_shift[l], st_wkv[l], w_in[l], conv_w[l], mu_rkv[l], mu_wag[l],
            w0[l], w1[l], w2[l], a0[l], a1[l], a2[l], g1[l], g2[l], k_k[l], k_a[l], r_k[l],
            gn_w[l], gn_b[l], w_pa[l], w_pb[l], w_o[l])
        x = x + h
        x = x + peer_ffn(rms_norm(x, norm2_g[l]), peer_wq[l], peer_keys[l], peer_u[l], peer_v[l])
        convs.append(c_new)
        shifts.append(s_new)
        wkvs.append(w_new)
    return rms_norm(x, norm_f_g), jnp.stack(convs), jnp.stack(shifts), jnp.stack(wkvs)


def setup_inputs(seed: int = 0) -> dict:
    key = jax.random.key(seed)
    ks = jax.random.split(key, 40)
    f32 = jnp.float32

    def nrm(k, shape, scale):
        return jax.random.normal(k, shape, f32) * scale

    L = DEPTH
    return {
        "x_prompt": nrm(ks[0], (BATCH, SEQ, D_MODEL), 1.0),
        "x_sample": nrm(ks[1], (DEC_BATCH, DEC_SEQ, D_MODEL), 1.0),
        "state_conv": nrm(ks[2], (L, DEC_BATCH, CONV_W - 1, D_CONV), 1.0),
        "state_shift": nrm(ks[3], (L, DEC_BATCH, D_MODEL), 1.0),
        "state_wkv": nrm(ks[4], (L, DEC_BATCH, RWKV_HEADS, HEAD_DIM, HEAD_DIM), 0.1),
        "norm1_g": 1.0 + nrm(ks[5], (L, D_MODEL), 0.02),
        "w_in": nrm(ks[6], (L, D_MODEL, D_IN), D_MODEL ** -0.5),
        "conv_w": nrm(ks[7], (L, CONV_W, D_CONV), 0.5),
        "mu_rkv": jax.random.uniform(ks[8], (L, 3 * D_RWKV), f32),
        "mu_wag": jax.random.uniform(ks[9], (L, 3, D_MODEL), f32),
        "w0": -1.0 + nrm(ks[10], (L, D_RWKV), 1.0),
        "w1": nrm(ks[11], (L, D_MODEL, LORA_DECAY), D_MODEL ** -0.5),
        "w2": nrm(ks[12], (L, LORA_DECAY, D_RWKV), 0.5 * LORA_DECAY ** -0.5),
        "a0": nrm(ks[13], (L, D_RWKV), 0.5),
        "a1": nrm(ks[14], (L, D_MODEL, LORA_AAA), D_MODEL ** -0.5),
        "a2": nrm(ks[15], (L, LORA_AAA, D_RWKV), 0.5 * LORA_AAA ** -0.5),
        "g1": nrm(ks[16], (L, D_MODEL, LORA_GATE), D_MODEL ** -0.5),
        "g2": nrm(ks[17], (L, LORA_GATE, D_RWKV), LORA_GATE ** -0.5),
        "k_k": 0.85 + nrm(ks[18], (L, D_RWKV), 0.05),
        "k_a": 1.0 + nrm(ks[19], (L, D_RWKV), 0.05),
        "r_k": nrm(ks[20], (L, RWKV_HEADS, HEAD_DIM), 0.1),
        "gn_w": 1.0 + nrm(ks[21], (L, D_RWKV), 0.02),
        "gn_b": nrm(ks[22], (L, D_RWKV), 0.02),
        "w_pa": nrm(ks[23], (L, D_CONV, D_MODEL), D_CONV ** -0.5),
        "w_pb": nrm(ks[24], (L, D_RWKV, D_MODEL), D_RWKV ** -0.5),
        "w_o": nrm(ks[25], (L, D_MODEL, D_MODEL), D_MODEL ** -0.5),
        "norm2_g": 1.0 + nrm(ks[26], (L, D_MODEL), 0.02),
        "peer_wq": nrm(ks[27], (L, D_MODEL, PEER_HEADS * PEER_QDIM), D_MODEL ** -0.5),
        "peer_keys": nrm(ks[28], (L, PEER_HEADS, 2, PEER_KEYS, PEER_HALF), PEER_HALF ** -0.5),
        "peer_u": nrm(ks[29], (L, N_EXPERTS, D_MODEL), D_MODEL ** -0.5),
        "peer_v": nrm(ks[30], (L, N_EXPERTS, D_MODEL), 0.5 * PEER_HEADS ** -0.5),
        "norm_f_g": 1.0 + nrm(ks[31], (D_MODEL,), 0.02),
    }


def reference(x_prompt, x_sample, state_conv, state_shift, state_wkv,
              norm1_g, w_in, conv_w, mu_rkv, mu_wag, w0, w1, w2, a0, a1, a2, g1, g2,
              k_k, k_a, r_k, gn_w, gn_b, w_pa, w_pb, w_o, norm2_g,
              peer_wq, peer_keys, peer_u, peer_v, norm_f_g):
    weights = (norm1_g, w_in, conv_w, mu_rkv, mu_wag, w0, w1, w2, a0, a1, a2, g1, g2, k_k, k_a, r_k,
               gn_w, gn_b, w_pa, w_pb, w_o, norm2_g, peer_wq, peer_keys, peer_u, peer_v, norm_f_g)
    bp = x_prompt.shape[0]
    dt = x_prompt.dtype
    zero_conv = jnp.zeros((DEPTH, bp, CONV_W - 1, D_CONV), dt)
    zero_shift = jnp.zeros((DEPTH, bp, D_MODEL), dt)
    zero_wkv = jnp.zeros((DEPTH, bp, RWKV_HEADS, HEAD_DIM, HEAD_DIM), state_wkv.dtype)
    y_prompt, new_conv_prompt, new_shift_prompt, new_wkv_prompt = run_trunk(
        x_prompt, zero_conv, zero_shift, zero_wkv, weights)
    y_sample, new_conv_sample, new_shift_sample, new_wkv_sample = run_trunk(
        x_sample, state_conv, state_shift, state_wkv, weights)
    return (y_prompt, y_sample, new_conv_prompt, new_shift_prompt, new_wkv_prompt,
            new_conv_sample, new_shift_sample, new_wkv_sample)
```

```python
import numpy as np
import concourse.bass as bass
import concourse.mybir as mybir
from concourse.bass_utils import run_bass_kernel_spmd

F32 = mybir.dt.float32
BF16 = mybir.dt.bfloat16
I32 = mybir.dt.int32
U32 = mybir.dt.uint32
AF = mybir.ActivationFunctionType
ALU = mybir.AluOpType
AX = mybir.AxisListType

D = 1024
NSLAB_FIXED = 11
NGRP = 16
NSLAB = NSLAB_FIXED + 2 * NGRP
SLABW = 8192
RMS_EPS = 1e-6
GN_EPS = 64e-5
C0 = float(np.exp(-0.5))
NEG = -1e30

PF = {}
_o = 0
for _nm, _w in (("mu_rkv", 12), ("w0", 4), ("a0", 4), ("k_k", 4), ("k_a", 4), ("r_k", 4), ("gn_w", 4),
                ("gn_b", 4), ("cw0", 4), ("cw1", 4), ("cw2", 4), ("muw", 8), ("mua", 8), ("mug", 8)):
    PF[_nm] = _o
    _o += _w
NPF = _o


class TT:
    def __init__(self, h, name):
        self.h = h
        self.name = name
        self.w = None
        self.r = {}
        self.dsem = None
        self.dcnt = 0

    def __getitem__(self, k):
        return self.h[k]


class KB:
    def __init__(self):
        nc = self.nc = bass.Bass("TRN2", target_bir_lowering=False)
        self.E = dict(pe=nc.tensor, act=nc.scalar, dve=nc.vector, pool=nc.gpsimd, sp=nc.sync)
        self.sem = {e: nc.alloc_semaphore("c_" + e) for e in ("pe", "act", "dve", "pool")}
        self.cnt = dict.fromkeys(self.sem, 0)
        self.waited = {}
        self.dts = []
        self.off = 16512
        self.top = 229344
        self.nid = 0

    def sb(self, name, shape, dt, off=None):
        esz = 4 if dt in (F32, I32, U32) else 2
        nbytes = int(np.prod(shape[1:])) * esz
        nbytes = (nbytes + 63) // 64 * 64
        if off is None:
            off = self.off
            self.off += nbytes
            assert self.off <= self.top, (name, self.off)
        h = self.nc.alloc_sbuf_tensor_at(name, list(shape), dt, offset=off)
        t = TT(h, name)
        t.nbytes = nbytes
        t.off = off
        return t

    def ps(self, name, shape, dt):
        return TT(self.nc.alloc_psum_tensor(name, list(shape), dt), name)

    def dram(self, name, shape, dt, kind):
        return TT(self.nc.dram_tensor(name, list(shape), dt, kind=kind).ap(), name)

    def _wait(self, e, evs):
        for (sm, key, val) in evs:
            k = (e, key)
            if self.waited.get(k, 0) < val:
                self.E[e].wait_ge(sm, val)
                self.waited[k] = val

    @staticmethod
    def _deps(r, w):
        evs = []
        for t in r:
            if t.w is not None:
                evs.append(t.w)
        for t in w:
            if t.w is not None:
                evs.append(t.w)
            evs.extend(t.r.values())
        return evs

    def op(self, e, fn, r=(), w=()):
        evs = self._deps(r, w)
        if e == "pe":
            evs = [ev for ev in evs if ev[1] != "c_pe"]
        self._wait(e, evs)
        inst = fn(self.E[e])
        self.cnt[e] += 1
        inst.then_inc(self.sem[e], 1)
        ev = (self.sem[e], "c_" + e, self.cnt[e])
        for t in r:
            t.r[ev[1]] = ev
        for t in w:
            t.w = ev
            t.r = {}
        return inst

    def dma(self, q, out, in_, r=(), w=(), st=None):
        evs = self._deps(r, w)
        self._wait(q, evs)
        t = st if st is not None else (w[0] if w else r[0])
        kind = "sw" if q == "pool" else "hw"
        if not hasattr(t, "ds"):
            t.ds = {}
        if kind not in t.ds:
            nm = "d%s_%s" % (kind, t.name)
            t.ds[kind] = [self.nc.alloc_semaphore(nm), nm, 0]
            self.dts.append(t.ds[kind])
        d = t.ds[kind]
        inst = self.E[q].dma_start(out=out, in_=in_)
        d[2] += 16
        inst.then_inc(d[0], 16)
        ev = (d[0], d[1], d[2])
        for x in r:
            x.r[ev[1]] = ev
        for x in w:
            x.w = ev
            x.r = {}

    def barrier(self, engines=("pe", "act", "dve", "pool", "sp")):
        evs = [(self.sem[e], "c_" + e, self.cnt[e]) for e in self.sem if self.cnt[e] > 0]
        evs += [(d[0], d[1], d[2]) for d in self.dts]
        for e in engines:
            self._wait(e, [ev for ev in evs if ev[1] != "c_" + e])

    def mm(self, P, out, lhsT, rhs, start, stop, r):
        self.op("pe", lambda e: e.matmul(out, lhsT, rhs, start=start, stop=stop), r=r, w=[P])

    def tr(self, P, out, in_, ident, r):
        self.op("pe", lambda e: e.transpose(out, in_, ident), r=r, w=[P])

    def tt(self, e, out, a, b, op, r, w):
        self.op(e, lambda g: g.tensor_tensor(out=out, in0=a, in1=b, op=op), r=r, w=w)

    def ts(self, e, out, a, s1, s2, op0, op1, r, w):
        if s2 is None:
            self.op(e, lambda g: g.tensor_scalar(out=out, in0=a, scalar1=s1, scalar2=None, op0=op0), r=r, w=w)
        else:
            self.op(e, lambda g: g.tensor_scalar(out=out, in0=a, scalar1=s1, scalar2=s2, op0=op0, op1=op1), r=r, w=w)

    def stt(self, e, out, a, sc, b, op0, op1, r, w):
        self.op(e, lambda g: g.scalar_tensor_tensor(out=out, in0=a, scalar=sc, in1=b, op0=op0, op1=op1), r=r, w=w)

    def cp(self, e, out, in_, r, w):
        if e == "act":
            self.op(e, lambda g: g.activation(out=out, in_=in_, func=AF.Copy), r=r, w=w)
        else:
            self.op(e, lambda g: g.tensor_copy(out=out, in_=in_), r=r, w=w)

    def act(self, out, in_, func, r, w, bias=None, scale=None, accum=None):
        kw = {}
        if bias is not None:
            kw["bias"] = bias
        if scale is not None:
            kw["scale"] = scale
        if accum is not None:
            kw["accum_out"] = accum
        self.op("act", lambda g: g.activation(out=out, in_=in_, func=func, **kw), r=r, w=w)


def bc(ap, axis, shape):
    return ap.unsqueeze(axis).to_broadcast(list(shape))


def build(NT=32, dbg=False, stop=None):
    kb = KB()
    nc = kb.nc
    NS = 4
    NSEQ = 1 + NS

    def din(name, shape, dt=F32):
        return nc.dram_tensor(name, list(shape), dt, kind="ExternalInput").ap()

    xp = din("xp", [NT * 128, D])
    xs = din("xs", [NS * 16, D])
    st_shift = din("st_shift", [NS, 128, 8])
    st_conv = din("st_conv", [128, 4, NS, 2])
    st_wkv = din("st_wkv", [NS, 128, 4, 64])
    w_in = din("w_in", [D, 5120])
    w1 = din("w1", [D, 64])
    a1 = din("a1", [D, 64])
    g1 = din("g1", [D, 128])
    w2 = din("w2", [64, 512])
    a2 = din("a2", [64, 512])
    g2 = din("g2", [128, 512])
    w_pa = din("w_pa", [512, D])
    w_pb = din("w_pb", [512, D])
    w_o = din("w_o", [D, D])
    wq = din("wq", [D, 2048])
    keysT = din("keysT", [128, 16, 128])
    UT = din("UT", [D, 16384])
    V = din("V", [16384, D])
    pf_d = din("pf", [128, NPF])
    mu_rkv_row = din("mu_rkv_row", [1, 1536])
    n1g = din("n1g", [1, D])
    n2g = din("n2g", [1, D])
    nfg = din("nfg", [1, D])

    def dout(name, shape):
        return nc.dram_tensor(name, list(shape), F32, kind="ExternalOutput").ap()

    yp = dout("yp", [NT * 128, D])
    ys = dout("ys", [NS * 16, D])
    conv_o = dout("conv_o", [128, 4, NSEQ, 2])
    shift_o = dout("shift_o", [NSEQ, D])
    wkv_o = dout("wkv_o", [NSEQ, 128, 4, 64])
    WS = nc.dram_tensor("WS", [NSLAB, 128, SLABW], BF16, kind="Internal").ap()
    OUTS = TT(None, "outs")

    Q = [kb.ps("Q%d" % i, [128, 512], F32) for i in range(7)]
    PB = kb.ps("PB", [128, 1024], BF16)
    qrr = [0]

    def nextq():
        q = Q[qrr[0] % 5]
        qrr[0] += 1
        return q

    ident_f = kb.sb("ident_f", [128, 128], F32)
    ident_b = kb.sb("ident_b", [128, 128], BF16)
    iota_c = kb.sb("iota_c", [128, 128], F32)
    m_su = kb.sb("m_su", [128, 128], F32)
    m_sl = kb.sb("m_sl", [128, 128], F32)
    m_ui = kb.sb("m_ui", [128, 128], F32)
    bones = kb.sb("bones", [128, 128], F32)
    ones_f = kb.sb("ones_f", [128, 128], F32)
    pf = kb.sb("pf", [128, NPF], F32)
    pf2 = kb.sb("pf2", [128, 64], F32)
    g1b = kb.sb("g1b", [128, D], F32)
    g2b = kb.sb("g2b", [128, D], F32)
    gfb = kb.sb("gfb", [128, D], F32)
    keys_b = kb.sb("keys_b", [128, 16, 128], BF16)
    w2a2 = kb.sb("w2a2", [128, 512], BF16)
    g2_b = kb.sb("g2_b", [128, 512], BF16)
    slabs = [kb.sb("slab%d" % i, [128, SLABW], BF16) for i in range(3)]
    xt = [kb.sb("xt%d" % i, [128, D], F32) for i in range(2)]
    x1 = kb.sb("x1", [128, D], F32)
    sq = kb.sb("sq", [128, D], F32)
    xnf = kb.sb("xnf", [128, D], F32)
    xnb = kb.sb("xnb", [128, D], BF16)
    xnT = kb.sb("xnT", [128, 8, 128], BF16)
    xpT = kb.sb("xpT", [128, 8, 128], BF16)
    carryx = kb.sb("carryx", [128, 8, 1], BF16)
    stat = kb.sb("stat", [128, 8], F32)
    uext = kb.sb("uext", [128, 4, 130], F32)
    uexs = kb.sb("uexs", [128, 4, NS, 18], F32)
    Hf = kb.sb("Hf", [128, 4, 128], F32)
    Hb = kb.sb("Hb", [128, 4, 128], BF16)
    stS = kb.sb("stS", [128, NS, 8], F32)
    tmpH = kb.sb("tmpH", [128, 4, 128], F32)
    pm = kb.sb("pm", [128, 2], F32)
    cm = [kb.sb("cm%d" % i, [128, 512], BF16) for i in range(2)]
    ARENA = kb.off

    ii = kb.sb("ii", [128, 128], I32)
    ip = kb.sb("ip", [128, 128], I32)
    rowf = kb.sb("rowf", [128, 128], F32)
    rb = kb.sb("rb", [128, 128], F32)
    cb = kb.sb("cb", [128, 128], F32)
    kb.op("pool", lambda g: g.iota(ii[:, :], pattern=[[1, 128]], base=0, channel_multiplier=0), w=[ii])
    kb.op("pool", lambda g: g.iota(ip[:, :], pattern=[[0, 128]], base=0, channel_multiplier=1), w=[ip])
    kb.cp("dve", iota_c[:, :], ii[:, :], [ii], [iota_c])
    kb.cp("dve", rowf[:, :], ip[:, :], [ip], [rowf])
    kb.tt("dve", ident_f[:, :], rowf[:, :], iota_c[:, :], ALU.is_equal, [rowf, iota_c], [ident_f])
    kb.cp("dve", ident_b[:, :], ident_f[:, :], [ident_f], [ident_b])
    kb.tt("dve", m_su[:, :], rowf[:, :], iota_c[:, :], ALU.is_lt, [rowf, iota_c], [m_su])
    kb.tt("dve", m_sl[:, :], rowf[:, :], iota_c[:, :], ALU.is_gt, [rowf, iota_c], [m_sl])
    kb.tt("dve", m_ui[:, :], rowf[:, :], iota_c[:, :], ALU.is_le, [rowf, iota_c], [m_ui])
    kb.ts("dve", rb[:, :], rowf[:, :], 64.0, None, ALU.is_ge, None, [rowf], [rb])
    kb.ts("dve", cb[:, :], iota_c[:, :], 64.0, None, ALU.is_ge, None, [iota_c], [cb])
    kb.tt("dve", bones[:, :], rb[:, :], cb[:, :], ALU.is_equal, [rb, cb], [bones])
    kb.op("dve", lambda g: g.memset(ones_f[:, :], 1.0), w=[ones_f])
    kb.op("dve", lambda g: g.memset(carryx[:, :, :], 0.0), w=[carryx])
    kb.op("dve", lambda g: g.memset(uext[:, :, :], 0.0), w=[uext])
    kb.op("dve", lambda g: g.memset(Hf[:, :, :], 0.0), w=[Hf])
    kb.op("dve", lambda g: g.memset(Hb[:, :, :], 0.0), w=[Hb])
    kb.dma("sp", pf[:, :], pf_d, w=[pf])
    kb.dma("sp", g1b[:, :], n1g.to_broadcast([128, D]), w=[g1b])
    kb.dma("sp", g2b[:, :], n2g.to_broadcast([128, D]), w=[g2b])
    kb.dma("sp", gfb[:, :], nfg.to_broadcast([128, D]), w=[gfb])
    kb.dma("pool", keys_b[:, :, :], keysT, w=[keys_b])
    kb.dma("pool", w2a2[0:64, :], w2, w=[w2a2])
    kb.dma("pool", w2a2[64:128, :], a2, w=[w2a2])
    kb.dma("pool", g2_b[:, :], g2, w=[g2_b])
    kb.dma("sp", stS[:, :, :], st_shift.rearrange("q p c -> p q c"), w=[stS])
    for fc in range(4):
        kb.dma("sp", uexs[:, fc, :, 0:2], st_conv[:, fc, :, :], w=[uexs])
    kb.cp("dve", pm[:, 1:2], rb[:, 0:1], [rb], [pm])
    kb.ts("dve", pm[:, 0:1], rb[:, 0:1], -1.0, 1.0, ALU.mult, ALU.add, [rb], [pm])
    ii5 = kb.sb("ii5", [128, 512], I32)
    cf5 = kb.sb("cf5", [128, 512], F32)
    kb.op("pool", lambda g: g.iota(ii5[:, :], pattern=[[1, 512]], base=0, channel_multiplier=0), w=[ii5])
    kb.op("dve", lambda g: g.tensor_single_scalar(out=ii5[:, :], in_=ii5[:, :], scalar=6, op=ALU.logical_shift_right), r=[ii5], w=[ii5])
    kb.op("dve", lambda g: g.tensor_single_scalar(out=ii5[:, :], in_=ii5[:, :], scalar=1, op=ALU.bitwise_and), r=[ii5], w=[ii5])
    kb.cp("dve", cf5[:, :], ii5[:, :], [ii5], [cf5])
    kb.cp("dve", cm[1][:, :], cf5[:, :], [cf5], [cm[1]])
    kb.ts("dve", cm[0][:, :], cf5[:, :], -1.0, 1.0, ALU.mult, ALU.add, [cf5], [cm[0]])
    OM_RKV, OM_KA, OM_W, OM_A, OM_G = 0, 12, 16, 24, 32
    for (dst, src, wd) in ((OM_RKV, PF["mu_rkv"], 12), (OM_KA, PF["k_a"], 4), (OM_W, PF["muw"], 8),
                           (OM_A, PF["mua"], 8), (OM_G, PF["mug"], 8)):
        kb.ts("dve", pf2[:, dst:dst + wd], pf[:, src:src + wd], -1.0, 1.0, ALU.mult, ALU.add, [pf], [pf2])

    def pfc(name, j=0):
        c = PF[name] + j
        return pf[:, c:c + 1]

    mub = kb.sb("mub", [128, 1536], F32)
    omub = kb.sb("omub", [128, 1536], F32)
    kb.dma("sp", mub[:, :], mu_rkv_row.to_broadcast([128, 1536]), w=[mub])
    kb.ts("dve", omub[:, :], mub[:, :], -1.0, 1.0, ALU.mult, ALU.add, [mub], [omub])
    asm = [kb.sb("asm%d" % i, [128, SLABW], BF16) for i in range(2)]
    st32 = [kb.sb("st32_%d" % i, [128, 8, 128], F32) for i in range(2)]
    st_i = [0]

    def wcols(c0):
        return w_in[:, c0:c0 + 128].rearrange("(dc p) j -> p dc j", p=128)

    def blk(a, bi):
        return a.h[:, bi * 1024:(bi + 1) * 1024].rearrange("p (dc j) -> p dc j", j=128)

    def plain_block(a, bi, c0):
        kb.dma("pool", blk(a, bi), wcols(c0), w=[a])

    def scaled_pair(a, bi, c0, mcol0):
        s = st32[st_i[0] % 2]
        st_i[0] += 1
        kb.dma("sp", s[:, :, :], wcols(c0), w=[s])
        kb.tt("dve", blk(a, bi), s[:, :, :], bc(omub[:, mcol0:mcol0 + 128], 1, [128, 8, 128]), ALU.mult, [s, omub], [a])
        kb.tt("dve", blk(a, bi + 1), s[:, :, :], bc(mub[:, mcol0:mcol0 + 128], 1, [128, 8, 128]), ALU.mult, [s, mub], [a])

    def store_slab(a, si):
        kb.dma("sp", WS[si], a[:, :], r=[a], st=a)

    a = asm[0]
    for fc in range(4):
        plain_block(a, fc, 512 + fc * 128)
        plain_block(a, 4 + fc, 1024 + fc * 128)
    store_slab(a, 0)
    a = asm[1]
    for fc in range(4):
        plain_block(a, fc, fc * 128)
    s = st32[0]
    kb.dma("sp", s[:, :, 0:64], w1.rearrange("(dc p) j -> p dc j", p=128), w=[s])
    kb.dma("sp", s[:, :, 64:128], a1.rearrange("(dc p) j -> p dc j", p=128), w=[s])
    s2 = st32[1]
    kb.dma("sp", s2[:, :, :], g1.rearrange("(dc p) j -> p dc j", p=128), w=[s2])

    def mu3(tile, c0, width):
        return bc(tile[:, c0:c0 + 8], 2, [128, 8, width])

    kb.tt("dve", blk(a, 4)[:, :, 0:64], s[:, :, 0:64], mu3(pf2, OM_W, 64), ALU.mult, [s, pf2], [a])
    kb.tt("dve", blk(a, 4)[:, :, 64:128], s[:, :, 64:128], mu3(pf2, OM_A, 64), ALU.mult, [s, pf2], [a])
    kb.tt("dve", blk(a, 5), s2[:, :, :], mu3(pf2, OM_G, 128), ALU.mult, [s2, pf2], [a])
    kb.tt("dve", blk(a, 6)[:, :, 0:64], s[:, :, 0:64], mu3(pf, PF["muw"], 64), ALU.mult, [s, pf], [a])
    kb.tt("dve", blk(a, 6)[:, :, 64:128], s[:, :, 64:128], mu3(pf, PF["mua"], 64), ALU.mult, [s, pf], [a])
    kb.tt("dve", blk(a, 7), s2[:, :, :], mu3(pf, PF["mug"], 128), ALU.mult, [s2, pf], [a])
    store_slab(a, 1)
    for j in range(3):
        a = asm[j % 2]
        for fb in range(4):
            scaled_pair(a, 2 * fb, 1536 + j * 512 + fb * 128, j * 512 + fb * 128)
        store_slab(a, 2 + j)
    for j in range(2):
        a = asm[(j + 1) % 2]
        for ob in range(8):
            plain_block(a, ob, 3072 + j * 1024 + ob * 128)
        store_slab(a, 5 + j)
    a = asm[1]
    for j, wsrc in enumerate((w_pa, w_pb)):
        for ob in range(8):
            kb.dma("pool", a.h[:, j * 4096 + ob * 512: j * 4096 + (ob + 1) * 512].rearrange("p (kc j) -> p kc j", j=128),
                   wsrc[:, ob * 128:(ob + 1) * 128].rearrange("(kc p) j -> p kc j", p=128), w=[a])
    store_slab(a, 7)
    a = asm[0]
    kb.dma("pool", a.h[:, :].rearrange("p (kc j) -> p kc j", j=1024), w_o.rearrange("(kc p) j -> p kc j", p=128), w=[a])
    store_slab(a, 8)
    for j in range(2):
        a = asm[(j + 1) % 2]
        kb.dma("pool", a.h[:, :].rearrange("p (dc j) -> p dc j", j=1024),
               wq[:, j * 1024:(j + 1) * 1024].rearrange("(dc p) j -> p dc j", p=128), w=[a])
        store_slab(a, 9 + j)
    for g in range(NGRP):
        a = asm[0]
        kb.dma("pool", a.h[:, :].rearrange("p (dc e) -> p dc e", e=1024),
               UT[:, g * 1024:(g + 1) * 1024].rearrange("(dc p) e -> p dc e", p=128), w=[a])
        store_slab(a, NSLAB_FIXED + 2 * g)
        a = asm[1]
        kb.dma("pool", a.h[:, :].rearrange("p (i d) -> p i d", d=1024),
               V[g * 1024:(g + 1) * 1024, :].rearrange("(i j) d -> j i d", j=128), w=[a])
        store_slab(a, NSLAB_FIXED + 2 * g + 1)
    kb.barrier()
    if stop == "prologue":
        return nc

    def arena_alloc(specs):
        kb.off = ARENA
        return {nm: kb.sb("ar_" + nm, shp, dt) for (nm, shp, dt) in specs}

    MX = arena_alloc([
        ("zh", [128, 4, 128], BF16), ("cc", [128, 4, 128], F32), ("yaT", [128, 4, 128], BF16),
        ("l01", [128, 128], BF16), ("lg", [128, 128], BF16),
        ("sg", [128, 4, 128], F32), ("asig", [128, 4, 128], F32), ("gs", [128, 4, 128], BF16),
        ("cum", [128, 4, 128], F32), ("eW", [128, 4, 128], F32), ("eWm", [128, 4, 128], F32), ("eWi", [128, 4, 128], F32),
        ("rs", [128, 4, 128], F32), ("ks", [128, 4, 128], F32), ("kk", [128, 4, 128], F32), ("kmod", [128, 4, 128], F32),
        ("t1", [128, 4, 128], F32), ("t2", [128, 4, 128], F32), ("vs", [128, 4, 128], F32),
        ("bonus", [128, 4, 128], F32),
        ("RT", [128, 4, 128], BF16), ("AT", [128, 4, 128], BF16), ("BT", [128, 4, 128], BF16), ("KT", [128, 4, 128], BF16),
        ("VT", [128, 4, 128], BF16),
        ("Btok", [128, 512], BF16), ("Ktok", [128, 512], BF16), ("Vtok", [128, 512], BF16),
        ("Ys", [128, 4, 128], F32), ("cen", [128, 4, 128], F32), ("ybT", [128, 4, 128], BF16),
        ("sga", [128, 8, 128], BF16), ("sgb", [128, 8, 128], BF16), ("m1", [128, 8, 128], F32), ("mT", [128, 8, 128], BF16),
        ("M0", [128, 8, 128], BF16), ("M1", [128, 8, 128], BF16), ("M2", [128, 8, 128], BF16), ("M3", [128, 8, 128], BF16),
        ("M4", [128, 8, 128], BF16), ("M5", [128, 8, 128], BF16), ("M6", [128, 8, 128], BF16),
        ("N0", [128, 8, 128], BF16), ("N1", [128, 8, 128], BF16),
        ("AkT", [128, 8, 128], BF16), ("LrbT", [128, 8, 128], BF16), ("LrkT", [128, 8, 128], BF16),
        ("Xb0", [128, 512], BF16), ("Xb1", [128, 512], BF16),
        ("ATz0", [128, 4, 128], BF16), ("ATz1", [128, 4, 128], BF16), ("BTz0", [128, 4, 128], BF16), ("BTz1", [128, 4, 128], BF16),
        ("RTz0", [128, 4, 128], BF16), ("RTz1", [128, 4, 128], BF16),
        ("Vz0", [128, 512], BF16), ("Vz1", [128, 512], BF16), ("Uz0", [128, 512], BF16), ("Uz1", [128, 512], BF16),
    ])
    mx_end = kb.off
    PR = arena_alloc([
        ("xn2T", [128, 8, 128], BF16), ("qT", [128, 16, 128], BF16),
        ("S", [128, 16, 128], F32), ("top", [128, 16, 16], F32), ("idx", [128, 16, 16], U32), ("idxf", [128, 16, 16], F32),
        ("cand", [128, 8, 256], F32), ("sc", [128, 8, 16], F32), ("pos", [128, 8, 16], U32),
        ("pa", [128, 8, 16], U32), ("pb", [128, 8, 16], U32), ("paf", [128, 8, 16], F32), ("pbf", [128, 8, 16], F32),
        ("oh", [128, 8, 16, 16], F32), ("ex", [128, 8, 16], F32), ("Z", [128, 8], F32),
        ("i_f", [128, 128], F32), ("j_f", [128, 128], F32), ("gate", [128, 128], F32),
        ("iT", [128, 128], F32), ("jT", [128, 128], F32), ("gT", [128, 128], BF16), ("ijg", [128, 3, 128], BF16), ("ijgT", [128, 3, 128], BF16),
        ("A", [128, 32, 128], BF16), ("Bm", [128, 32, 128], BF16),
        ("CT", [128, 128, 128], BF16), ("acts", [128, 4, 128], BF16), ("hT", [128, 4, 128], BF16),
        ("x2", [128, D], F32),
    ])
    pr_end = kb.off
    kb.off = max(mx_end, pr_end)
    assert kb.off <= kb.top, kb.off

    slab_state = {"next": 0}
    total_slabs = (NT + 1) * NSLAB

    def issue_slab():
        k = slab_state["next"]
        if k >= total_slabs:
            return
        buf = slabs[k % 3]
        kb.dma("sp", buf[:, :], WS[k % NSLAB], w=[buf])
        slab_state["next"] = k + 1

    slab_use = {"k": 0}

    def get_slab():
        k = slab_use["k"]
        while slab_state["next"] <= k + 1 and slab_state["next"] < total_slabs:
            issue_slab()
        slab_use["k"] = k + 1
        return slabs[k % 3]

    def done_slab():
        issue_slab()

    def rmsnorm(xin, gb, n, out_f):
        kb.act(sq[:n, :], xin[:n, :], AF.Square, [xin], [sq, stat], accum=stat[:n, 0:1])
        kb.ts("dve", stat[:n, 1:2], stat[:n, 0:1], 1.0 / D, RMS_EPS, ALU.mult, ALU.add, [stat], [stat])
        kb.act(stat[:n, 2:3], stat[:n, 1:2], AF.Sqrt, [stat], [stat])
        kb.op("dve", lambda g: g.reciprocal(out=stat[:n, 3:4], in_=stat[:n, 2:3]), r=[stat], w=[stat])
        kb.stt("dve", out_f[:n, :], xin[:n, :], stat[:n, 3:4], gb[:n, :], ALU.mult, ALU.mult, [xin, stat, gb], [out_f])

    def to_featmajor(src_b, n, dstT):
        for dc in range(8):
            kb.tr(PB, PB[:, dc * 128: dc * 128 + n], src_b[:n, dc * 128:(dc + 1) * 128], ident_b[:n, :n], [src_b, ident_b])
        kb.cp("act", dstT[:, :, :n], PB[:, :].rearrange("p (c t) -> p c t", t=128)[:, :, :n], [PB], [dstT])

    def proj_block(P, out, slab, boff, n, prev_boff=None):
        nmm = 8 if prev_boff is None else 16
        i = 0
        for dc in range(8):
            kb.mm(P, out, slab[:, boff + dc * 128: boff + (dc + 1) * 128], xnT[:, dc, :n], i == 0, i == nmm - 1, [slab, xnT])
            i += 1
        if prev_boff is not None:
            for dc in range(8):
                kb.mm(P, out, slab[:, prev_boff + dc * 128: prev_boff + (dc + 1) * 128], xpT[:, dc, :n], False, i == nmm - 1, [slab, xpT])
                i += 1

    def v4(P, n):
        return P.h[:, 0:4 * n].rearrange("p (a t) -> p a t", t=n)

    def wkv_pass(n, c0, seq_slot, PY, ntot, load_q):
        m = MX
        nl = int(np.log2(n))
        cols = slice(c0, c0 + n)
        Ml = [m["M%d" % l] for l in range(7)]
        Nl = [m["N0"], m["N1"]]
        if load_q is not None:
            kb.op("dve", lambda g: g.memset(Hf[:, :, :], 0.0), w=[Hf])
            kb.dma("sp", Hf[0:64, :, 0:64], st_wkv[load_q, 0:64, :, :], w=[Hf])
            kb.dma("sp", Hf[64:128, :, 64:128], st_wkv[load_q, 64:128, :, :], w=[Hf])
        kb.cp("act", Hb[:, :, :], Hf[:, :, :], [Hf], [Hb])
        if c0 == 0:
            for nm in ("AT", "BT", "RT"):
                for par in range(2):
                    kb.ts("pool", m[nm + "z%d" % par][:, :, :ntot], m[nm][:, :, :ntot], pm[:, par:par + 1], None, ALU.mult, None,
                          [m[nm], pm], [m[nm + "z%d" % par]])
        for (src, dst) in ((m["BT"], m["Btok"]), (m["KT"], m["Ktok"]), (m["VT"], m["Vtok"])):
            for fb in range(4):
                kb.tr(PB, PB[:n, fb * 128:(fb + 1) * 128], src[:, fb, cols], ident_b[:, :], [src, ident_b])
            kb.cp("dve", dst[:n, :], PB[:n, 0:512], [PB], [dst])
        for par in range(2):
            kb.tt("pool", m["Vz%d" % par][:n, :], m["Vtok"][:n, :], cm[par][:n, :], ALU.mult, [m["Vtok"], cm[par]], [m["Vz%d" % par]])

        def pair_products(lT, rname, mask, dst):
            for half in range(2):
                P = nextq()
                for hh in range(4):
                    h = half * 4 + hh
                    b, par = h // 2, h % 2
                    rz = m[rname + "z%d" % par]
                    kb.mm(P, P[:n, hh * n:(hh + 1) * n], lT[:, b, cols], rz[:, b, cols], True, True, [lT, rz])
                kb.tt("dve", dst[:n, half * 4:half * 4 + 4, :n], v4(P, n)[:n], bc(mask[:n, :n], 1, [n, 4, n]), ALU.mult, [P, mask], [dst])

        pair_products(m["BT"], "AT", m_su, Ml[0])
        pair_products(m["AT"], "BT", m_sl, Nl[0])
        pair_products(m["KT"], "AT", m_su, m["AkT"])
        pair_products(m["BT"], "RT", m_ui, m["LrbT"])
        pair_products(m["KT"], "RT", m_ui, m["LrkT"])
        for l in range(nl - 1):
            Mc, Nc = Ml[l], Nl[l % 2]
            for half in range(2):
                P = nextq()
                for hh in range(4):
                    h = half * 4 + hh
                    kb.mm(P, P[:n, hh * n:(hh + 1) * n], Nc[:n, h, :n], Mc[:n, h, :n], True, True, [Nc, Mc])
                kb.cp("act" if half else "dve", Ml[l + 1][:n, half * 4:half * 4 + 4, :n], v4(P, n)[:n], [P], [Ml[l + 1]])
            if l + 1 < nl - 1:
                Nn = Nl[(l + 1) % 2]
                for half in range(2):
                    P = nextq()
                    for hh in range(4):
                        h = half * 4 + hh
                        kb.mm(P, P[:n, hh * n:(hh + 1) * n], Mc[:n, h, :n], Nc[:n, h, :n], True, True, [Nc, Mc])
                    kb.cp("act" if half else "dve", Nn[:n, half * 4:half * 4 + 4, :n], v4(P, n)[:n], [P], [Nn])
        P = nextq()
        for h in range(8):
            b, par = h // 2, h % 2
            hc = slice(h * 64, (h + 1) * 64)
            kb.mm(P, P[:n, hc], m["AT"][:, b, cols], Hb[:, b, par * 64:(par + 1) * 64], True, False, [m["AT"], Hb])
            kb.mm(P, P[:n, hc], m["AkT"][:n, h, :n], m["Vtok"][:n, hc], False, True, [m["AkT"], m["Vtok"]])
        Xc = m["Xb0"]
        kb.cp("dve", Xc[:n, :], P[:n, :], [P], [Xc])
        for l in range(nl):
            P = nextq()
            for h in range(8):
                hc = slice(h * 64, (h + 1) * 64)
                kb.mm(P, P[:n, hc], ident_b[:n, :n], Xc[:n, hc], True, False, [ident_b, Xc])
                kb.mm(P, P[:n, hc], Ml[l][:n, h, :n], Xc[:n, hc], False, True, [Ml[l], Xc])
            Xn = m["Xb1"] if Xc is m["Xb0"] else m["Xb0"]
            kb.cp("dve", Xn[:n, :], P[:n, :], [P], [Xn])
            Xc = Xn
        U = Xc
        for par in range(2):
            kb.tt("pool", m["Uz%d" % par][:n, :], U[:n, :], cm[par][:n, :], ALU.mult, [U, cm[par]], [m["Uz%d" % par]])
        PYv = PY.h[:, 0:4 * ntot].rearrange("p (a t) -> p a t", t=ntot)
        for b in range(4):
            o = PYv[:, b, cols]
            bs = slice(b * 128, (b + 1) * 128)
            kb.mm(PY, o, Hb[:, b, :], m["RT"][:, b, cols], True, False, [Hb, m["RT"]])
            for par in range(2):
                h = 2 * b + par
                Uz, Vz = m["Uz%d" % par], m["Vz%d" % par]
                kb.mm(PY, o, Uz[:n, bs], m["LrbT"][:n, h, :n], False, False, [Uz, m["LrbT"]])
                kb.mm(PY, o, Vz[:n, bs], m["LrkT"][:n, h, :n], False, par == 1, [Vz, m["LrkT"]])
        P = nextq()
        Pv = P.h[:, 0:512].rearrange("p (a v) -> p a v", v=128)
        for b in range(4):
            bs = slice(b * 128, (b + 1) * 128)
            kb.mm(P, Pv[:, b, :], m["Btok"][:n, bs], U[:n, bs], True, False, [m["Btok"], U])
            kb.mm(P, Pv[:, b, :], m["Ktok"][:n, bs], m["Vtok"][:n, bs], False, True, [m["Ktok"], m["Vtok"]])
        kb.tt("dve", tmpH[:, :, :], Pv, Hf[:, :, :], ALU.add, [P, Hf], [tmpH])
        kb.tt("dve", tmpH[:, :, :], tmpH[:, :, :], m["eW"][:, :, c0 + n - 1:c0 + n].to_broadcast([128, 4, 128]), ALU.mult,
              [tmpH, m["eW"]], [tmpH])
        kb.tt("dve", Hf[:, :, :], tmpH[:, :, :], bc(bones[:, :], 1, [128, 4, 128]), ALU.mult, [tmpH, bones], [Hf])
        if seq_slot is not None:
            kb.dma("pool", wkv_o[seq_slot, 0:64, :, :], Hf[0:64, :, 0:64], r=[Hf], st=Hf)
            kb.dma("pool", wkv_o[seq_slot, 64:128, :, :], Hf[64:128, :, 64:128], r=[Hf], st=Hf)

    for ti in range(NT + 1):
        sample = ti == NT
        n = 64 if sample else 128
        L = 16 if sample else 128
        nseq = NS if sample else 1
        last_prompt = ti == NT - 1
        xin = xt[ti % 2]
        m = MX
        src = xs if sample else xp[ti * 128:(ti + 1) * 128, :]
        kb.dma("sp", xin[:n, :], src, w=[xin])
        rmsnorm(xin, g1b, n, xnf)
        kb.cp("act", xnb[:n, :], xnf[:n, :], [xnf], [xnb])
        if sample:
            for q in range(NS):
                kb.dma("pool", shift_o[1 + q:2 + q, :], xnf[16 * q + 15:16 * q + 16, :], r=[xnf], st=xnf)
        elif last_prompt:
            kb.dma("pool", shift_o[0:1, :], xnf[127:128, :], r=[xnf], st=xnf)
        to_featmajor(xnb, n, xnT)
        kb.cp("dve", xpT[:, :, 1:n], xnT[:, :, 0:n - 1], [xnT], [xpT])
        if sample:
            kb.cp("dve", xpT[:, :, 0:n].rearrange("p c (q l) -> p c q l", l=16)[:, :, :, 0],
                  stS[:, :, :].rearrange("p q c -> p c q"), [stS], [xpT])
        else:
            kb.cp("dve", xpT[:, :, 0:1], carryx[:, :, :], [carryx], [xpT])
            kb.cp("dve", carryx[:, :, :], xnT[:, :, n - 1:n], [xnT], [carryx])
        if stop == "A":
            kb.barrier()
            return nc
        ue = uexs if sample else uext
        if sample:
            ucur = ue[:, :, :, 2:18]
            uv = lambda a, b_: ue[:, :, :, a:b_]
            v3 = lambda t: t[:, :, :n].rearrange("p f (q l) -> p f q l", l=16)
        else:
            ucur = ue[:, :, 2:130]
            uv = lambda a, b_: ue[:, :, a:b_]
            v3 = lambda t: t[:, :, :n]
        sl = get_slab()
        Pc, Ph = nextq(), nextq()
        for fc in range(4):
            proj_block(Pc, v4(Pc, n)[:, fc, :], sl, fc * 1024, n)
        for fc in range(4):
            proj_block(Ph, v4(Ph, n)[:, fc, :], sl, (4 + fc) * 1024, n)
        done_slab()
        kb.cp("act", m["zh"][:, :, :n], v4(Ph, n), [Ph], [m["zh"]])
        kb.tt("dve", ucur, v3(v4(Pc, n)) if sample else v4(Pc, n), v3(m["zh"]), ALU.mult, [Pc, m["zh"]], [ue])
        cw = lambda j: bc(pf[:, PF["cw%d" % j]:PF["cw%d" % j] + 4], 2, [128, 4, n]) if not sample else \
            pf[:, PF["cw%d" % j]:PF["cw%d" % j] + 4].unsqueeze(2).unsqueeze(3).to_broadcast([128, 4, NS, 16])
        kb.tt("pool", v3(m["cc"]), uv(0, L), cw(0), ALU.mult, [ue, pf], [m["cc"]])
        kb.tt("pool", v3(m["t1"]), uv(1, L + 1), cw(1), ALU.mult, [ue, pf], [m["t1"]])
        kb.tt("pool", m["cc"][:, :, :n], m["cc"][:, :, :n], m["t1"][:, :, :n], ALU.add, [m["cc"], m["t1"]], [m["cc"]])
        kb.tt("pool", v3(m["t1"]), uv(2, L + 2), cw(2), ALU.mult, [ue, pf], [m["t1"]])
        kb.tt("pool", m["cc"][:, :, :n], m["cc"][:, :, :n], m["t1"][:, :, :n], ALU.add, [m["cc"], m["t1"]], [m["cc"]])
        if sample:
            for fc in range(4):
                kb.dma("pool", conv_o[:, fc, 1:1 + NS, :], ue[:, fc, :, 16:18], r=[ue], st=ue)
        else:
            if last_prompt:
                kb.dma("pool", conv_o[:, :, 0, :], ue[:, :, 128:130], r=[ue], st=ue)
            kb.cp("pool", ue[:, :, 0:2], ue[:, :, 128:130], [ue], [ue])
        if stop == "B1":
            kb.barrier()
            return nc
        sl = get_slab()
        Pz = nextq()
        for fc in range(4):
            proj_block(Pz, v4(Pz, n)[:, fc, :], sl, fc * 1024, n)
        kb.tt("dve", m["yaT"][:, :, :n], v4(Pz, n), m["cc"][:, :, :n], ALU.mult, [Pz, m["cc"]], [m["yaT"]])
        Pl = nextq()
        proj_block(Pl, Pl[:, 0:n], sl, 4 * 1024, n, prev_boff=6 * 1024)
        proj_block(Pl, Pl[:, n:2 * n], sl, 5 * 1024, n, prev_boff=7 * 1024)
        done_slab()
        kb.act(m["l01"][0:64, :n], Pl[0:64, 0:n], AF.Tanh, [Pl], [m["l01"]])
        kb.act(m["l01"][64:128, :n], Pl[64:128, 0:n], AF.Copy, [Pl], [m["l01"]])
        kb.act(m["lg"][:, :n], Pl[:, n:2 * n], AF.Sigmoid, [Pl], [m["lg"]])
        Pw, Pa, Pg = nextq(), nextq(), nextq()
        for fb in range(4):
            kb.mm(Pw, v4(Pw, n)[:, fb, :], w2a2[0:64, fb * 128:(fb + 1) * 128], m["l01"][0:64, :n], True, True, [w2a2, m["l01"]])
        for fb in range(4):
            kb.mm(Pa, v4(Pa, n)[:, fb, :], w2a2[64:128, fb * 128:(fb + 1) * 128], m["l01"][64:128, :n], True, True, [w2a2, m["l01"]])
        for fb in range(4):
            kb.mm(Pg, v4(Pg, n)[:, fb, :], g2_b[:, fb * 128:(fb + 1) * 128], m["lg"][:, :n], True, True, [g2_b, m["lg"]])
        for fb in range(4):
            kb.act(m["sg"][:, fb, :n], v4(Pw, n)[:, fb, :], AF.Sigmoid, [Pw, pf], [m["sg"]], bias=pfc("w0", fb))
            kb.act(m["asig"][:, fb, :n], v4(Pa, n)[:, fb, :], AF.Sigmoid, [Pa, pf], [m["asig"]], bias=pfc("a0", fb))
        kb.cp("act", m["gs"][:, :, :n], v4(Pg, n), [Pg], [m["gs"]])
        if stop == "B2":
            kb.barrier()
            return nc
        for q in range(nseq):
            for fb in range(4):
                cs = slice(q * L, (q + 1) * L)
                kb.op("dve", lambda g, fb=fb, cs=cs: g.tensor_tensor_scan(
                    out=m["cum"][:, fb, cs], data0=ones_f[:, 0:L], data1=m["sg"][:, fb, cs], initial=0.0,
                    op0=ALU.mult, op1=ALU.add), r=[ones_f, m["sg"]], w=[m["cum"]])
        kb.act(m["eW"][:, :, :n], m["cum"][:, :, :n], AF.Exp, [m["cum"]], [m["eW"]], scale=-C0)
        kb.act(m["eWi"][:, :, :n], m["cum"][:, :, :n], AF.Exp, [m["cum"]], [m["eWi"]], scale=C0)
        kb.tt("pool", m["t1"][:, :, :n], m["cum"][:, :, :n], m["sg"][:, :, :n], ALU.subtract, [m["cum"], m["sg"]], [m["t1"]])
        kb.act(m["eWm"][:, :, :n], m["t1"][:, :, :n], AF.Exp, [m["t1"]], [m["eWm"]], scale=-C0)
        if stop == "B3":
            kb.barrier()
            return nc
        sl = get_slab()
        Pr = nextq()
        for fb in range(4):
            proj_block(Pr, v4(Pr, n)[:, fb, :], sl, (2 * fb) * 1024, n, prev_boff=(2 * fb + 1) * 1024)
        done_slab()
        kb.cp("act", m["rs"][:, :, :n], v4(Pr, n), [Pr], [m["rs"]])
        kb.tt("pool", m["RT"][:, :, :n], m["rs"][:, :, :n], m["eW"][:, :, :n], ALU.mult, [m["rs"], m["eW"]], [m["RT"]])
        sl = get_slab()
        Pk = nextq()
        for fb in range(4):
            proj_block(Pk, v4(Pk, n)[:, fb, :], sl, (2 * fb) * 1024, n, prev_boff=(2 * fb + 1) * 1024)
        done_slab()
        kb.cp("act", m["ks"][:, :, :n], v4(Pk, n), [Pk], [m["ks"]])
        kb.tt("dve", m["kk"][:, :, :n], m["ks"][:, :, :n], bc(pf[:, PF["k_k"]:PF["k_k"] + 4], 2, [128, 4, n]), ALU.mult,
              [m["ks"], pf], [m["kk"]])
        kb.tt("pool", m["t2"][:, :, :n], m["kk"][:, :, :n], m["kk"][:, :, :n], ALU.mult, [m["kk"]], [m["t2"]])
        Pn = nextq()
        kb.mm(Pn, Pn[:, 0:4 * n], bones[:, :], m["t2"][:, :, :n], True, True, [bones, m["t2"]])
        kb.act(m["t2"][:, :, :n], v4(Pn, n), AF.Sqrt, [Pn], [m["t2"]])
        kb.ts("dve", m["t2"][:, :, :n], m["t2"][:, :, :n], 1e-12, None, ALU.max, None, [m["t2"]], [m["t2"]])
        kb.op("dve", lambda g: g.reciprocal(out=m["t2"][:, :, :n], in_=m["t2"][:, :, :n]), r=[m["t2"]], w=[m["t2"]])
        kb.tt("dve", m["kk"][:, :, :n], m["kk"][:, :, :n], m["t2"][:, :, :n], ALU.mult, [m["kk"], m["t2"]], [m["kk"]])
        kb.tt("pool", m["t1"][:, :, :n], m["asig"][:, :, :n], bc(pf[:, PF["k_a"]:PF["k_a"] + 4], 2, [128, 4, n]), ALU.mult,
              [m["asig"], pf], [m["t1"]])
        kb.tt("pool", m["t1"][:, :, :n], m["t1"][:, :, :n], bc(pf2[:, OM_KA:OM_KA + 4], 2, [128, 4, n]), ALU.add,
              [m["t1"], pf2], [m["t1"]])
        kb.tt("pool", m["kmod"][:, :, :n], m["ks"][:, :, :n], m["t1"][:, :, :n], ALU.mult, [m["ks"], m["t1"]], [m["kmod"]])
        kb.stt("dve", m["AT"][:, :, :n], m["kk"][:, :, :n], -1.0, m["eWm"][:, :, :n], ALU.mult, ALU.mult, [m["kk"], m["eWm"]], [m["AT"]])
        kb.tt("dve", m["t1"][:, :, :n], m["kk"][:, :, :n], m["asig"][:, :, :n], ALU.mult, [m["kk"], m["asig"]], [m["t1"]])
        kb.tt("dve", m["BT"][:, :, :n], m["t1"][:, :, :n], m["eWi"][:, :, :n], ALU.mult, [m["t1"], m["eWi"]], [m["BT"]])
        kb.tt("pool", m["KT"][:, :, :n], m["kmod"][:, :, :n], m["eWi"][:, :, :n], ALU.mult, [m["kmod"], m["eWi"]], [m["KT"]])
        sl = get_slab()
        Pv_ = nextq()
        for fb in range(4):
            proj_block(Pv_, v4(Pv_, n)[:, fb, :], sl, (2 * fb) * 1024, n, prev_boff=(2 * fb + 1) * 1024)
        done_slab()
        kb.cp("act", m["vs"][:, :, :n], v4(Pv_, n), [Pv_], [m["vs"]])
        kb.cp("pool", m["VT"][:, :, :n], m["vs"][:, :, :n], [m["vs"]], [m["VT"]])
        kb.tt("dve", m["t1"][:, :, :n], m["rs"][:, :, :n], m["kmod"][:, :, :n], ALU.mult, [m["rs"], m["kmod"]], [m["t1"]])
        kb.tt("dve", m["t1"][:, :, :n], m["t1"][:, :, :n], bc(pf[:, PF["r_k"]:PF["r_k"] + 4], 2, [128, 4, n]), ALU.mult,
              [m["t1"], pf], [m["t1"]])
        Pb_ = nextq()
        kb.mm(Pb_, Pb_[:, 0:4 * n], bones[:, :], m["t1"][:, :, :n], True, True, [bones, m["t1"]])
        kb.tt("dve", m["bonus"][:, :, :n], v4(Pb_, n), m["vs"][:, :, :n], ALU.mult, [Pb_, m["vs"]], [m["bonus"]])
        if stop == "B4":
            kb.barrier()
            return nc
        PY = Q[5]
        if sample:
            for q in range(NS):
                wkv_pass(16, 16 * q, 1 + q, PY, n, q)
        else:
            wkv_pass(128, 0, 0 if last_prompt else None, PY, n, None)
        if stop == "WKV":
            kb.barrier()
            return nc
        kb.cp("act", m["Ys"][:, :, :n], v4(PY, n), [PY], [m["Ys"]])
        Pm = nextq()
        kb.mm(Pm, Pm[:, 0:4 * n], bones[:, :], m["Ys"][:, :, :n], True, True, [bones, m["Ys"]])
        kb.stt("dve", m["cen"][:, :, :n], v4(Pm, n), -1.0 / 64, m["Ys"][:, :, :n], ALU.mult, ALU.add, [Pm, m["Ys"]], [m["cen"]])
        kb.tt("pool", m["t1"][:, :, :n], m["cen"][:, :, :n], m["cen"][:, :, :n], ALU.mult, [m["cen"]], [m["t1"]])
        Pv2 = nextq()
        kb.mm(Pv2, Pv2[:, 0:4 * n], bones[:, :], m["t1"][:, :, :n], True, True, [bones, m["t1"]])
        kb.ts("dve", m["t2"][:, :, :n], v4(Pv2, n), 1.0 / 64, GN_EPS, ALU.mult, ALU.add, [Pv2], [m["t2"]])
        kb.act(m["t2"][:, :, :n], m["t2"][:, :, :n], AF.Sqrt, [m["t2"]], [m["t2"]])
        kb.op("dve", lambda g: g.reciprocal(out=m["t2"][:, :, :n], in_=m["t2"][:, :, :n]), r=[m["t2"]], w=[m["t2"]])
        kb.tt("dve", m["cen"][:, :, :n], m["cen"][:, :, :n], m["t2"][:, :, :n], ALU.mult, [m["cen"], m["t2"]], [m["cen"]])
        kb.tt("pool", m["cen"][:, :, :n], m["cen"][:, :, :n], bc(pf[:, PF["gn_w"]:PF["gn_w"] + 4], 2, [128, 4, n]), ALU.mult,
              [m["cen"], pf], [m["cen"]])
        kb.tt("pool", m["cen"][:, :, :n], m["cen"][:, :, :n], bc(pf[:, PF["gn_b"]:PF["gn_b"] + 4], 2, [128, 4, n]), ALU.add,
              [m["cen"], pf], [m["cen"]])
        kb.tt("dve", m["cen"][:, :, :n], m["cen"][:, :, :n], m["bonus"][:, :, :n], ALU.add, [m["cen"], m["bonus"]], [m["cen"]])
        kb.tt("dve", m["ybT"][:, :, :n], m["cen"][:, :, :n], m["gs"][:, :, :n], ALU.mult, [m["cen"], m["gs"]], [m["ybT"]])
        if stop == "GN":
            kb.barrier()
            return nc
        for j, dst in enumerate((m["sga"], m["sgb"])):
            sl = get_slab()
            for half in range(2):
                P = nextq()
                for o in range(4):
                    proj_block(P, v4(P, n)[:, o, :], sl, (half * 4 + o) * 1024, n)
                kb.act(dst[:, half * 4:half * 4 + 4, :n], v4(P, n), AF.Sigmoid, [P], [dst])
            done_slab()
        sl = get_slab()
        for j, (srcT, gate) in enumerate(((m["yaT"], m["sga"]), (m["ybT"], m["sgb"]))):
            for half in range(2):
                P = nextq()
                for o in range(4):
                    ob = half * 4 + o
                    for kc in range(4):
                        off = j * 4096 + ob * 512 + kc * 128
                        kb.mm(P, v4(P, n)[:, o, :], sl[:, off:off + 128], srcT[:, kc, :n], kc == 0, kc == 3, [sl, srcT])
                if j == 0:
                    kb.tt("dve", m["m1"][:, half * 4:half * 4 + 4, :n], v4(P, n), gate[:, half * 4:half * 4 + 4, :n], ALU.mult,
                          [P, gate], [m["m1"]])
                else:
                    kb.tt("dve", m["cen"][:, :, :n], v4(P, n), gate[:, half * 4:half * 4 + 4, :n], ALU.mult, [P, gate], [m["cen"]])
                    kb.tt("pool", m["mT"][:, half * 4:half * 4 + 4, :n], m["cen"][:, :, :n], m["m1"][:, half * 4:half * 4 + 4, :n],
                          ALU.add, [m["cen"], m["m1"]], [m["mT"]])
        done_slab()
        sl = get_slab()
        for half in range(2):
            P = Q[5 + half]
            for kc in range(8):
                kb.mm(P, P[:n, :], m["mT"][:, kc, :n], sl[:, kc * 1024 + half * 512: kc * 1024 + half * 512 + 512], kc == 0, kc == 7,
                      [m["mT"], sl])
            kb.tt("dve", x1[:n, half * 512:(half + 1) * 512], P[:n, :], xin[:n, half * 512:(half + 1) * 512], ALU.add, [P, xin], [x1])
        done_slab()
        kb.barrier(engines=("pe", "act", "dve", "pool"))
        if stop == "MIX":
            kb.barrier()
            return nc
        p = PR
        rmsnorm(x1, g2b, n, xnf)
        kb.cp("act", xnb[:n, :], xnf[:n, :], [xnf], [xnb])
        to_featmajor(xnb, n, p["xn2T"])
        for j in range(2):
            sl = get_slab()
            for half in range(2):
                P = nextq()
                for o in range(4):
                    hp_l = half * 4 + o
                    for dc in range(8):
                        off = dc * 1024 + hp_l * 128
                        kb.mm(P, v4(P, n)[:, o, :], sl[:, off:off + 128], p["xn2T"][:, dc, :n], dc == 0, dc == 7, [sl, p["xn2T"]])
                hp0 = j * 8 + half * 4
                kb.cp("act" if half else "dve", p["qT"][:, hp0:hp0 + 4, :n], v4(P, n), [P], [p["qT"]])
            done_slab()
        for grp in range(4):
            P = nextq()
            for o in range(4):
                hp = grp * 4 + o
                kb.mm(P, P[:n, o * 128:(o + 1) * 128], p["qT"][:, hp, :n], keys_b[:, hp, :], True, True, [p["qT"], keys_b])
            kb.cp("act" if grp % 2 else "dve", p["S"][:n, grp * 4:grp * 4 + 4, :], P[:n, :].rearrange("t (a k) -> t a k", k=128), [P], [p["S"]])
        if stop == "C3":
            kb.barrier()
            return nc
        S, top, idx = p["S"], p["top"], p["idx"]
        for hp in range(16):
            kb.op("dve", lambda g, hp=hp: g.max(out=top[:n, hp, 0:8], in_=S[:n, hp, :]), r=[S], w=[top])
            kb.op("dve", lambda g, hp=hp: g.max_index(out=idx[:n, hp, 0:8], in_max=top[:n, hp, 0:8], in_values=S[:n, hp, :]), r=[S, top], w=[idx])
            kb.op("dve", lambda g, hp=hp: g.match_replace(out=S[:n, hp, :], in_to_replace=top[:n, hp, 0:8], in_values=S[:n, hp, :], imm_value=NEG),
                  r=[top], w=[S])
            kb.op("dve", lambda g, hp=hp: g.max(out=top[:n, hp, 8:16], in_=S[:n, hp, :]), r=[S], w=[top])
            kb.op("dve", lambda g, hp=hp: g.max_index(out=idx[:n, hp, 8:16], in_max=top[:n, hp, 8:16], in_values=S[:n, hp, :]), r=[S, top], w=[idx])
        kb.cp("dve", p["idxf"][:n, :, :], idx[:n, :, :], [idx], [p["idxf"]])
        top4 = top[:n, :, :].rearrange("t (h q) a -> t h q a", q=2)
        idx4 = p["idxf"][:n, :, :].rearrange("t (h q) a -> t h q a", q=2)
        cand = p["cand"]
        cand4 = cand[:n, :, :].rearrange("t h (a b) -> t h a b", b=16)
        kb.tt("dve", cand4, bc(top4[:, :, 0, :], 3, [n, 8, 16, 16]), bc(top4[:, :, 1, :], 2, [n, 8, 16, 16]), ALU.add, [top], [cand])
        sc, pos = p["sc"], p["pos"]
        for h in range(8):
            kb.op("dve", lambda g, h=h: g.max(out=sc[:n, h, 0:8], in_=cand[:n, h, :]), r=[cand], w=[sc])
            kb.op("dve", lambda g, h=h: g.max_index(out=pos[:n, h, 0:8], in_max=sc[:n, h, 0:8], in_values=cand[:n, h, :]), r=[cand, sc], w=[pos])
            kb.op("dve", lambda g, h=h: g.match_replace(out=cand[:n, h, :], in_to_replace=sc[:n, h, 0:8], in_values=cand[:n, h, :], imm_value=NEG),
                  r=[sc], w=[cand])
            kb.op("dve", lambda g, h=h: g.max(out=sc[:n, h, 8:16], in_=cand[:n, h, :]), r=[cand], w=[sc])
            kb.op("dve", lambda g, h=h: g.max_index(out=pos[:n, h, 8:16], in_max=sc[:n, h, 8:16], in_values=cand[:n, h, :]), r=[cand, sc], w=[pos])
        if stop == "C4":
            kb.barrier()
            return nc
        kb.tt("dve", p["ex"][:n, :, :], sc[:n, :, :], sc[:n, :, 0:1].to_broadcast([n, 8, 16]), ALU.subtract, [sc], [p["ex"]])
        kb.act(p["ex"][:n, :, :], p["ex"][:n, :, :], AF.Exp, [p["ex"]], [p["ex"]])
        kb.op("dve", lambda g: g.tensor_reduce(out=p["Z"][:n, :], in_=p["ex"][:n, :, :], axis=AX.X, op=ALU.add), r=[p["ex"]], w=[p["Z"]])
        kb.op("dve", lambda g: g.reciprocal(out=p["Z"][:n, :], in_=p["Z"][:n, :]), r=[p["Z"]], w=[p["Z"]])
        gate3 = p["gate"][:n, :].rearrange("t (h k) -> t h k", k=16)
        kb.tt("dve", gate3, p["ex"][:n, :, :], bc(p["Z"][:n, :], 2, [n, 8, 16]), ALU.mult, [p["ex"], p["Z"]], [p["gate"]])
        kb.op("dve", lambda g: g.tensor_single_scalar(out=p["pa"][:n, :, :], in_=pos[:n, :, :], scalar=4, op=ALU.logical_shift_right), r=[pos], w=[p["pa"]])
        kb.op("dve", lambda g: g.tensor_single_scalar(out=p["pb"][:n, :, :], in_=pos[:n, :, :], scalar=15, op=ALU.bitwise_and), r=[pos], w=[p["pb"]])
        kb.cp("dve", p["paf"][:n, :, :], p["pa"][:n, :, :], [p["pa"]], [p["paf"]])
        kb.cp("dve", p["pbf"][:n, :, :], p["pb"][:n, :, :], [p["pb"]], [p["pbf"]])
        oh = p["oh"]
        iota16 = iota_c[:n, 0:16].unsqueeze(1).unsqueeze(2).to_broadcast([n, 8, 16, 16])
        for (pf_, q_, dst) in ((p["paf"], 0, p["i_f"]), (p["pbf"], 1, p["j_f"])):
            kb.tt("dve", oh[:n], iota16, bc(pf_[:n, :, :], 3, [n, 8, 16, 16]), ALU.is_equal, [iota_c, pf_], [oh])
            kb.tt("dve", oh[:n], oh[:n], bc(idx4[:, :, q_, :], 2, [n, 8, 16, 16]), ALU.mult, [oh, p["idxf"]], [oh])
            kb.op("dve", lambda g, dst=dst: g.tensor_reduce(out=dst[:n, :].rearrange("t (h k) -> t h k", k=16), in_=oh[:n], axis=AX.X, op=ALU.add),
                  r=[oh], w=[dst])
        if stop == "C5":
            kb.barrier()
            return nc
        ijg = p["ijg"]
        kb.cp("dve", ijg[:n, 0, :], p["i_f"][:n, :], [p["i_f"]], [ijg])
        kb.cp("dve", ijg[:n, 1, :], p["j_f"][:n, :], [p["j_f"]], [ijg])
        kb.cp("dve", ijg[:n, 2, :], p["gate"][:n, :], [p["gate"]], [ijg])
        if stop == "C5b":
            kb.barrier()
            return nc
        for k3 in range(3):
            kb.tr(PB, PB[:, k3 * 128:k3 * 128 + n], ijg[:n, k3, :], ident_b[:n, :n], [ijg, ident_b])
        if stop == "C5c":
            kb.barrier()
            return nc
        ijgT = p["ijgT"]
        kb.cp("dve", ijgT[:, :, :n], PB[:, 0:384].rearrange("p (k t) -> p k t", t=128)[:, :, :n], [PB], [ijgT])
        if stop == "C6":
            kb.barrier()
            return nc
        A, Bm, CT = p["A"], p["Bm"], p["CT"]
        TG = 32
        for tg in range(n // TG):
            ts_ = slice(tg * TG, (tg + 1) * TG)
            iob = bc(iota_c[:, :], 1, [128, TG, 128])
            kb.tt("dve", A[:, :, :], iob, bc(ijgT[:, 0, ts_], 2, [128, TG, 128]), ALU.is_equal, [iota_c, ijgT], [A])
            kb.tt("pool", A[:, :, :], A[:, :, :], bc(ijgT[:, 2, ts_], 2, [128, TG, 128]), ALU.mult, [A, ijgT], [A])
            kb.tt("dve", Bm[:, :, :], iob, bc(ijgT[:, 1, ts_], 2, [128, TG, 128]), ALU.is_equal, [iota_c, ijgT], [Bm])
            for t4 in range(TG // 4):
                P = nextq()
                for tt_ in range(4):
                    tl = t4 * 4 + tt_
                    kb.mm(P, P[:, tt_ * 128:(tt_ + 1) * 128], Bm[:, tl, :], A[:, tl, :], True, True, [Bm, A])
                t0 = tg * TG + t4 * 4
                kb.cp("act", CT[:, :, t0:t0 + 4].rearrange("j i t -> j t i"),
                      P[:, :].rearrange("j (t i) -> j t i", i=128), [P], [CT])
        if stop == "C7":
            kb.barrier()
            return nc
        PO = (Q[5], Q[6])
        for g in range(NGRP):
            slU = get_slab()
            slV = get_slab()
            for half in range(2):
                P = nextq()
                for o in range(4):
                    il = half * 4 + o
                    for dc in range(8):
                        off = dc * 1024 + il * 128
                        kb.mm(P, v4(P, n)[:, o, :], slU[:, off:off + 128], p["xn2T"][:, dc, :n], dc == 0, dc == 7, [slU, p["xn2T"]])
                i0 = g * 8 + half * 4
                kb.act(p["acts"][:, :, :n], v4(P, n), AF.Gelu_apprx_tanh, [P], [p["acts"]])
                kb.tt("dve", p["hT"][:, :, :n], p["acts"][:, :, :n], CT[:, i0:i0 + 4, :n], ALU.mult, [p["acts"], CT], [p["hT"]])
                for o in range(4):
                    i = i0 + o
                    il = half * 4 + o
                    for hv in range(2):
                        kb.mm(PO[hv], PO[hv][:n, :], p["hT"][:, o, :n], slV[:, il * 1024 + hv * 512: il * 1024 + hv * 512 + 512],
                              i == 0, i == 127, [p["hT"], slV])
            done_slab()
            done_slab()
        if stop == "C8":
            kb.barrier()
            return nc
        x2 = p["x2"]
        for hv in range(2):
            kb.tt("dve", x2[:n, hv * 512:(hv + 1) * 512], PO[hv][:n, :], x1[:n, hv * 512:(hv + 1) * 512], ALU.add, [PO[hv], x1], [x2])
        rmsnorm(x2, gfb, n, xnf)
        dst = ys if sample else yp[ti * 128:(ti + 1) * 128, :]
        kb.dma("pool", dst, xnf[:n, :], r=[xnf], st=xnf)
        kb.barrier(engines=("pe", "act", "dve", "pool"))
    kb.barrier()
    return nc


def _fm(v, c):
    return np.ascontiguousarray(np.asarray(v, np.float32).reshape(c, 128).T)


def make_in_maps(inp, NT=32, n_cores=8):
    f = lambda k: np.asarray(inp[k], np.float32)
    pfa = np.concatenate([
        _fm(f("mu_rkv")[0], 12), _fm(f("w0")[0], 4), _fm(f("a0")[0], 4), _fm(f("k_k")[0], 4), _fm(f("k_a")[0], 4),
        _fm(f("r_k")[0].reshape(-1), 4), _fm(f("gn_w")[0], 4), _fm(f("gn_b")[0], 4),
        _fm(f("conv_w")[0, 0], 4), _fm(f("conv_w")[0, 1], 4), _fm(f("conv_w")[0, 2], 4),
        _fm(f("mu_wag")[0, 0], 8), _fm(f("mu_wag")[0, 1], 8), _fm(f("mu_wag")[0, 2], 8)], axis=1)
    assert pfa.shape == (128, NPF)
    shared = dict(
        w_in=np.ascontiguousarray(f("w_in")[0]), w1=np.ascontiguousarray(f("w1")[0]), a1=np.ascontiguousarray(f("a1")[0]),
        g1=np.ascontiguousarray(f("g1")[0]), w2=np.ascontiguousarray(f("w2")[0]), a2=np.ascontiguousarray(f("a2")[0]),
        g2=np.ascontiguousarray(f("g2")[0]), w_pa=np.ascontiguousarray(f("w_pa")[0]), w_pb=np.ascontiguousarray(f("w_pb")[0]),
        w_o=np.ascontiguousarray(f("w_o")[0]), wq=np.ascontiguousarray(f("peer_wq")[0]),
        keysT=np.ascontiguousarray(f("peer_keys")[0].reshape(16, 128, 128).transpose(2, 0, 1)),
        UT=np.ascontiguousarray(f("peer_u")[0].T), V=np.ascontiguousarray(f("peer_v")[0]),
        pf=np.ascontiguousarray(pfa), mu_rkv_row=np.ascontiguousarray(f("mu_rkv")[0].reshape(1, 1536)),
        n1g=np.ascontiguousarray(f("norm1_g")[0].reshape(1, D)), n2g=np.ascontiguousarray(f("norm2_g")[0].reshape(1, D)),
        nfg=np.ascontiguousarray(f("norm_f_g").reshape(1, D)),
    )
    maps = []
    for c in range(n_cores):
        sl = slice(4 * c, 4 * c + 4)
        m = dict(shared)
        m["xp"] = np.ascontiguousarray(f("x_prompt")[c, :NT * 128])
        m["xs"] = np.ascontiguousarray(f("x_sample")[sl].reshape(64, D))
        m["st_shift"] = np.ascontiguousarray(f("state_shift")[0, sl].reshape(4, 8, 128).transpose(0, 2, 1))
        m["st_conv"] = np.ascontiguousarray(f("state_conv")[0, sl].reshape(4, 2, 4, 128).transpose(3, 2, 0, 1))
        m["st_wkv"] = np.ascontiguousarray(f("state_wkv")[0, sl].reshape(4, 4, 2, 64, 64).transpose(0, 2, 4, 1, 3).reshape(4, 128, 4, 64))
        maps.append(m)
    return maps


def assemble(results, NT=32, n_cores=8):
    T = NT * 128
    y_prompt = np.stack([r["yp"] for r in results]).reshape(n_cores, T, D)
    y_sample = np.concatenate([r["ys"].reshape(4, 16, D) for r in results], axis=0)
    conv = np.stack([r["conv_o"] for r in results])
    conv = conv.transpose(0, 3, 4, 2, 1).reshape(n_cores, 5, 2, 512)
    shift = np.stack([r["shift_o"] for r in results])
    wkv = np.stack([r["wkv_o"] for r in results])
    wkv = wkv.reshape(n_cores, 5, 2, 64, 4, 64).transpose(0, 1, 4, 2, 5, 3).reshape(n_cores, 5, 8, 64, 64)
    f32 = np.float32
    return (y_prompt.astype(f32), y_sample.astype(f32),
            np.ascontiguousarray(conv[:, 0])[None].astype(f32), np.ascontiguousarray(shift[:, 0])[None].astype(f32),
            np.ascontiguousarray(wkv[:, 0])[None].astype(f32),
            np.ascontiguousarray(conv[:, 1:].reshape(n_cores * 4, 2, 512))[None].astype(f32),
            np.ascontiguousarray(shift[:, 1:].reshape(n_cores * 4, D))[None].astype(f32),
            np.ascontiguousarray(wkv[:, 1:].reshape(n_cores * 4, 8, 64, 64))[None].astype(f32))


def kernel(**inputs):
    nc = build(32)
    maps = make_in_maps(inputs, 32)
    res = run_bass_kernel_spmd(nc, maps, core_ids=list(range(8)))
    return assemble(res.results, 32)
```

```python
import numpy as np
import concourse.bass as bass
import concourse.mybir as mybir
from concourse.bass_utils import run_bass_kernel_spmd

F32 = mybir.dt.float32
BF16 = mybir.dt.bfloat16
I32 = mybir.dt.int32
U32 = mybir.dt.uint32
AF = mybir.ActivationFunctionType
ALU = mybir.AluOpType
AX = mybir.AxisListType

D = 1024
NSLAB_FIXED = 11
NGRP = 16
NSLAB = NSLAB_FIXED + 2 * NGRP
SLABW = 8192
RMS_EPS = 1e-6
GN_EPS = 64e-5
C0 = float(np.exp(-0.5))
NEG = -1e30

PF = {}
_o = 0
for _nm, _w in (("mu_rkv", 12), ("w0", 4), ("a0", 4), ("k_k", 4), ("k_a", 4), ("r_k", 4), ("gn_w", 4),
                ("gn_b", 4), ("cw0", 4), ("cw1", 4), ("cw2", 4), ("muw", 8), ("mua", 8), ("mug", 8)):
    PF[_nm] = _o
    _o += _w
NPF = _o


class TT:
    def __init__(self, h, name):
        self.h = h
        self.name = name
        self.w = None
        self.r = {}
        self.dsem = None
        self.dcnt = 0

    def __getitem__(self, k):
        return self.h[k]


class KB:
    def __init__(self):
        nc = self.nc = bass.Bass("TRN2", target_bir_lowering=False)
        self.E = dict(pe=nc.tensor, act=nc.scalar, dve=nc.vector, pool=nc.gpsimd, sp=nc.sync)
        self.sem = {e: nc.alloc_semaphore("c_" + e) for e in ("pe", "act", "dve", "pool")}
        self.cnt = dict.fromkeys(self.sem, 0)
        self.waited = {}
        self.dts = []
        self.off = 16512
        self.top = 229344
        self.nid = 0

    def sb(self, name, shape, dt, off=None):
        esz = 4 if dt in (F32, I32, U32) else 2
        nbytes = int(np.prod(shape[1:])) * esz
        nbytes = (nbytes + 63) // 64 * 64
        if off is None:
            off = self.off
            self.off += nbytes
            assert self.off <= self.top, (name, self.off)
        h = self.nc.alloc_sbuf_tensor_at(name, list(shape), dt, offset=off)
        t = TT(h, name)
        t.nbytes = nbytes
        t.off = off
        return t

    def ps(self, name, shape, dt):
        return TT(self.nc.alloc_psum_tensor(name, list(shape), dt), name)

    def dram(self, name, shape, dt, kind):
        return TT(self.nc.dram_tensor(name, list(shape), dt, kind=kind).ap(), name)

    def _wait(self, e, evs):
        for (sm, key, val) in evs:
            k = (e, key)
            if self.waited.get(k, 0) < val:
                self.E[e].wait_ge(sm, val)
                self.waited[k] = val

    @staticmethod
    def _deps(r, w):
        evs = []
        for t in r:
            if t.w is not None:
                evs.append(t.w)
        for t in w:
            if t.w is not None:
                evs.append(t.w)
            evs.extend(t.r.values())
        return evs

    def op(self, e, fn, r=(), w=()):
        evs = self._deps(r, w)
        if e == "pe":
            evs = [ev for ev in evs if ev[1] != "c_pe"]
        self._wait(e, evs)
        inst = fn(self.E[e])
        self.cnt[e] += 1
        inst.then_inc(self.sem[e], 1)
        ev = (self.sem[e], "c_" + e, self.cnt[e])
        for t in r:
            t.r[ev[1]] = ev
        for t in w:
            t.w = ev
            t.r = {}
        return inst

    def dma(self, q, out, in_, r=(), w=(), st=None):
        evs = self._deps(r, w)
        self._wait(q, evs)
        t = st if st is not None else (w[0] if w else r[0])
        kind = "sw" if q == "pool" else "hw"
        if not hasattr(t, "ds"):
            t.ds = {}
        if kind not in t.ds:
            nm = "d%s_%s" % (kind, t.name)
            t.ds[kind] = [self.nc.alloc_semaphore(nm), nm, 0]
            self.dts.append(t.ds[kind])
        d = t.ds[kind]
        inst = self.E[q].dma_start(out=out, in_=in_)
        d[2] += 16
        inst.then_inc(d[0], 16)
        ev = (d[0], d[1], d[2])
        for x in r:
            x.r[ev[1]] = ev
        for x in w:
            x.w = ev
            x.r = {}

    def barrier(self, engines=("pe", "act", "dve", "pool", "sp")):
        evs = [(self.sem[e], "c_" + e, self.cnt[e]) for e in self.sem if self.cnt[e] > 0]
        evs += [(d[0], d[1], d[2]) for d in self.dts]
        for e in engines:
            self._wait(e, [ev for ev in evs if ev[1] != "c_" + e])

    def mm(self, P, out, lhsT, rhs, start, stop, r):
        self.op("pe", lambda e: e.matmul(out, lhsT, rhs, start=start, stop=stop), r=r, w=[P])

    def tr(self, P, out, in_, ident, r):
        self.op("pe", lambda e: e.transpose(out, in_, ident), r=r, w=[P])

    def tt(self, e, out, a, b, op, r, w):
        self.op(e, lambda g: g.tensor_tensor(out=out, in0=a, in1=b, op=op), r=r, w=w)

    def ts(self, e, out, a, s1, s2, op0, op1, r, w):
        if s2 is None:
            self.op(e, lambda g: g.tensor_scalar(out=out, in0=a, scalar1=s1, scalar2=None, op0=op0), r=r, w=w)
        else:
            self.op(e, lambda g: g.tensor_scalar(out=out, in0=a, scalar1=s1, scalar2=s2, op0=op0, op1=op1), r=r, w=w)

    def stt(self, e, out, a, sc, b, op0, op1, r, w):
        self.op(e, lambda g: g.scalar_tensor_tensor(out=out, in0=a, scalar=sc, in1=b, op0=op0, op1=op1), r=r, w=w)

    def cp(self, e, out, in_, r, w):
        if e == "act":
            self.op(e, lambda g: g.activation(out=out, in_=in_, func=AF.Copy), r=r, w=w)
        else:
            self.op(e, lambda g: g.tensor_copy(out=out, in_=in_), r=r, w=w)

    def act(self, out, in_, func, r, w, bias=None, scale=None, accum=None):
        kw = {}
        if bias is not None:
            kw["bias"] = bias
        if scale is not None:
            kw["scale"] = scale
        if accum is not None:
            kw["accum_out"] = accum
        self.op("act", lambda g: g.activation(out=out, in_=in_, func=func, **kw), r=r, w=w)


def bc(ap, axis, shape):
    return ap.unsqueeze(axis).to_broadcast(list(shape))


def build(NT=32, dbg=False, stop=None):
    kb = KB()
    nc = kb.nc
    NS = 4
    NSEQ = 1 + NS

    def din(name, shape, dt=F32):
        return nc.dram_tensor(name, list(shape), dt, kind="ExternalInput").ap()

    xp = din("xp", [NT * 128, D])
    xs = din("xs", [NS * 16, D])
    st_shift = din("st_shift", [NS, 128, 8])
    st_conv = din("st_conv", [128, 4, NS, 2])
    st_wkv = din("st_wkv", [NS, 128, 4, 64])
    w_in = din("w_in", [D, 5120])
    w1 = din("w1", [D, 64])
    a1 = din("a1", [D, 64])
    g1 = din("g1", [D, 128])
    w2 = din("w2", [64, 512])
    a2 = din("a2", [64, 512])
    g2 = din("g2", [128, 512])
    w_pa = din("w_pa", [512, D])
    w_pb = din("w_pb", [512, D])
    w_o = din("w_o", [D, D])
    wq = din("wq", [D, 2048])
    keysT = din("keysT", [128, 16, 128])
    UT = din("UT", [D, 16384])
    V = din("V", [16384, D])
    pf_d = din("pf", [128, NPF])
    mu_rkv_row = din("mu_rkv_row", [1, 1536])
    n1g = din("n1g", [1, D])
    n2g = din("n2g", [1, D])
    nfg = din("nfg", [1, D])

    def dout(name, shape):
        return nc.dram_tensor(name, list(shape), F32, kind="ExternalOutput").ap()

    yp = dout("yp", [NT * 128, D])
    ys = dout("ys", [NS * 16, D])
    conv_o = dout("conv_o", [128, 4, NSEQ, 2])
    shift_o = dout("shift_o", [NSEQ, D])
    wkv_o = dout("wkv_o", [NSEQ, 128, 4, 64])
    WS = nc.dram_tensor("WS", [NSLAB, 128, SLABW], BF16, kind="Internal").ap()
    OUTS = TT(None, "outs")

    Q = [kb.ps("Q%d" % i, [128, 512], F32) for i in range(7)]
    PB = kb.ps("PB", [128, 1024], BF16)
    qrr = [0]

    def nextq():
        q = Q[qrr[0] % 5]
        qrr[0] += 1
        return q

    ident_f = kb.sb("ident_f", [128, 128], F32)
    ident_b = kb.sb("ident_b", [128, 128], BF16)
    iota_c = kb.sb("iota_c", [128, 128], F32)
    m_su = kb.sb("m_su", [128, 128], F32)
    m_sl = kb.sb("m_sl", [128, 128], F32)
    m_ui = kb.sb("m_ui", [128, 128], F32)
    bones = kb.sb("bones", [128, 128], F32)
    ones_f = kb.sb("ones_f", [128, 128], F32)
    pf = kb.sb("pf", [128, NPF], F32)
    pf2 = kb.sb("pf2", [128, 64], F32)
    g1b = kb.sb("g1b", [128, D], F32)
    g2b = kb.sb("g2b", [128, D], F32)
    gfb = kb.sb("gfb", [128, D], F32)
    keys_b = kb.sb("keys_b", [128, 16, 128], BF16)
    w2a2 = kb.sb("w2a2", [128, 512], BF16)
    g2_b = kb.sb("g2_b", [128, 512], BF16)
    slabs = [kb.sb("slab%d" % i, [128, SLABW], BF16) for i in range(3)]
    xt = [kb.sb("xt%d" % i, [128, D], F32) for i in range(2)]
    x1 = kb.sb("x1", [128, D], F32)
    sq = kb.sb("sq", [128, D], F32)
    xnf = kb.sb("xnf", [128, D], F32)
    xnb = kb.sb("xnb", [128, D], BF16)
    xnT = kb.sb("xnT", [128, 8, 128], BF16)
    xpT = kb.sb("xpT", [128, 8, 128], BF16)
    carryx = kb.sb("carryx", [128, 8, 1], BF16)
    stat = kb.sb("stat", [128, 8], F32)
    uext = kb.sb("uext", [128, 4, 130], F32)
    uexs = kb.sb("uexs", [128, 4, NS, 18], F32)
    Hf = kb.sb("Hf", [128, 4, 128], F32)
    Hb = kb.sb("Hb", [128, 4, 128], BF16)
    stS = kb.sb("stS", [128, NS, 8], F32)
    tmpH = kb.sb("tmpH", [128, 4, 128], F32)
    pm = kb.sb("pm", [128, 2], F32)
    cm = [kb.sb("cm%d" % i, [128, 512], BF16) for i in range(2)]
    ARENA = kb.off

    ii = kb.sb("ii", [128, 128], I32)
    ip = kb.sb("ip", [128, 128], I32)
    rowf = kb.sb("rowf", [128, 128], F32)
    rb = kb.sb("rb", [128, 128], F32)
    cb = kb.sb("cb", [128, 128], F32)
    kb.op("pool", lambda g: g.iota(ii[:, :], pattern=[[1, 128]], base=0, channel_multiplier=0), w=[ii])
    kb.op("pool", lambda g: g.iota(ip[:, :], pattern=[[0, 128]], base=0, channel_multiplier=1), w=[ip])
    kb.cp("dve", iota_c[:, :], ii[:, :], [ii], [iota_c])
    kb.cp("dve", rowf[:, :], ip[:, :], [ip], [rowf])
    kb.tt("dve", ident_f[:, :], rowf[:, :], iota_c[:, :], ALU.is_equal, [rowf, iota_c], [ident_f])
    kb.cp("dve", ident_b[:, :], ident_f[:, :], [ident_f], [ident_b])
    kb.tt("dve", m_su[:, :], rowf[:, :], iota_c[:, :], ALU.is_lt, [rowf, iota_c], [m_su])
    kb.tt("dve", m_sl[:, :], rowf[:, :], iota_c[:, :], ALU.is_gt, [rowf, iota_c], [m_sl])
    kb.tt("dve", m_ui[:, :], rowf[:, :], iota_c[:, :], ALU.is_le, [rowf, iota_c], [m_ui])
    kb.ts("dve", rb[:, :], rowf[:, :], 64.0, None, ALU.is_ge, None, [rowf], [rb])
    kb.ts("dve", cb[:, :], iota_c[:, :], 64.0, None, ALU.is_ge, None, [iota_c], [cb])
    kb.tt("dve", bones[:, :], rb[:, :], cb[:, :], ALU.is_equal, [rb, cb], [bones])
    kb.op("dve", lambda g: g.memset(ones_f[:, :], 1.0), w=[ones_f])
    kb.op("dve", lambda g: g.memset(carryx[:, :, :], 0.0), w=[carryx])
    kb.op("dve", lambda g: g.memset(uext[:, :, :], 0.0), w=[uext])
    kb.op("dve", lambda g: g.memset(Hf[:, :, :], 0.0), w=[Hf])
    kb.op("dve", lambda g: g.memset(Hb[:, :, :], 0.0), w=[Hb])
    kb.dma("sp", pf[:, :], pf_d, w=[pf])
    kb.dma("sp", g1b[:, :], n1g.to_broadcast([128, D]), w=[g1b])
    kb.dma("sp", g2b[:, :], n2g.to_broadcast([128, D]), w=[g2b])
    kb.dma("sp", gfb[:, :], nfg.to_broadcast([128, D]), w=[gfb])
    kb.dma("pool", keys_b[:, :, :], keysT, w=[keys_b])
    kb.dma("pool", w2a2[0:64, :], w2, w=[w2a2])
    kb.dma("pool", w2a2[64:128, :], a2, w=[w2a2])
    kb.dma("pool", g2_b[:, :], g2, w=[g2_b])
    kb.dma("sp", stS[:, :, :], st_shift.rearrange("q p c -> p q c"), w=[stS])
    for fc in range(4):
        kb.dma("sp", uexs[:, fc, :, 0:2], st_conv[:, fc, :, :], w=[uexs])
    kb.cp("dve", pm[:, 1:2], rb[:, 0:1], [rb], [pm])
    kb.ts("dve", pm[:, 0:1], rb[:, 0:1], -1.0, 1.0, ALU.mult, ALU.add, [rb], [pm])
    ii5 = kb.sb("ii5", [128, 512], I32)
    cf5 = kb.sb("cf5", [128, 512], F32)
    kb.op("pool", lambda g: g.iota(ii5[:, :], pattern=[[1, 512]], base=0, channel_multiplier=0), w=[ii5])
    kb.op("dve", lambda g: g.tensor_single_scalar(out=ii5[:, :], in_=ii5[:, :], scalar=6, op=ALU.logical_shift_right), r=[ii5], w=[ii5])
    kb.op("dve", lambda g: g.tensor_single_scalar(out=ii5[:, :], in_=ii5[:, :], scalar=1, op=ALU.bitwise_and), r=[ii5], w=[ii5])
    kb.cp("dve", cf5[:, :], ii5[:, :], [ii5], [cf5])
    kb.cp("dve", cm[1][:, :], cf5[:, :], [cf5], [cm[1]])
    kb.ts("dve", cm[0][:, :], cf5[:, :], -1.0, 1.0, ALU.mult, ALU.add, [cf5], [cm[0]])
    OM_RKV, OM_KA, OM_W, OM_A, OM_G = 0, 12, 16, 24, 32
    for (dst, src, wd) in ((OM_RKV, PF["mu_rkv"], 12), (OM_KA, PF["k_a"], 4), (OM_W, PF["muw"], 8),
                           (OM_A, PF["mua"], 8), (OM_G, PF["mug"], 8)):
        kb.ts("dve", pf2[:, dst:dst + wd], pf[:, src:src + wd], -1.0, 1.0, ALU.mult, ALU.add, [pf], [pf2])

    def pfc(name, j=0):
        c = PF[name] + j
        return pf[:, c:c + 1]

    mub = kb.sb("mub", [128, 1536], F32)
    omub = kb.sb("omub", [128, 1536], F32)
    kb.dma("sp", mub[:, :], mu_rkv_row.to_broadcast([128, 1536]), w=[mub])
    kb.ts("dve", omub[:, :], mub[:, :], -1.0, 1.0, ALU.mult, ALU.add, [mub], [omub])
    asm = [kb.sb("asm%d" % i, [128, SLABW], BF16) for i in range(2)]
    st32 = [kb.sb("st32_%d" % i, [128, 8, 128], F32) for i in range(2)]
    st_i = [0]

    def wcols(c0):
        return w_in[:, c0:c0 + 128].rearrange("(dc p) j -> p dc j", p=128)

    def blk(a, bi):
        return a.h[:, bi * 1024:(bi + 1) * 1024].rearrange("p (dc j) -> p dc j", j=128)

    def plain_block(a, bi, c0):
        kb.dma("pool", blk(a, bi), wcols(c0), w=[a])

    def scaled_pair(a, bi, c0, mcol0):
        s = st32[st_i[0] % 2]
        st_i[0] += 1
        kb.dma("sp", s[:, :, :], wcols(c0), w=[s])
        kb.tt("dve", blk(a, bi), s[:, :, :], bc(omub[:, mcol0:mcol0 + 128], 1, [128, 8, 128]), ALU.mult, [s, omub], [a])
        kb.tt("dve", blk(a, bi + 1), s[:, :, :], bc(mub[:, mcol0:mcol0 + 128], 1, [128, 8, 128]), ALU.mult, [s, mub], [a])

    def store_slab(a, si):
        kb.dma("sp", WS[si], a[:, :], r=[a], st=a)

    a = asm[0]
    for fc in range(4):
        plain_block(a, fc, 512 + fc * 128)
        plain_block(a, 4 + fc, 1024 + fc * 128)
    store_slab(a, 0)
    a = asm[1]
    for fc in range(4):
        plain_block(a, fc, fc * 128)
    s = st32[0]
    kb.dma("sp", s[:, :, 0:64], w1.rearrange("(dc p) j -> p dc j", p=128), w=[s])
    kb.dma("sp", s[:, :, 64:128], a1.rearrange("(dc p) j -> p dc j", p=128), w=[s])
    s2 = st32[1]
    kb.dma("sp", s2[:, :, :], g1.rearrange("(dc p) j -> p dc j", p=128), w=[s2])

    def mu3(tile, c0, width):
        return bc(tile[:, c0:c0 + 8], 2, [128, 8, width])

    kb.tt("dve", blk(a, 4)[:, :, 0:64], s[:, :, 0:64], mu3(pf2, OM_W, 64), ALU.mult, [s, pf2], [a])
    kb.tt("dve", blk(a, 4)[:, :, 64:128], s[:, :, 64:128], mu3(pf2, OM_A, 64), ALU.mult, [s, pf2], [a])
    kb.tt("dve", blk(a, 5), s2[:, :, :], mu3(pf2, OM_G, 128), ALU.mult, [s2, pf2], [a])
    kb.tt("dve", blk(a, 6)[:, :, 0:64], s[:, :, 0:64], mu3(pf, PF["muw"], 64), ALU.mult, [s, pf], [a])
    kb.tt("dve", blk(a, 6)[:, :, 64:128], s[:, :, 64:128], mu3(pf, PF["mua"], 64), ALU.mult, [s, pf], [a])
    kb.tt("dve", blk(a, 7), s2[:, :, :], mu3(pf, PF["mug"], 128), ALU.mult, [s2, pf], [a])
    store_slab(a, 1)
    for j in range(3):
        a = asm[j % 2]
        for fb in range(4):
            scaled_pair(a, 2 * fb, 1536 + j * 512 + fb * 128, j * 512 + fb * 128)
        store_slab(a, 2 + j)
    for j in range(2):
        a = asm[(j + 1) % 2]
        for ob in range(8):
            plain_block(a, ob, 3072 + j * 1024 + ob * 128)
        store_slab(a, 5 + j)
    a = asm[1]
    for j, wsrc in enumerate((w_pa, w_pb)):
        for ob in range(8):
            kb.dma("pool", a.h[:, j * 4096 + ob * 512: j * 4096 + (ob + 1) * 512].rearrange("p (kc j) -> p kc j", j=128),
                   wsrc[:, ob * 128:(ob + 1) * 128].rearrange("(kc p) j -> p kc j", p=128), w=[a])
    store_slab(a, 7)
    a = asm[0]
    kb.dma("pool", a.h[:, :].rearrange("p (kc j) -> p kc j", j=1024), w_o.rearrange("(kc p) j -> p kc j", p=128), w=[a])
    store_slab(a, 8)
    for j in range(2):
        a = asm[(j + 1) % 2]
        kb.dma("pool", a.h[:, :].rearrange("p (dc j) -> p dc j", j=1024),
               wq[:, j * 1024:(j + 1) * 1024].rearrange("(dc p) j -> p dc j", p=128), w=[a])
        store_slab(a, 9 + j)
    for g in range(NGRP):
        a = asm[0]
        kb.dma("pool", a.h[:, :].rearrange("p (dc e) -> p dc e", e=1024),
               UT[:, g * 1024:(g + 1) * 1024].rearrange("(dc p) e -> p dc e", p=128), w=[a])
        store_slab(a, NSLAB_FIXED + 2 * g)
        a = asm[1]
        kb.dma("pool", a.h[:, :].rearrange("p (i d) -> p i d", d=1024),
               V[g * 1024:(g + 1) * 1024, :].rearrange("(i j) d -> j i d", j=128), w=[a])
        store_slab(a, NSLAB_FIXED + 2 * g + 1)
    kb.barrier()
    if stop == "prologue":
        return nc

    def arena_alloc(specs):
        kb.off = ARENA
        return {nm: kb.sb("ar_" + nm, shp, dt) for (nm, shp, dt) in specs}

    MX = arena_alloc([
        ("zh", [128, 4, 128], BF16), ("cc", [128, 4, 128], F32), ("yaT", [128, 4, 128], BF16),
        ("l01", [128, 128], BF16), ("lg", [128, 128], BF16),
        ("sg", [128, 4, 128], F32), ("asig", [128, 4, 128], F32), ("gs", [128, 4, 128], BF16),
        ("cum", [128, 4, 128], F32), ("eW", [128, 4, 128], F32), ("eWm", [128, 4, 128], F32), ("eWi", [128, 4, 128], F32),
        ("rs", [128, 4, 128], F32), ("ks", [128, 4, 128], F32), ("kk", [128, 4, 128], F32), ("kmod", [128, 4, 128], F32),
        ("t1", [128, 4, 128], F32), ("t2", [128, 4, 128], F32), ("vs", [128, 4, 128], F32),
        ("bonus", [128, 4, 128], F32),
        ("RT", [128, 4, 128], BF16), ("AT", [128, 4, 128], BF16), ("BT", [128, 4, 128], BF16), ("KT", [128, 4, 128], BF16),
        ("VT", [128, 4, 128], BF16),
        ("Btok", [128, 512], BF16), ("Ktok", [128, 512], BF16), ("Vtok", [128, 512], BF16),
        ("Ys", [128, 4, 128], F32), ("cen", [128, 4, 128], F32), ("ybT", [128, 4, 128], BF16),
        ("sga", [128, 8, 128], BF16), ("sgb", [128, 8, 128], BF16), ("m1", [128, 8, 128], F32), ("mT", [128, 8, 128], BF16),
        ("M0", [128, 8, 128], BF16), ("M1", [128, 8, 128], BF16), ("M2", [128, 8, 128], BF16), ("M3", [128, 8, 128], BF16),
        ("M4", [128, 8, 128], BF16), ("M5", [128, 8, 128], BF16), ("M6", [128, 8, 128], BF16),
        ("N0", [128, 8, 128], BF16), ("N1", [128, 8, 128], BF16),
        ("AkT", [128, 8, 128], BF16), ("LrbT", [128, 8, 128], BF16), ("LrkT", [128, 8, 128], BF16),
        ("Xb0", [128, 512], BF16), ("Xb1", [128, 512], BF16),
        ("ATz0", [128, 4, 128], BF16), ("ATz1", [128, 4, 128], BF16), ("BTz0", [128, 4, 128], BF16), ("BTz1", [128, 4, 128], BF16),
        ("RTz0", [128, 4, 128], BF16), ("RTz1", [128, 4, 128], BF16),
        ("Vz0", [128, 512], BF16), ("Vz1", [128, 512], BF16), ("Uz0", [128, 512], BF16), ("Uz1", [128, 512], BF16),
    ])
    mx_end = kb.off
    PR = arena_alloc([
        ("xn2T", [128, 8, 128], BF16), ("qT", [128, 16, 128], BF16),
        ("S", [128, 16, 128], F32), ("top", [128, 16, 16], F32), ("idx", [128, 16, 16], U32), ("idxf", [128, 16, 16], F32),
        ("cand", [128, 8, 256], F32), ("sc", [128, 8, 16], F32), ("pos", [128, 8, 16], U32),
        ("pa", [128, 8, 16], U32), ("pb", [128, 8, 16], U32), ("paf", [128, 8, 16], F32), ("pbf", [128, 8, 16], F32),
        ("oh", [128, 8, 16, 16], F32), ("ex", [128, 8, 16], F32), ("Z", [128, 8], F32),
        ("i_f", [128, 128], F32), ("j_f", [128, 128], F32), ("gate", [128, 128], F32),
        ("iT", [128, 128], F32), ("jT", [128, 128], F32), ("gT", [128, 128], BF16), ("ijg", [128, 3, 128], BF16), ("ijgT", [128, 3, 128], BF16),
        ("A", [128, 32, 128], BF16), ("Bm", [128, 32, 128], BF16),
        ("CT", [128, 128, 128], BF16), ("acts", [128, 4, 128], BF16), ("hT", [128, 4, 128], BF16), ("acts1", [128, 4, 128], BF16), ("hT1", [128, 4, 128], BF16),
        ("x2", [128, D], F32),
    ])
    pr_end = kb.off
    kb.off = max(mx_end, pr_end)
    assert kb.off <= kb.top, kb.off

    slab_state = {"next": 0}
    total_slabs = (NT + 1) * NSLAB

    def issue_slab():
        k = slab_state["next"]
        if k >= total_slabs:
            return
        buf = slabs[k % 3]
        kb.dma("sp", buf[:, :], WS[k % NSLAB], w=[buf])
        slab_state["next"] = k + 1

    slab_use = {"k": 0}

    def get_slab():
        k = slab_use["k"]
        while slab_state["next"] <= k + 1 and slab_state["next"] < total_slabs:
            issue_slab()
        slab_use["k"] = k + 1
        return slabs[k % 3]

    def done_slab():
        issue_slab()

    def rmsnorm(xin, gb, n, out_f):
        kb.act(sq[:n, :], xin[:n, :], AF.Square, [xin], [sq, stat], accum=stat[:n, 0:1])
        kb.ts("dve", stat[:n, 1:2], stat[:n, 0:1], 1.0 / D, RMS_EPS, ALU.mult, ALU.add, [stat], [stat])
        kb.act(stat[:n, 2:3], stat[:n, 1:2], AF.Sqrt, [stat], [stat])
        kb.op("dve", lambda g: g.reciprocal(out=stat[:n, 3:4], in_=stat[:n, 2:3]), r=[stat], w=[stat])
        kb.stt("dve", out_f[:n, :], xin[:n, :], stat[:n, 3:4], gb[:n, :], ALU.mult, ALU.mult, [xin, stat, gb], [out_f])

    def to_featmajor(src_b, n, dstT):
        for dc in range(8):
            kb.tr(PB, PB[:, dc * 128: dc * 128 + n], src_b[:n, dc * 128:(dc + 1) * 128], ident_b[:n, :n], [src_b, ident_b])
        kb.cp("act", dstT[:, :, :n], PB[:, :].rearrange("p (c t) -> p c t", t=128)[:, :, :n], [PB], [dstT])

    def proj_block(P, out, slab, boff, n, prev_boff=None):
        nmm = 8 if prev_boff is None else 16
        i = 0
        for dc in range(8):
            kb.mm(P, out, slab[:, boff + dc * 128: boff + (dc + 1) * 128], xnT[:, dc, :n], i == 0, i == nmm - 1, [slab, xnT])
            i += 1
        if prev_boff is not None:
            for dc in range(8):
                kb.mm(P, out, slab[:, prev_boff + dc * 128: prev_boff + (dc + 1) * 128], xpT[:, dc, :n], False, i == nmm - 1, [slab, xpT])
                i += 1

    def v4(P, n):
        return P.h[:, 0:4 * n].rearrange("p (a t) -> p a t", t=n)

    def wkv_pass(n, c0, seq_slot, PY, ntot, load_q):
        m = MX
        nl = int(np.log2(n))
        cols = slice(c0, c0 + n)
        Ml = [m["M%d" % l] for l in range(7)]
        Nl = [m["N0"], m["N1"]]
        if load_q is not None:
            kb.op("dve", lambda g: g.memset(Hf[:, :, :], 0.0), w=[Hf])
            kb.dma("sp", Hf[0:64, :, 0:64], st_wkv[load_q, 0:64, :, :], w=[Hf])
            kb.dma("sp", Hf[64:128, :, 64:128], st_wkv[load_q, 64:128, :, :], w=[Hf])
        kb.cp("act", Hb[:, :, :], Hf[:, :, :], [Hf], [Hb])
        if c0 == 0:
            for nm in ("AT", "BT", "RT"):
                for par in range(2):
                    kb.ts("pool", m[nm + "z%d" % par][:, :, :ntot], m[nm][:, :, :ntot], pm[:, par:par + 1], None, ALU.mult, None,
                          [m[nm], pm], [m[nm + "z%d" % par]])
        for (src, dst) in ((m["BT"], m["Btok"]), (m["KT"], m["Ktok"]), (m["VT"], m["Vtok"])):
            for fb in range(4):
                kb.tr(PB, PB[:n, fb * 128:(fb + 1) * 128], src[:, fb, cols], ident_b[:, :], [src, ident_b])
            kb.cp("dve", dst[:n, :], PB[:n, 0:512], [PB], [dst])
        for par in range(2):
            kb.tt("pool", m["Vz%d" % par][:n, :], m["Vtok"][:n, :], cm[par][:n, :], ALU.mult, [m["Vtok"], cm[par]], [m["Vz%d" % par]])

        def pair_products(lT, rname, mask, dst):
            for half in range(2):
                P = nextq()
                for hh in range(4):
                    h = half * 4 + hh
                    b, par = h // 2, h % 2
                    rz = m[rname + "z%d" % par]
                    kb.mm(P, P[:n, hh * n:(hh + 1) * n], lT[:, b, cols], rz[:, b, cols], True, True, [lT, rz])
                kb.tt("dve", dst[:n, half * 4:half * 4 + 4, :n], v4(P, n)[:n], bc(mask[:n, :n], 1, [n, 4, n]), ALU.mult, [P, mask], [dst])

        pair_products(m["BT"], "AT", m_su, Ml[0])
        pair_products(m["AT"], "BT", m_sl, Nl[0])
        pair_products(m["KT"], "AT", m_su, m["AkT"])
        pair_products(m["BT"], "RT", m_ui, m["LrbT"])
        pair_products(m["KT"], "RT", m_ui, m["LrkT"])
        for l in range(nl - 1):
            Mc, Nc = Ml[l], Nl[l % 2]
            for half in range(2):
                P = nextq()
                for hh in range(4):
                    h = half * 4 + hh
                    kb.mm(P, P[:n, hh * n:(hh + 1) * n], Nc[:n, h, :n], Mc[:n, h, :n], True, True, [Nc, Mc])
                kb.cp("act" if half else "dve", Ml[l + 1][:n, half * 4:half * 4 + 4, :n], v4(P, n)[:n], [P], [Ml[l + 1]])
            if l + 1 < nl - 1:
                Nn = Nl[(l + 1) % 2]
                for half in range(2):
                    P = nextq()
                    for hh in range(4):
                        h = half * 4 + hh
                        kb.mm(P, P[:n, hh * n:(hh + 1) * n], Mc[:n, h, :n], Nc[:n, h, :n], True, True, [Nc, Mc])
                    kb.cp("act" if half else "dve", Nn[:n, half * 4:half * 4 + 4, :n], v4(P, n)[:n], [P], [Nn])
        P = nextq()
        for h in range(8):
            b, par = h // 2, h % 2
            hc = slice(h * 64, (h + 1) * 64)
            kb.mm(P, P[:n, hc], m["AT"][:, b, cols], Hb[:, b, par * 64:(par + 1) * 64], True, False, [m["AT"], Hb])
            kb.mm(P, P[:n, hc], m["AkT"][:n, h, :n], m["Vtok"][:n, hc], False, True, [m["AkT"], m["Vtok"]])
        Xc = m["Xb0"]
        kb.cp("dve", Xc[:n, :], P[:n, :], [P], [Xc])
        for l in range(nl):
            P = nextq()
            for h in range(8):
                hc = slice(h * 64, (h + 1) * 64)
                kb.mm(P, P[:n, hc], ident_b[:n, :n], Xc[:n, hc], True, False, [ident_b, Xc])
                kb.mm(P, P[:n, hc], Ml[l][:n, h, :n], Xc[:n, hc], False, True, [Ml[l], Xc])
            Xn = m["Xb1"] if Xc is m["Xb0"] else m["Xb0"]
            kb.cp("dve", Xn[:n, :], P[:n, :], [P], [Xn])
            Xc = Xn
        U = Xc
        for par in range(2):
            kb.tt("pool", m["Uz%d" % par][:n, :], U[:n, :], cm[par][:n, :], ALU.mult, [U, cm[par]], [m["Uz%d" % par]])
        PYv = PY.h[:, 0:4 * ntot].rearrange("p (a t) -> p a t", t=ntot)
        for b in range(4):
            o = PYv[:, b, cols]
            bs = slice(b * 128, (b + 1) * 128)
            kb.mm(PY, o, Hb[:, b, :], m["RT"][:, b, cols], True, False, [Hb, m["RT"]])
            for par in range(2):
                h = 2 * b + par
                Uz, Vz = m["Uz%d" % par], m["Vz%d" % par]
                kb.mm(PY, o, Uz[:n, bs], m["LrbT"][:n, h, :n], False, False, [Uz, m["LrbT"]])
                kb.mm(PY, o, Vz[:n, bs], m["LrkT"][:n, h, :n], False, par == 1, [Vz, m["LrkT"]])
        P = nextq()
        Pv = P.h[:, 0:512].rearrange("p (a v) -> p a v", v=128)
        for b in range(4):
            bs = slice(b * 128, (b + 1) * 128)
            kb.mm(P, Pv[:, b, :], m["Btok"][:n, bs], U[:n, bs], True, False, [m["Btok"], U])
            kb.mm(P, Pv[:, b, :], m["Ktok"][:n, bs], m["Vtok"][:n, bs], False, True, [m["Ktok"], m["Vtok"]])
        kb.tt("dve", tmpH[:, :, :], Pv, Hf[:, :, :], ALU.add, [P, Hf], [tmpH])
        kb.tt("dve", tmpH[:, :, :], tmpH[:, :, :], m["eW"][:, :, c0 + n - 1:c0 + n].to_broadcast([128, 4, 128]), ALU.mult,
              [tmpH, m["eW"]], [tmpH])
        kb.tt("dve", Hf[:, :, :], tmpH[:, :, :], bc(bones[:, :], 1, [128, 4, 128]), ALU.mult, [tmpH, bones], [Hf])
        if seq_slot is not None:
            kb.dma("pool", wkv_o[seq_slot, 0:64, :, :], Hf[0:64, :, 0:64], r=[Hf], st=Hf)
            kb.dma("pool", wkv_o[seq_slot, 64:128, :, :], Hf[64:128, :, 64:128], r=[Hf], st=Hf)

    for ti in range(NT + 1):
        sample = ti == NT
        n = 64 if sample else 128
        L = 16 if sample else 128
        nseq = NS if sample else 1
        last_prompt = ti == NT - 1
        xin = xt[ti % 2]
        m = MX
        src = xs if sample else xp[ti * 128:(ti + 1) * 128, :]
        kb.dma("sp", xin[:n, :], src, w=[xin])
        rmsnorm(xin, g1b, n, xnf)
        kb.cp("act", xnb[:n, :], xnf[:n, :], [xnf], [xnb])
        if sample:
            for q in range(NS):
                kb.dma("pool", shift_o[1 + q:2 + q, :], xnf[16 * q + 15:16 * q + 16, :], r=[xnf], st=xnf)
        elif last_prompt:
            kb.dma("pool", shift_o[0:1, :], xnf[127:128, :], r=[xnf], st=xnf)
        to_featmajor(xnb, n, xnT)
        kb.cp("dve", xpT[:, :, 1:n], xnT[:, :, 0:n - 1], [xnT], [xpT])
        if sample:
            kb.cp("dve", xpT[:, :, 0:n].rearrange("p c (q l) -> p c q l", l=16)[:, :, :, 0],
                  stS[:, :, :].rearrange("p q c -> p c q"), [stS], [xpT])
        else:
            kb.cp("dve", xpT[:, :, 0:1], carryx[:, :, :], [carryx], [xpT])
            kb.cp("dve", carryx[:, :, :], xnT[:, :, n - 1:n], [xnT], [carryx])
        if stop == "A":
            kb.barrier()
            return nc
        ue = uexs if sample else uext
        if sample:
            ucur = ue[:, :, :, 2:18]
            uv = lambda a, b_: ue[:, :, :, a:b_]
            v3 = lambda t: t[:, :, :n].rearrange("p f (q l) -> p f q l", l=16)
        else:
            ucur = ue[:, :, 2:130]
            uv = lambda a, b_: ue[:, :, a:b_]
            v3 = lambda t: t[:, :, :n]
        sl = get_slab()
        Pc, Ph = nextq(), nextq()
        for fc in range(4):
            proj_block(Pc, v4(Pc, n)[:, fc, :], sl, fc * 1024, n)
        for fc in range(4):
            proj_block(Ph, v4(Ph, n)[:, fc, :], sl, (4 + fc) * 1024, n)
        done_slab()
        kb.cp("act", m["zh"][:, :, :n], v4(Ph, n), [Ph], [m["zh"]])
        kb.tt("dve", ucur, v3(v4(Pc, n)) if sample else v4(Pc, n), v3(m["zh"]), ALU.mult, [Pc, m["zh"]], [ue])
        cw = lambda j: bc(pf[:, PF["cw%d" % j]:PF["cw%d" % j] + 4], 2, [128, 4, n]) if not sample else \
            pf[:, PF["cw%d" % j]:PF["cw%d" % j] + 4].unsqueeze(2).unsqueeze(3).to_broadcast([128, 4, NS, 16])
        kb.tt("pool", v3(m["cc"]), uv(0, L), cw(0), ALU.mult, [ue, pf], [m["cc"]])
        kb.tt("pool", v3(m["t1"]), uv(1, L + 1), cw(1), ALU.mult, [ue, pf], [m["t1"]])
        kb.tt("pool", m["cc"][:, :, :n], m["cc"][:, :, :n], m["t1"][:, :, :n], ALU.add, [m["cc"], m["t1"]], [m["cc"]])
        kb.tt("pool", v3(m["t1"]), uv(2, L + 2), cw(2), ALU.mult, [ue, pf], [m["t1"]])
        kb.tt("pool", m["cc"][:, :, :n], m["cc"][:, :, :n], m["t1"][:, :, :n], ALU.add, [m["cc"], m["t1"]], [m["cc"]])
        if sample:
            for fc in range(4):
                kb.dma("pool", conv_o[:, fc, 1:1 + NS, :], ue[:, fc, :, 16:18], r=[ue], st=ue)
        else:
            if last_prompt:
                kb.dma("pool", conv_o[:, :, 0, :], ue[:, :, 128:130], r=[ue], st=ue)
            kb.cp("pool", ue[:, :, 0:2], ue[:, :, 128:130], [ue], [ue])
        if stop == "B1":
            kb.barrier()
            return nc
        sl = get_slab()
        Pz = nextq()
        for fc in range(4):
            proj_block(Pz, v4(Pz, n)[:, fc, :], sl, fc * 1024, n)
        kb.tt("dve", m["yaT"][:, :, :n], v4(Pz, n), m["cc"][:, :, :n], ALU.mult, [Pz, m["cc"]], [m["yaT"]])
        Pl = nextq()
        proj_block(Pl, Pl[:, 0:n], sl, 4 * 1024, n, prev_boff=6 * 1024)
        proj_block(Pl, Pl[:, n:2 * n], sl, 5 * 1024, n, prev_boff=7 * 1024)
        done_slab()
        kb.act(m["l01"][0:64, :n], Pl[0:64, 0:n], AF.Tanh, [Pl], [m["l01"]])
        kb.act(m["l01"][64:128, :n], Pl[64:128, 0:n], AF.Copy, [Pl], [m["l01"]])
        kb.act(m["lg"][:, :n], Pl[:, n:2 * n], AF.Sigmoid, [Pl], [m["lg"]])
        Pw, Pa, Pg = nextq(), nextq(), nextq()
        for fb in range(4):
            kb.mm(Pw, v4(Pw, n)[:, fb, :], w2a2[0:64, fb * 128:(fb + 1) * 128], m["l01"][0:64, :n], True, True, [w2a2, m["l01"]])
        for fb in range(4):
            kb.mm(Pa, v4(Pa, n)[:, fb, :], w2a2[64:128, fb * 128:(fb + 1) * 128], m["l01"][64:128, :n], True, True, [w2a2, m["l01"]])
        for fb in range(4):
            kb.mm(Pg, v4(Pg, n)[:, fb, :], g2_b[:, fb * 128:(fb + 1) * 128], m["lg"][:, :n], True, True, [g2_b, m["lg"]])
        for fb in range(4):
            kb.act(m["sg"][:, fb, :n], v4(Pw, n)[:, fb, :], AF.Sigmoid, [Pw, pf], [m["sg"]], bias=pfc("w0", fb))
            kb.act(m["asig"][:, fb, :n], v4(Pa, n)[:, fb, :], AF.Sigmoid, [Pa, pf], [m["asig"]], bias=pfc("a0", fb))
        kb.cp("act", m["gs"][:, :, :n], v4(Pg, n), [Pg], [m["gs"]])
        if stop == "B2":
            kb.barrier()
            return nc
        for q in range(nseq):
            for fb in range(4):
                cs = slice(q * L, (q + 1) * L)
                kb.op("dve", lambda g, fb=fb, cs=cs: g.tensor_tensor_scan(
                    out=m["cum"][:, fb, cs], data0=ones_f[:, 0:L], data1=m["sg"][:, fb, cs], initial=0.0,
                    op0=ALU.mult, op1=ALU.add), r=[ones_f, m["sg"]], w=[m["cum"]])
        kb.act(m["eW"][:, :, :n], m["cum"][:, :, :n], AF.Exp, [m["cum"]], [m["eW"]], scale=-C0)
        kb.act(m["eWi"][:, :, :n], m["cum"][:, :, :n], AF.Exp, [m["cum"]], [m["eWi"]], scale=C0)
        kb.tt("pool", m["t1"][:, :, :n], m["cum"][:, :, :n], m["sg"][:, :, :n], ALU.subtract, [m["cum"], m["sg"]], [m["t1"]])
        kb.act(m["eWm"][:, :, :n], m["t1"][:, :, :n], AF.Exp, [m["t1"]], [m["eWm"]], scale=-C0)
        if stop == "B3":
            kb.barrier()
            return nc
        sl = get_slab()
        Pr = nextq()
        for fb in range(4):
            proj_block(Pr, v4(Pr, n)[:, fb, :], sl, (2 * fb) * 1024, n, prev_boff=(2 * fb + 1) * 1024)
        done_slab()
        kb.cp("act", m["rs"][:, :, :n], v4(Pr, n), [Pr], [m["rs"]])
        kb.tt("pool", m["RT"][:, :, :n], m["rs"][:, :, :n], m["eW"][:, :, :n], ALU.mult, [m["rs"], m["eW"]], [m["RT"]])
        sl = get_slab()
        Pk = nextq()
        for fb in range(4):
            proj_block(Pk, v4(Pk, n)[:, fb, :], sl, (2 * fb) * 1024, n, prev_boff=(2 * fb + 1) * 1024)
        done_slab()
        kb.cp("act", m["ks"][:, :, :n], v4(Pk, n), [Pk], [m["ks"]])
        kb.tt("dve", m["kk"][:, :, :n], m["ks"][:, :, :n], bc(pf[:, PF["k_k"]:PF["k_k"] + 4], 2, [128, 4, n]), ALU.mult,
              [m["ks"], pf], [m["kk"]])
        kb.tt("pool", m["t2"][:, :, :n], m["kk"][:, :, :n], m["kk"][:, :, :n], ALU.mult, [m["kk"]], [m["t2"]])
        Pn = nextq()
        kb.mm(Pn, Pn[:, 0:4 * n], bones[:, :], m["t2"][:, :, :n], True, True, [bones, m["t2"]])
        kb.act(m["t2"][:, :, :n], v4(Pn, n), AF.Sqrt, [Pn], [m["t2"]])
        kb.ts("dve", m["t2"][:, :, :n], m["t2"][:, :, :n], 1e-12, None, ALU.max, None, [m["t2"]], [m["t2"]])
        kb.op("dve", lambda g: g.reciprocal(out=m["t2"][:, :, :n], in_=m["t2"][:, :, :n]), r=[m["t2"]], w=[m["t2"]])
        kb.tt("dve", m["kk"][:, :, :n], m["kk"][:, :, :n], m["t2"][:, :, :n], ALU.mult, [m["kk"], m["t2"]], [m["kk"]])
        kb.tt("pool", m["t1"][:, :, :n], m["asig"][:, :, :n], bc(pf[:, PF["k_a"]:PF["k_a"] + 4], 2, [128, 4, n]), ALU.mult,
              [m["asig"], pf], [m["t1"]])
        kb.tt("pool", m["t1"][:, :, :n], m["t1"][:, :, :n], bc(pf2[:, OM_KA:OM_KA + 4], 2, [128, 4, n]), ALU.add,
              [m["t1"], pf2], [m["t1"]])
        kb.tt("pool", m["kmod"][:, :, :n], m["ks"][:, :, :n], m["t1"][:, :, :n], ALU.mult, [m["ks"], m["t1"]], [m["kmod"]])
        kb.stt("dve", m["AT"][:, :, :n], m["kk"][:, :, :n], -1.0, m["eWm"][:, :, :n], ALU.mult, ALU.mult, [m["kk"], m["eWm"]], [m["AT"]])
        kb.tt("dve", m["t1"][:, :, :n], m["kk"][:, :, :n], m["asig"][:, :, :n], ALU.mult, [m["kk"], m["asig"]], [m["t1"]])
        kb.tt("dve", m["BT"][:, :, :n], m["t1"][:, :, :n], m["eWi"][:, :, :n], ALU.mult, [m["t1"], m["eWi"]], [m["BT"]])
        kb.tt("pool", m["KT"][:, :, :n], m["kmod"][:, :, :n], m["eWi"][:, :, :n], ALU.mult, [m["kmod"], m["eWi"]], [m["KT"]])
        sl = get_slab()
        Pv_ = nextq()
        for fb in range(4):
            proj_block(Pv_, v4(Pv_, n)[:, fb, :], sl, (2 * fb) * 1024, n, prev_boff=(2 * fb + 1) * 1024)
        done_slab()
        kb.cp("act", m["vs"][:, :, :n], v4(Pv_, n), [Pv_], [m["vs"]])
        kb.cp("pool", m["VT"][:, :, :n], m["vs"][:, :, :n], [m["vs"]], [m["VT"]])
        kb.tt("dve", m["t1"][:, :, :n], m["rs"][:, :, :n], m["kmod"][:, :, :n], ALU.mult, [m["rs"], m["kmod"]], [m["t1"]])
        kb.tt("dve", m["t1"][:, :, :n], m["t1"][:, :, :n], bc(pf[:, PF["r_k"]:PF["r_k"] + 4], 2, [128, 4, n]), ALU.mult,
              [m["t1"], pf], [m["t1"]])
        Pb_ = nextq()
        kb.mm(Pb_, Pb_[:, 0:4 * n], bones[:, :], m["t1"][:, :, :n], True, True, [bones, m["t1"]])
        kb.tt("dve", m["bonus"][:, :, :n], v4(Pb_, n), m["vs"][:, :, :n], ALU.mult, [Pb_, m["vs"]], [m["bonus"]])
        if stop == "B4":
            kb.barrier()
            return nc
        PY = Q[5]
        if sample:
            for q in range(NS):
                wkv_pass(16, 16 * q, 1 + q, PY, n, q)
        else:
            wkv_pass(128, 0, 0 if last_prompt else None, PY, n, None)
        if stop == "WKV":
            kb.barrier()
            return nc
        kb.cp("act", m["Ys"][:, :, :n], v4(PY, n), [PY], [m["Ys"]])
        Pm = nextq()
        kb.mm(Pm, Pm[:, 0:4 * n], bones[:, :], m["Ys"][:, :, :n], True, True, [bones, m["Ys"]])
        kb.stt("dve", m["cen"][:, :, :n], v4(Pm, n), -1.0 / 64, m["Ys"][:, :, :n], ALU.mult, ALU.add, [Pm, m["Ys"]], [m["cen"]])
        kb.tt("pool", m["t1"][:, :, :n], m["cen"][:, :, :n], m["cen"][:, :, :n], ALU.mult, [m["cen"]], [m["t1"]])
        Pv2 = nextq()
        kb.mm(Pv2, Pv2[:, 0:4 * n], bones[:, :], m["t1"][:, :, :n], True, True, [bones, m["t1"]])
        kb.ts("dve", m["t2"][:, :, :n], v4(Pv2, n), 1.0 / 64, GN_EPS, ALU.mult, ALU.add, [Pv2], [m["t2"]])
        kb.act(m["t2"][:, :, :n], m["t2"][:, :, :n], AF.Sqrt, [m["t2"]], [m["t2"]])
        kb.op("dve", lambda g: g.reciprocal(out=m["t2"][:, :, :n], in_=m["t2"][:, :, :n]), r=[m["t2"]], w=[m["t2"]])
        kb.tt("dve", m["cen"][:, :, :n], m["cen"][:, :, :n], m["t2"][:, :, :n], ALU.mult, [m["cen"], m["t2"]], [m["cen"]])
        kb.tt("pool", m["cen"][:, :, :n], m["cen"][:, :, :n], bc(pf[:, PF["gn_w"]:PF["gn_w"] + 4], 2, [128, 4, n]), ALU.mult,
              [m["cen"], pf], [m["cen"]])
        kb.tt("pool", m["cen"][:, :, :n], m["cen"][:, :, :n], bc(pf[:, PF["gn_b"]:PF["gn_b"] + 4], 2, [128, 4, n]), ALU.add,
              [m["cen"], pf], [m["cen"]])
        kb.tt("dve", m["cen"][:, :, :n], m["cen"][:, :, :n], m["bonus"][:, :, :n], ALU.add, [m["cen"], m["bonus"]], [m["cen"]])
        kb.tt("dve", m["ybT"][:, :, :n], m["cen"][:, :, :n], m["gs"][:, :, :n], ALU.mult, [m["cen"], m["gs"]], [m["ybT"]])
        if stop == "GN":
            kb.barrier()
            return nc
        for j, dst in enumerate((m["sga"], m["sgb"])):
            sl = get_slab()
            for half in range(2):
                P = nextq()
                for o in range(4):
                    proj_block(P, v4(P, n)[:, o, :], sl, (half * 4 + o) * 1024, n)
                kb.act(dst[:, half * 4:half * 4 + 4, :n], v4(P, n), AF.Sigmoid, [P], [dst])
            done_slab()
        sl = get_slab()
        for j, (srcT, gate) in enumerate(((m["yaT"], m["sga"]), (m["ybT"], m["sgb"]))):
            for half in range(2):
                P = nextq()
                for o in range(4):
                    ob = half * 4 + o
                    for kc in range(4):
                        off = j * 4096 + ob * 512 + kc * 128
                        kb.mm(P, v4(P, n)[:, o, :], sl[:, off:off + 128], srcT[:, kc, :n], kc == 0, kc == 3, [sl, srcT])
                if j == 0:
                    kb.tt("dve", m["m1"][:, half * 4:half * 4 + 4, :n], v4(P, n), gate[:, half * 4:half * 4 + 4, :n], ALU.mult,
                          [P, gate], [m["m1"]])
                else:
                    kb.tt("dve", m["cen"][:, :, :n], v4(P, n), gate[:, half * 4:half * 4 + 4, :n], ALU.mult, [P, gate], [m["cen"]])
                    kb.tt("pool", m["mT"][:, half * 4:half * 4 + 4, :n], m["cen"][:, :, :n], m["m1"][:, half * 4:half * 4 + 4, :n],
                          ALU.add, [m["cen"], m["m1"]], [m["mT"]])
        done_slab()
        sl = get_slab()
        for half in range(2):
            P = Q[5 + half]
            for kc in range(8):
                kb.mm(P, P[:n, :], m["mT"][:, kc, :n], sl[:, kc * 1024 + half * 512: kc * 1024 + half * 512 + 512], kc == 0, kc == 7,
                      [m["mT"], sl])
            kb.tt("dve", x1[:n, half * 512:(half + 1) * 512], P[:n, :], xin[:n, half * 512:(half + 1) * 512], ALU.add, [P, xin], [x1])
        done_slab()
        kb.barrier(engines=("pe", "act", "dve", "pool"))
        if stop == "MIX":
            kb.barrier()
            return nc
        p = PR
        rmsnorm(x1, g2b, n, xnf)
        kb.cp("act", xnb[:n, :], xnf[:n, :], [xnf], [xnb])
        to_featmajor(xnb, n, p["xn2T"])
        for j in range(2):
            sl = get_slab()
            for half in range(2):
                P = nextq()
                for o in range(4):
                    hp_l = half * 4 + o
                    for dc in range(8):
                        off = dc * 1024 + hp_l * 128
                        kb.mm(P, v4(P, n)[:, o, :], sl[:, off:off + 128], p["xn2T"][:, dc, :n], dc == 0, dc == 7, [sl, p["xn2T"]])
                hp0 = j * 8 + half * 4
                kb.cp("act" if half else "dve", p["qT"][:, hp0:hp0 + 4, :n], v4(P, n), [P], [p["qT"]])
            done_slab()
        for grp in range(4):
            P = nextq()
            for o in range(4):
                hp = grp * 4 + o
                kb.mm(P, P[:n, o * 128:(o + 1) * 128], p["qT"][:, hp, :n], keys_b[:, hp, :], True, True, [p["qT"], keys_b])
            kb.cp("act" if grp % 2 else "dve", p["S"][:n, grp * 4:grp * 4 + 4, :], P[:n, :].rearrange("t (a k) -> t a k", k=128), [P], [p["S"]])
        if stop == "C3":
            kb.barrier()
            return nc
        S, top, idx = p["S"], p["top"], p["idx"]
        for hp in range(16):
            kb.op("dve", lambda g, hp=hp: g.max(out=top[:n, hp, 0:8], in_=S[:n, hp, :]), r=[S], w=[top])
            kb.op("dve", lambda g, hp=hp: g.max_index(out=idx[:n, hp, 0:8], in_max=top[:n, hp, 0:8], in_values=S[:n, hp, :]), r=[S, top], w=[idx])
            kb.op("dve", lambda g, hp=hp: g.match_replace(out=S[:n, hp, :], in_to_replace=top[:n, hp, 0:8], in_values=S[:n, hp, :], imm_value=NEG),
                  r=[top], w=[S])
            kb.op("dve", lambda g, hp=hp: g.max(out=top[:n, hp, 8:16], in_=S[:n, hp, :]), r=[S], w=[top])
            kb.op("dve", lambda g, hp=hp: g.max_index(out=idx[:n, hp, 8:16], in_max=top[:n, hp, 8:16], in_values=S[:n, hp, :]), r=[S, top], w=[idx])
        kb.cp("dve", p["idxf"][:n, :, :], idx[:n, :, :], [idx], [p["idxf"]])
        top4 = top[:n, :, :].rearrange("t (h q) a -> t h q a", q=2)
        idx4 = p["idxf"][:n, :, :].rearrange("t (h q) a -> t h q a", q=2)
        cand = p["cand"]
        cand4 = cand[:n, :, :].rearrange("t h (a b) -> t h a b", b=16)
        kb.tt("dve", cand4, bc(top4[:, :, 0, :], 3, [n, 8, 16, 16]), bc(top4[:, :, 1, :], 2, [n, 8, 16, 16]), ALU.add, [top], [cand])
        sc, pos = p["sc"], p["pos"]
        for h in range(8):
            kb.op("dve", lambda g, h=h: g.max(out=sc[:n, h, 0:8], in_=cand[:n, h, :]), r=[cand], w=[sc])
            kb.op("dve", lambda g, h=h: g.max_index(out=pos[:n, h, 0:8], in_max=sc[:n, h, 0:8], in_values=cand[:n, h, :]), r=[cand, sc], w=[pos])
            kb.op("dve", lambda g, h=h: g.match_replace(out=cand[:n, h, :], in_to_replace=sc[:n, h, 0:8], in_values=cand[:n, h, :], imm_value=NEG),
                  r=[sc], w=[cand])
            kb.op("dve", lambda g, h=h: g.max(out=sc[:n, h, 8:16], in_=cand[:n, h, :]), r=[cand], w=[sc])
            kb.op("dve", lambda g, h=h: g.max_index(out=pos[:n, h, 8:16], in_max=sc[:n, h, 8:16], in_values=cand[:n, h, :]), r=[cand, sc], w=[pos])
        if stop == "C4":
            kb.barrier()
            return nc
        kb.tt("dve", p["ex"][:n, :, :], sc[:n, :, :], sc[:n, :, 0:1].to_broadcast([n, 8, 16]), ALU.subtract, [sc], [p["ex"]])
        kb.act(p["ex"][:n, :, :], p["ex"][:n, :, :], AF.Exp, [p["ex"]], [p["ex"]])
        kb.op("dve", lambda g: g.tensor_reduce(out=p["Z"][:n, :], in_=p["ex"][:n, :, :], axis=AX.X, op=ALU.add), r=[p["ex"]], w=[p["Z"]])
        kb.op("dve", lambda g: g.reciprocal(out=p["Z"][:n, :], in_=p["Z"][:n, :]), r=[p["Z"]], w=[p["Z"]])
        gate3 = p["gate"][:n, :].rearrange("t (h k) -> t h k", k=16)
        kb.tt("dve", gate3, p["ex"][:n, :, :], bc(p["Z"][:n, :], 2, [n, 8, 16]), ALU.mult, [p["ex"], p["Z"]], [p["gate"]])
        kb.op("dve", lambda g: g.tensor_single_scalar(out=p["pa"][:n, :, :], in_=pos[:n, :, :], scalar=4, op=ALU.logical_shift_right), r=[pos], w=[p["pa"]])
        kb.op("dve", lambda g: g.tensor_single_scalar(out=p["pb"][:n, :, :], in_=pos[:n, :, :], scalar=15, op=ALU.bitwise_and), r=[pos], w=[p["pb"]])
        kb.cp("dve", p["paf"][:n, :, :], p["pa"][:n, :, :], [p["pa"]], [p["paf"]])
        kb.cp("dve", p["pbf"][:n, :, :], p["pb"][:n, :, :], [p["pb"]], [p["pbf"]])
        oh = p["oh"]
        iota16 = iota_c[:n, 0:16].unsqueeze(1).unsqueeze(2).to_broadcast([n, 8, 16, 16])
        for (pf_, q_, dst) in ((p["paf"], 0, p["i_f"]), (p["pbf"], 1, p["j_f"])):
            kb.tt("dve", oh[:n], iota16, bc(pf_[:n, :, :], 3, [n, 8, 16, 16]), ALU.is_equal, [iota_c, pf_], [oh])
            kb.tt("dve", oh[:n], oh[:n], bc(idx4[:, :, q_, :], 2, [n, 8, 16, 16]), ALU.mult, [oh, p["idxf"]], [oh])
            kb.op("dve", lambda g, dst=dst: g.tensor_reduce(out=dst[:n, :].rearrange("t (h k) -> t h k", k=16), in_=oh[:n], axis=AX.X, op=ALU.add),
                  r=[oh], w=[dst])
        if stop == "C5":
            kb.barrier()
            return nc
        ijg = p["ijg"]
        kb.cp("dve", ijg[:n, 0, :], p["i_f"][:n, :], [p["i_f"]], [ijg])
        kb.cp("dve", ijg[:n, 1, :], p["j_f"][:n, :], [p["j_f"]], [ijg])
        kb.cp("dve", ijg[:n, 2, :], p["gate"][:n, :], [p["gate"]], [ijg])
        if stop == "C5b":
            kb.barrier()
            return nc
        for k3 in range(3):
            kb.tr(PB, PB[:, k3 * 128:k3 * 128 + n], ijg[:n, k3, :], ident_b[:n, :n], [ijg, ident_b])
        if stop == "C5c":
            kb.barrier()
            return nc
        ijgT = p["ijgT"]
        kb.cp("dve", ijgT[:, :, :n], PB[:, 0:384].rearrange("p (k t) -> p k t", t=128)[:, :, :n], [PB], [ijgT])
        if stop == "C6":
            kb.barrier()
            return nc
        A, Bm, CT = p["A"], p["Bm"], p["CT"]
        TG = 32
        for tg in range(n // TG):
            ts_ = slice(tg * TG, (tg + 1) * TG)
            iob = bc(iota_c[:, :], 1, [128, TG, 128])
            kb.tt("dve", A[:, :, :], iob, bc(ijgT[:, 0, ts_], 2, [128, TG, 128]), ALU.is_equal, [iota_c, ijgT], [A])
            kb.tt("pool", A[:, :, :], A[:, :, :], bc(ijgT[:, 2, ts_], 2, [128, TG, 128]), ALU.mult, [A, ijgT], [A])
            kb.tt("dve", Bm[:, :, :], iob, bc(ijgT[:, 1, ts_], 2, [128, TG, 128]), ALU.is_equal, [iota_c, ijgT], [Bm])
            for t4 in range(TG // 4):
                P = nextq()
                for tt_ in range(4):
                    tl = t4 * 4 + tt_
                    kb.mm(P, P[:, tt_ * 128:(tt_ + 1) * 128], Bm[:, tl, :], A[:, tl, :], True, True, [Bm, A])
                t0 = tg * TG + t4 * 4
                kb.cp("act", CT[:, :, t0:t0 + 4].rearrange("j i t -> j t i"),
                      P[:, :].rearrange("j (t i) -> j t i", i=128), [P], [CT])
        if stop == "C7":
            kb.barrier()
            return nc
        PO = (Q[5], Q[6])
        acts2 = (p["acts"], p["acts1"])
        hT2 = (p["hT"], p["hT1"])
        sl_of = {}

        def emit_U(st_):
            g, half = st_ // 2, st_ % 2
            if half == 0:
                sl_of[g] = (get_slab(), get_slab())
            slU = sl_of[g][0]
            P = nextq()
            for o in range(4):
                il = half * 4 + o
                for dc in range(8):
                    off = dc * 1024 + il * 128
                    kb.mm(P, v4(P, n)[:, o, :], slU[:, off:off + 128], p["xn2T"][:, dc, :n], dc == 0, dc == 7, [slU, p["xn2T"]])
            i0_ = g * 8 + half * 4
            a_, h_ = acts2[st_ % 2], hT2[st_ % 2]
            kb.act(a_[:, :, :n], v4(P, n), AF.Gelu_apprx_tanh, [P], [a_])
            kb.tt("dve", h_[:, :, :n], a_[:, :, :n], CT[:, i0_:i0_ + 4, :n], ALU.mult, [a_, CT], [h_])

        def emit_V(st_):
            g, half = st_ // 2, st_ % 2
            slV = sl_of[g][1]
            h_ = hT2[st_ % 2]
            for o in range(4):
                i = g * 8 + half * 4 + o
                il = half * 4 + o
                for hv in range(2):
                    kb.mm(PO[hv], PO[hv][:n, :], h_[:, o, :n], slV[:, il * 1024 + hv * 512: il * 1024 + hv * 512 + 512],
                          i == 0, i == 127, [h_, slV])
            if half == 1:
                done_slab()
                done_slab()

        for g in range(NGRP):
            emit_U(2 * g)
            emit_U(2 * g + 1)
            emit_V(2 * g)
            emit_V(2 * g + 1)
        if stop == "C8":
            kb.barrier()
            return nc
        x2 = p["x2"]
        for hv in range(2):
            kb.tt("dve", x2[:n, hv * 512:(hv + 1) * 512], PO[hv][:n, :], x1[:n, hv * 512:(hv + 1) * 512], ALU.add, [PO[hv], x1], [x2])
        rmsnorm(x2, gfb, n, xnf)
        dst = ys if sample else yp[ti * 128:(ti + 1) * 128, :]
        kb.dma("pool", dst, xnf[:n, :], r=[xnf], st=xnf)
        kb.barrier(engines=("pe", "act", "dve", "pool"))
    kb.barrier()
    return nc


def _fm(v, c):
    return np.ascontiguousarray(np.asarray(v, np.float32).reshape(c, 128).T)


def make_in_maps(inp, NT=32, n_cores=8):
    f = lambda k: np.asarray(inp[k], np.float32)
    pfa = np.concatenate([
        _fm(f("mu_rkv")[0], 12), _fm(f("w0")[0], 4), _fm(f("a0")[0], 4), _fm(f("k_k")[0], 4), _fm(f("k_a")[0], 4),
        _fm(f("r_k")[0].reshape(-1), 4), _fm(f("gn_w")[0], 4), _fm(f("gn_b")[0], 4),
        _fm(f("conv_w")[0, 0], 4), _fm(f("conv_w")[0, 1], 4), _fm(f("conv_w")[0, 2], 4),
        _fm(f("mu_wag")[0, 0], 8), _fm(f("mu_wag")[0, 1], 8), _fm(f("mu_wag")[0, 2], 8)], axis=1)
    assert pfa.shape == (128, NPF)
    shared = dict(
        w_in=np.ascontiguousarray(f("w_in")[0]), w1=np.ascontiguousarray(f("w1")[0]), a1=np.ascontiguousarray(f("a1")[0]),
        g1=np.ascontiguousarray(f("g1")[0]), w2=np.ascontiguousarray(f("w2")[0]), a2=np.ascontiguousarray(f("a2")[0]),
        g2=np.ascontiguousarray(f("g2")[0]), w_pa=np.ascontiguousarray(f("w_pa")[0]), w_pb=np.ascontiguousarray(f("w_pb")[0]),
        w_o=np.ascontiguousarray(f("w_o")[0]), wq=np.ascontiguousarray(f("peer_wq")[0]),
        keysT=np.ascontiguousarray(f("peer_keys")[0].reshape(16, 128, 128).transpose(2, 0, 1)),
        UT=np.ascontiguousarray(f("peer_u")[0].T), V=np.ascontiguousarray(f("peer_v")[0]),
        pf=np.ascontiguousarray(pfa), mu_rkv_row=np.ascontiguousarray(f("mu_rkv")[0].reshape(1, 1536)),
        n1g=np.ascontiguousarray(f("norm1_g")[0].reshape(1, D)), n2g=np.ascontiguousarray(f("norm2_g")[0].reshape(1, D)),
        nfg=np.ascontiguousarray(f("norm_f_g").reshape(1, D)),
    )
    maps = []
    for c in range(n_cores):
        sl = slice(4 * c, 4 * c + 4)
        m = dict(shared)
        m["xp"] = np.ascontiguousarray(f("x_prompt")[c, :NT * 128])
        m["xs"] = np.ascontiguousarray(f("x_sample")[sl].reshape(64, D))
        m["st_shift"] = np.ascontiguousarray(f("state_shift")[0, sl].reshape(4, 8, 128).transpose(0, 2, 1))
        m["st_conv"] = np.ascontiguousarray(f("state_conv")[0, sl].reshape(4, 2, 4, 128).transpose(3, 2, 0, 1))
        m["st_wkv"] = np.ascontiguousarray(f("state_wkv")[0, sl].reshape(4, 4, 2, 64, 64).transpose(0, 2, 4, 1, 3).reshape(4, 128, 4, 64))
        maps.append(m)
    return maps


def assemble(results, NT=32, n_cores=8):
    T = NT * 128
    y_prompt = np.stack([r["yp"] for r in results]).reshape(n_cores, T, D)
    y_sample = np.concatenate([r["ys"].reshape(4, 16, D) for r in results], axis=0)
    conv = np.stack([r["conv_o"] for r in results])
    conv = conv.transpose(0, 3, 4, 2, 1).reshape(n_cores, 5, 2, 512)
    shift = np.stack([r["shift_o"] for r in results])
    wkv = np.stack([r["wkv_o"] for r in results])
    wkv = wkv.reshape(n_cores, 5, 2, 64, 4, 64).transpose(0, 1, 4, 2, 5, 3).reshape(n_cores, 5, 8, 64, 64)
    f32 = np.float32
    return (y_prompt.astype(f32), y_sample.astype(f32),
            np.ascontiguousarray(conv[:, 0])[None].astype(f32), np.ascontiguousarray(shift[:, 0])[None].astype(f32),
            np.ascontiguousarray(wkv[:, 0])[None].astype(f32),
            np.ascontiguousarray(conv[:, 1:].reshape(n_cores * 4, 2, 512))[None].astype(f32),
            np.ascontiguousarray(shift[:, 1:].reshape(n_cores * 4, D))[None].astype(f32),
            np.ascontiguousarray(wkv[:, 1:].reshape(n_cores * 4, 8, 64, 64))[None].astype(f32))


def kernel(**inputs):
    nc = build(32)
    maps = make_in_maps(inputs, 32)
    res = run_bass_kernel_spmd(nc, maps, core_ids=list(range(8)))
    return assemble(res.results, 32)
```

```python
import numpy as np
import concourse.bass as bass
import concourse.mybir as mybir
from concourse.bass_utils import run_bass_kernel_spmd

F32 = mybir.dt.float32
BF16 = mybir.dt.bfloat16
I32 = mybir.dt.int32
U32 = mybir.dt.uint32
AF = mybir.ActivationFunctionType
ALU = mybir.AluOpType
AX = mybir.AxisListType

D = 1024
NSLAB_FIXED = 11
NGRP = 16
NSLAB = NSLAB_FIXED + 2 * NGRP
SLABW = 8192
RMS_EPS = 1e-6
GN_EPS = 64e-5
C0 = float(np.exp(-0.5))
NEG = -1e30

PF = {}
_o = 0
for _nm, _w in (("mu_rkv", 12), ("w0", 4), ("a0", 4), ("k_k", 4), ("k_a", 4), ("r_k", 4), ("gn_w", 4),
                ("gn_b", 4), ("cw0", 4), ("cw1", 4), ("cw2", 4), ("muw", 8), ("mua", 8), ("mug", 8)):
    PF[_nm] = _o
    _o += _w
NPF = _o


class TT:
    def __init__(self, h, name):
        self.h = h
        self.name = name
        self.w = None
        self.r = {}
        self.dsem = None
        self.dcnt = 0

    def __getitem__(self, k):
        return self.h[k]


class KB:
    def __init__(self):
        nc = self.nc = bass.Bass("TRN2", target_bir_lowering=False)
        self.E = dict(pe=nc.tensor, act=nc.scalar, dve=nc.vector, pool=nc.gpsimd, sp=nc.sync)
        self.sem = {e: nc.alloc_semaphore("c_" + e) for e in ("pe", "act", "dve", "pool")}
        self.cnt = dict.fromkeys(self.sem, 0)
        self.waited = {}
        self.dts = []
        self.off = 16512
        self.top = 229344
        self.nid = 0

    def sb(self, name, shape, dt, off=None):
        esz = 4 if dt in (F32, I32, U32) else 2
        nbytes = int(np.prod(shape[1:])) * esz
        nbytes = (nbytes + 63) // 64 * 64
        if off is None:
            off = self.off
            self.off += nbytes
            assert self.off <= self.top, (name, self.off)
        h = self.nc.alloc_sbuf_tensor_at(name, list(shape), dt, offset=off)
        t = TT(h, name)
        t.nbytes = nbytes
        t.off = off
        return t

    def ps(self, name, shape, dt):
        return TT(self.nc.alloc_psum_tensor(name, list(shape), dt), name)

    def dram(self, name, shape, dt, kind):
        return TT(self.nc.dram_tensor(name, list(shape), dt, kind=kind).ap(), name)

    def _wait(self, e, evs):
        for (sm, key, val) in evs:
            k = (e, key)
            if self.waited.get(k, 0) < val:
                self.E[e].wait_ge(sm, val)
                self.waited[k] = val

    @staticmethod
    def _deps(r, w):
        evs = []
        for t in r:
            if t.w is not None:
                evs.append(t.w)
        for t in w:
            if t.w is not None:
                evs.append(t.w)
            evs.extend(t.r.values())
        return evs

    def op(self, e, fn, r=(), w=()):
        evs = self._deps(r, w)
        if e == "pe":
            evs = [ev for ev in evs if ev[1] != "c_pe"]
        self._wait(e, evs)
        inst = fn(self.E[e])
        self.cnt[e] += 1
        inst.then_inc(self.sem[e], 1)
        ev = (self.sem[e], "c_" + e, self.cnt[e])
        for t in r:
            t.r[ev[1]] = ev
        for t in w:
            t.w = ev
            t.r = {}
        return inst

    def dma(self, q, out, in_, r=(), w=(), st=None):
        evs = self._deps(r, w)
        self._wait(q, evs)
        t = st if st is not None else (w[0] if w else r[0])
        kind = "sw" if q == "pool" else "hw"
        if not hasattr(t, "ds"):
            t.ds = {}
        if kind not in t.ds:
            nm = "d%s_%s" % (kind, t.name)
            t.ds[kind] = [self.nc.alloc_semaphore(nm), nm, 0]
            self.dts.append(t.ds[kind])
        d = t.ds[kind]
        inst = self.E[q].dma_start(out=out, in_=in_)
        d[2] += 16
        inst.then_inc(d[0], 16)
        ev = (d[0], d[1], d[2])
        for x in r:
            x.r[ev[1]] = ev
        for x in w:
            x.w = ev
            x.r = {}

    def barrier(self, engines=("pe", "act", "dve", "pool", "sp")):
        evs = [(self.sem[e], "c_" + e, self.cnt[e]) for e in self.sem if self.cnt[e] > 0]
        evs += [(d[0], d[1], d[2]) for d in self.dts]
        for e in engines:
            self._wait(e, [ev for ev in evs if ev[1] != "c_" + e])

    def mm(self, P, out, lhsT, rhs, start, stop, r):
        self.op("pe", lambda e: e.matmul(out, lhsT, rhs, start=start, stop=stop), r=r, w=[P])

    def tr(self, P, out, in_, ident, r):
        self.op("pe", lambda e: e.transpose(out, in_, ident), r=r, w=[P])

    def tt(self, e, out, a, b, op, r, w):
        self.op(e, lambda g: g.tensor_tensor(out=out, in0=a, in1=b, op=op), r=r, w=w)

    def ts(self, e, out, a, s1, s2, op0, op1, r, w):
        if s2 is None:
            self.op(e, lambda g: g.tensor_scalar(out=out, in0=a, scalar1=s1, scalar2=None, op0=op0), r=r, w=w)
        else:
            self.op(e, lambda g: g.tensor_scalar(out=out, in0=a, scalar1=s1, scalar2=s2, op0=op0, op1=op1), r=r, w=w)

    def stt(self, e, out, a, sc, b, op0, op1, r, w):
        self.op(e, lambda g: g.scalar_tensor_tensor(out=out, in0=a, scalar=sc, in1=b, op0=op0, op1=op1), r=r, w=w)

    def cp(self, e, out, in_, r, w):
        if e == "act":
            self.op(e, lambda g: g.activation(out=out, in_=in_, func=AF.Copy), r=r, w=w)
        else:
            self.op(e, lambda g: g.tensor_copy(out=out, in_=in_), r=r, w=w)

    def act(self, out, in_, func, r, w, bias=None, scale=None, accum=None):
        kw = {}
        if bias is not None:
            kw["bias"] = bias
        if scale is not None:
            kw["scale"] = scale
        if accum is not None:
            kw["accum_out"] = accum
        self.op("act", lambda g: g.activation(out=out, in_=in_, func=func, **kw), r=r, w=w)


def bc(ap, axis, shape):
    return ap.unsqueeze(axis).to_broadcast(list(shape))


def build(NT=32, dbg=False, stop=None):
    kb = KB()
    nc = kb.nc
    NS = 4
    NSEQ = 1 + NS

    def din(name, shape, dt=F32):
        return nc.dram_tensor(name, list(shape), dt, kind="ExternalInput").ap()

    xp = din("xp", [NT * 128, D])
    xs = din("xs", [NS * 16, D])
    st_shift = din("st_shift", [NS, 128, 8])
    st_conv = din("st_conv", [128, 4, NS, 2])
    st_wkv = din("st_wkv", [NS, 128, 4, 64])
    w_in = din("w_in", [D, 5120])
    w1 = din("w1", [D, 64])
    a1 = din("a1", [D, 64])
    g1 = din("g1", [D, 128])
    w2 = din("w2", [64, 512])
    a2 = din("a2", [64, 512])
    g2 = din("g2", [128, 512])
    w_pa = din("w_pa", [512, D])
    w_pb = din("w_pb", [512, D])
    w_o = din("w_o", [D, D])
    wq = din("wq", [D, 2048])
    keysT = din("keysT", [128, 16, 128])
    UT = din("UT", [D, 16384])
    V = din("V", [16384, D])
    pf_d = din("pf", [128, NPF])
    mu_rkv_row = din("mu_rkv_row", [1, 1536])
    n1g = din("n1g", [1, D])
    n2g = din("n2g", [1, D])
    nfg = din("nfg", [1, D])

    def dout(name, shape):
        return nc.dram_tensor(name, list(shape), F32, kind="ExternalOutput").ap()

    yp = dout("yp", [NT * 128, D])
    ys = dout("ys", [NS * 16, D])
    conv_o = dout("conv_o", [128, 4, NSEQ, 2])
    shift_o = dout("shift_o", [NSEQ, D])
    wkv_o = dout("wkv_o", [NSEQ, 128, 4, 64])
    WS = nc.dram_tensor("WS", [NSLAB, 128, SLABW], BF16, kind="Internal").ap()
    OUTS = TT(None, "outs")

    Q = [kb.ps("Q%d" % i, [128, 512], F32) for i in range(7)]
    PB = kb.ps("PB", [128, 1024], BF16)
    qrr = [0]

    def nextq():
        q = Q[qrr[0] % 5]
        qrr[0] += 1
        return q

    ident_f = kb.sb("ident_f", [128, 128], F32)
    ident_b = kb.sb("ident_b", [128, 128], BF16)
    iota_c = kb.sb("iota_c", [128, 128], F32)
    m_su = kb.sb("m_su", [128, 128], F32)
    m_sl = kb.sb("m_sl", [128, 128], F32)
    m_ui = kb.sb("m_ui", [128, 128], F32)
    bones = kb.sb("bones", [128, 128], F32)
    ones_f = kb.sb("ones_f", [128, 128], F32)
    pf = kb.sb("pf", [128, NPF], F32)
    pf2 = kb.sb("pf2", [128, 64], F32)
    g1b = kb.sb("g1b", [128, D], F32)
    g2b = kb.sb("g2b", [128, D], F32)
    gfb = kb.sb("gfb", [128, D], F32)
    keys_b = kb.sb("keys_b", [128, 16, 128], BF16)
    w2a2 = kb.sb("w2a2", [128, 512], BF16)
    g2_b = kb.sb("g2_b", [128, 512], BF16)
    slabs = [kb.sb("slab%d" % i, [128, SLABW], BF16) for i in range(3)]
    xt = [kb.sb("xt%d" % i, [128, D], F32) for i in range(2)]
    x1 = kb.sb("x1", [128, D], F32)
    sq = kb.sb("sq", [128, D], F32)
    xnf = kb.sb("xnf", [128, D], F32)
    xnb = kb.sb("xnb", [128, D], BF16)
    xnT = kb.sb("xnT", [128, 8, 128], BF16)
    xpT = kb.sb("xpT", [128, 8, 128], BF16)
    carryx = kb.sb("carryx", [128, 8, 1], BF16)
    stat = kb.sb("stat", [128, 8], F32)
    uext = kb.sb("uext", [128, 4, 130], F32)
    uexs = kb.sb("uexs", [128, 4, NS, 18], F32)
    Hf = kb.sb("Hf", [128, 4, 128], F32)
    Hb = kb.sb("Hb", [128, 4, 128], BF16)
    stS = kb.sb("stS", [128, NS, 8], F32)
    tmpH = kb.sb("tmpH", [128, 4, 128], F32)
    pm = kb.sb("pm", [128, 2], F32)
    cm = [kb.sb("cm%d" % i, [128, 512], BF16) for i in range(2)]
    ARENA = kb.off

    ii = kb.sb("ii", [128, 128], I32)
    ip = kb.sb("ip", [128, 128], I32)
    rowf = kb.sb("rowf", [128, 128], F32)
    rb = kb.sb("rb", [128, 128], F32)
    cb = kb.sb("cb", [128, 128], F32)
    kb.op("pool", lambda g: g.iota(ii[:, :], pattern=[[1, 128]], base=0, channel_multiplier=0), w=[ii])
    kb.op("pool", lambda g: g.iota(ip[:, :], pattern=[[0, 128]], base=0, channel_multiplier=1), w=[ip])
    kb.cp("dve", iota_c[:, :], ii[:, :], [ii], [iota_c])
    kb.cp("dve", rowf[:, :], ip[:, :], [ip], [rowf])
    kb.tt("dve", ident_f[:, :], rowf[:, :], iota_c[:, :], ALU.is_equal, [rowf, iota_c], [ident_f])
    kb.cp("dve", ident_b[:, :], ident_f[:, :], [ident_f], [ident_b])
    kb.tt("dve", m_su[:, :], rowf[:, :], iota_c[:, :], ALU.is_lt, [rowf, iota_c], [m_su])
    kb.tt("dve", m_sl[:, :], rowf[:, :], iota_c[:, :], ALU.is_gt, [rowf, iota_c], [m_sl])
    kb.tt("dve", m_ui[:, :], rowf[:, :], iota_c[:, :], ALU.is_le, [rowf, iota_c], [m_ui])
    kb.ts("dve", rb[:, :], rowf[:, :], 64.0, None, ALU.is_ge, None, [rowf], [rb])
    kb.ts("dve", cb[:, :], iota_c[:, :], 64.0, None, ALU.is_ge, None, [iota_c], [cb])
    kb.tt("dve", bones[:, :], rb[:, :], cb[:, :], ALU.is_equal, [rb, cb], [bones])
    kb.op("dve", lambda g: g.memset(ones_f[:, :], 1.0), w=[ones_f])
    kb.op("dve", lambda g: g.memset(carryx[:, :, :], 0.0), w=[carryx])
    kb.op("dve", lambda g: g.memset(uext[:, :, :], 0.0), w=[uext])
    kb.op("dve", lambda g: g.memset(Hf[:, :, :], 0.0), w=[Hf])
    kb.op("dve", lambda g: g.memset(Hb[:, :, :], 0.0), w=[Hb])
    kb.dma("sp", pf[:, :], pf_d, w=[pf])
    kb.dma("sp", g1b[:, :], n1g.to_broadcast([128, D]), w=[g1b])
    kb.dma("sp", g2b[:, :], n2g.to_broadcast([128, D]), w=[g2b])
    kb.dma("sp", gfb[:, :], nfg.to_broadcast([128, D]), w=[gfb])
    kb.dma("pool", keys_b[:, :, :], keysT, w=[keys_b])
    kb.dma("pool", w2a2[0:64, :], w2, w=[w2a2])
    kb.dma("pool", w2a2[64:128, :], a2, w=[w2a2])
    kb.dma("pool", g2_b[:, :], g2, w=[g2_b])
    kb.dma("sp", stS[:, :, :], st_shift.rearrange("q p c -> p q c"), w=[stS])
    for fc in range(4):
        kb.dma("sp", uexs[:, fc, :, 0:2], st_conv[:, fc, :, :], w=[uexs])
    kb.cp("dve", pm[:, 1:2], rb[:, 0:1], [rb], [pm])
    kb.ts("dve", pm[:, 0:1], rb[:, 0:1], -1.0, 1.0, ALU.mult, ALU.add, [rb], [pm])
    ii5 = kb.sb("ii5", [128, 512], I32)
    cf5 = kb.sb("cf5", [128, 512], F32)
    kb.op("pool", lambda g: g.iota(ii5[:, :], pattern=[[1, 512]], base=0, channel_multiplier=0), w=[ii5])
    kb.op("dve", lambda g: g.tensor_single_scalar(out=ii5[:, :], in_=ii5[:, :], scalar=6, op=ALU.logical_shift_right), r=[ii5], w=[ii5])
    kb.op("dve", lambda g: g.tensor_single_scalar(out=ii5[:, :], in_=ii5[:, :], scalar=1, op=ALU.bitwise_and), r=[ii5], w=[ii5])
    kb.cp("dve", cf5[:, :], ii5[:, :], [ii5], [cf5])
    kb.cp("dve", cm[1][:, :], cf5[:, :], [cf5], [cm[1]])
    kb.ts("dve", cm[0][:, :], cf5[:, :], -1.0, 1.0, ALU.mult, ALU.add, [cf5], [cm[0]])
    OM_RKV, OM_KA, OM_W, OM_A, OM_G = 0, 12, 16, 24, 32
    for (dst, src, wd) in ((OM_RKV, PF["mu_rkv"], 12), (OM_KA, PF["k_a"], 4), (OM_W, PF["muw"], 8),
                           (OM_A, PF["mua"], 8), (OM_G, PF["mug"], 8)):
        kb.ts("dve", pf2[:, dst:dst + wd], pf[:, src:src + wd], -1.0, 1.0, ALU.mult, ALU.add, [pf], [pf2])

    def pfc(name, j=0):
        c = PF[name] + j
        return pf[:, c:c + 1]

    mub = kb.sb("mub", [128, 1536], F32)
    omub = kb.sb("omub", [128, 1536], F32)
    kb.dma("sp", mub[:, :], mu_rkv_row.to_broadcast([128, 1536]), w=[mub])
    kb.ts("dve", omub[:, :], mub[:, :], -1.0, 1.0, ALU.mult, ALU.add, [mub], [omub])
    asm = [kb.sb("asm%d" % i, [128, SLABW], BF16) for i in range(2)]
    st32 = [kb.sb("st32_%d" % i, [128, 8, 128], F32) for i in range(2)]
    st_i = [0]

    def wcols(c0):
        return w_in[:, c0:c0 + 128].rearrange("(dc p) j -> p dc j", p=128)

    def blk(a, bi):
        return a.h[:, bi * 1024:(bi + 1) * 1024].rearrange("p (dc j) -> p dc j", j=128)

    def plain_block(a, bi, c0):
        kb.dma("pool", blk(a, bi), wcols(c0), w=[a])

    def scaled_pair(a, bi, c0, mcol0):
        s = st32[st_i[0] % 2]
        st_i[0] += 1
        kb.dma("sp", s[:, :, :], wcols(c0), w=[s])
        kb.tt("dve", blk(a, bi), s[:, :, :], bc(omub[:, mcol0:mcol0 + 128], 1, [128, 8, 128]), ALU.mult, [s, omub], [a])
        kb.tt("dve", blk(a, bi + 1), s[:, :, :], bc(mub[:, mcol0:mcol0 + 128], 1, [128, 8, 128]), ALU.mult, [s, mub], [a])

    def store_slab(a, si):
        kb.dma("sp", WS[si], a[:, :], r=[a], st=a)

    a = asm[0]
    for fc in range(4):
        plain_block(a, fc, 512 + fc * 128)
        plain_block(a, 4 + fc, 1024 + fc * 128)
    store_slab(a, 0)
    a = asm[1]
    for fc in range(4):
        plain_block(a, fc, fc * 128)
    s = st32[0]
    kb.dma("sp", s[:, :, 0:64], w1.rearrange("(dc p) j -> p dc j", p=128), w=[s])
    kb.dma("sp", s[:, :, 64:128], a1.rearrange("(dc p) j -> p dc j", p=128), w=[s])
    s2 = st32[1]
    kb.dma("sp", s2[:, :, :], g1.rearrange("(dc p) j -> p dc j", p=128), w=[s2])

    def mu3(tile, c0, width):
        return bc(tile[:, c0:c0 + 8], 2, [128, 8, width])

    kb.tt("dve", blk(a, 4)[:, :, 0:64], s[:, :, 0:64], mu3(pf2, OM_W, 64), ALU.mult, [s, pf2], [a])
    kb.tt("dve", blk(a, 4)[:, :, 64:128], s[:, :, 64:128], mu3(pf2, OM_A, 64), ALU.mult, [s, pf2], [a])
    kb.tt("dve", blk(a, 5), s2[:, :, :], mu3(pf2, OM_G, 128), ALU.mult, [s2, pf2], [a])
    kb.tt("dve", blk(a, 6)[:, :, 0:64], s[:, :, 0:64], mu3(pf, PF["muw"], 64), ALU.mult, [s, pf], [a])
    kb.tt("dve", blk(a, 6)[:, :, 64:128], s[:, :, 64:128], mu3(pf, PF["mua"], 64), ALU.mult, [s, pf], [a])
    kb.tt("dve", blk(a, 7), s2[:, :, :], mu3(pf, PF["mug"], 128), ALU.mult, [s2, pf], [a])
    store_slab(a, 1)
    for j in range(3):
        a = asm[j % 2]
        for fb in range(4):
            scaled_pair(a, 2 * fb, 1536 + j * 512 + fb * 128, j * 512 + fb * 128)
        store_slab(a, 2 + j)
    for j in range(2):
        a = asm[(j + 1) % 2]
        for ob in range(8):
            plain_block(a, ob, 3072 + j * 1024 + ob * 128)
        store_slab(a, 5 + j)
    a = asm[1]
    for j, wsrc in enumerate((w_pa, w_pb)):
        for ob in range(8):
            kb.dma("pool", a.h[:, j * 4096 + ob * 512: j * 4096 + (ob + 1) * 512].rearrange("p (kc j) -> p kc j", j=128),
                   wsrc[:, ob * 128:(ob + 1) * 128].rearrange("(kc p) j -> p kc j", p=128), w=[a])
    store_slab(a, 7)
    a = asm[0]
    kb.dma("pool", a.h[:, :].rearrange("p (kc j) -> p kc j", j=1024), w_o.rearrange("(kc p) j -> p kc j", p=128), w=[a])
    store_slab(a, 8)
    for j in range(2):
        a = asm[(j + 1) % 2]
        kb.dma("pool", a.h[:, :].rearrange("p (dc j) -> p dc j", j=1024),
               wq[:, j * 1024:(j + 1) * 1024].rearrange("(dc p) j -> p dc j", p=128), w=[a])
        store_slab(a, 9 + j)
    for g in range(NGRP):
        a = asm[0]
        kb.dma("pool", a.h[:, :].rearrange("p (dc e) -> p dc e", e=1024),
               UT[:, g * 1024:(g + 1) * 1024].rearrange("(dc p) e -> p dc e", p=128), w=[a])
        store_slab(a, NSLAB_FIXED + 2 * g)
        a = asm[1]
        kb.dma("pool", a.h[:, :].rearrange("p (i d) -> p i d", d=1024),
               V[g * 1024:(g + 1) * 1024, :].rearrange("(i j) d -> j i d", j=128), w=[a])
        store_slab(a, NSLAB_FIXED + 2 * g + 1)
    kb.barrier()
    if stop == "prologue":
        return nc

    def arena_alloc(specs):
        kb.off = ARENA
        return {nm: kb.sb("ar_" + nm, shp, dt) for (nm, shp, dt) in specs}

    MX = arena_alloc([
        ("zh", [128, 4, 128], BF16), ("cc", [128, 4, 128], F32), ("yaT", [128, 4, 128], BF16),
        ("l01", [128, 128], BF16), ("lg", [128, 128], BF16),
        ("sg", [128, 4, 128], F32), ("asig", [128, 4, 128], F32), ("gs", [128, 4, 128], BF16),
        ("cum", [128, 4, 128], F32), ("eW", [128, 4, 128], F32), ("eWm", [128, 4, 128], F32), ("eWi", [128, 4, 128], F32),
        ("rs", [128, 4, 128], F32), ("ks", [128, 4, 128], F32), ("kk", [128, 4, 128], F32), ("kmod", [128, 4, 128], F32),
        ("t1", [128, 4, 128], F32), ("t2", [128, 4, 128], F32), ("vs", [128, 4, 128], F32),
        ("bonus", [128, 4, 128], F32),
        ("RT", [128, 4, 128], BF16), ("AT", [128, 4, 128], BF16), ("BT", [128, 4, 128], BF16), ("KT", [128, 4, 128], BF16),
        ("VT", [128, 4, 128], BF16),
        ("Btok", [128, 512], BF16), ("Ktok", [128, 512], BF16), ("Vtok", [128, 512], BF16),
        ("Ys", [128, 4, 128], F32), ("cen", [128, 4, 128], F32), ("ybT", [128, 4, 128], BF16),
        ("sga", [128, 8, 128], BF16), ("sgb", [128, 8, 128], BF16), ("m1", [128, 8, 128], F32), ("mT", [128, 8, 128], BF16),
        ("M0", [128, 8, 128], BF16), ("M1", [128, 8, 128], BF16), ("M2", [128, 8, 128], BF16), ("M3", [128, 8, 128], BF16),
        ("M4", [128, 8, 128], BF16), ("M5", [128, 8, 128], BF16), ("M6", [128, 8, 128], BF16),
        ("N0", [128, 8, 128], BF16), ("N1", [128, 8, 128], BF16),
        ("AkT", [128, 8, 128], BF16), ("LrbT", [128, 8, 128], BF16), ("LrkT", [128, 8, 128], BF16),
        ("Xb0", [128, 512], BF16), ("Xb1", [128, 512], BF16),
        ("ATz0", [128, 4, 128], BF16), ("ATz1", [128, 4, 128], BF16), ("BTz0", [128, 4, 128], BF16), ("BTz1", [128, 4, 128], BF16),
        ("RTz0", [128, 4, 128], BF16), ("RTz1", [128, 4, 128], BF16),
        ("Vz0", [128, 512], BF16), ("Vz1", [128, 512], BF16), ("Uz0", [128, 512], BF16), ("Uz1", [128, 512], BF16),
    ])
    mx_end = kb.off
    PR = arena_alloc([
        ("xn2T", [128, 8, 128], BF16), ("qT", [128, 16, 128], BF16),
        ("S", [128, 16, 128], F32), ("top", [128, 16, 16], F32), ("idx", [128, 16, 16], U32), ("idxf", [128, 16, 16], F32),
        ("cand", [128, 8, 256], F32), ("sc", [128, 8, 16], F32), ("pos", [128, 8, 16], U32),
        ("pa", [128, 8, 16], U32), ("pb", [128, 8, 16], U32), ("paf", [128, 8, 16], F32), ("pbf", [128, 8, 16], F32),
        ("oh", [128, 8, 16, 16], F32), ("ex", [128, 8, 16], F32), ("Z", [128, 8], F32),
        ("i_f", [128, 128], F32), ("j_f", [128, 128], F32), ("gate", [128, 128], F32),
        ("iT", [128, 128], F32), ("jT", [128, 128], F32), ("gT", [128, 128], BF16), ("ijg", [128, 3, 128], BF16), ("ijgT", [128, 3, 128], BF16),
        ("A", [128, 32, 128], BF16), ("Bm", [128, 32, 128], BF16),
        ("CT", [128, 128, 128], BF16), ("acts", [128, 4, 128], BF16), ("hT", [128, 4, 128], BF16), ("acts1", [128, 4, 128], BF16), ("hT1", [128, 4, 128], BF16),
        ("x2", [128, D], F32),
    ])
    pr_end = kb.off
    kb.off = max(mx_end, pr_end)
    assert kb.off <= kb.top, kb.off

    slab_state = {"next": 0}
    total_slabs = (NT + 1) * NSLAB

    def issue_slab():
        k = slab_state["next"]
        if k >= total_slabs:
            return
        buf = slabs[k % 3]
        kb.dma("sp", buf[:, :], WS[k % NSLAB], w=[buf])
        slab_state["next"] = k + 1

    slab_use = {"k": 0}

    def get_slab():
        k = slab_use["k"]
        while slab_state["next"] <= k + 1 and slab_state["next"] < total_slabs:
            issue_slab()
        slab_use["k"] = k + 1
        return slabs[k % 3]

    def done_slab():
        issue_slab()

    def rmsnorm(xin, gb, n, out_f):
        kb.act(sq[:n, :], xin[:n, :], AF.Square, [xin], [sq, stat], accum=stat[:n, 0:1])
        kb.ts("dve", stat[:n, 1:2], stat[:n, 0:1], 1.0 / D, RMS_EPS, ALU.mult, ALU.add, [stat], [stat])
        kb.act(stat[:n, 2:3], stat[:n, 1:2], AF.Sqrt, [stat], [stat])
        kb.op("dve", lambda g: g.reciprocal(out=stat[:n, 3:4], in_=stat[:n, 2:3]), r=[stat], w=[stat])
        kb.stt("dve", out_f[:n, :], xin[:n, :], stat[:n, 3:4], gb[:n, :], ALU.mult, ALU.mult, [xin, stat, gb], [out_f])

    def to_featmajor(src_b, n, dstT):
        for dc in range(8):
            kb.tr(PB, PB[:, dc * 128: dc * 128 + n], src_b[:n, dc * 128:(dc + 1) * 128], ident_b[:n, :n], [src_b, ident_b])
        kb.cp("act", dstT[:, :, :n], PB[:, :].rearrange("p (c t) -> p c t", t=128)[:, :, :n], [PB], [dstT])

    def proj_block(P, out, slab, boff, n, prev_boff=None):
        nmm = 8 if prev_boff is None else 16
        i = 0
        for dc in range(8):
            kb.mm(P, out, slab[:, boff + dc * 128: boff + (dc + 1) * 128], xnT[:, dc, :n], i == 0, i == nmm - 1, [slab, xnT])
            i += 1
        if prev_boff is not None:
            for dc in range(8):
                kb.mm(P, out, slab[:, prev_boff + dc * 128: prev_boff + (dc + 1) * 128], xpT[:, dc, :n], False, i == nmm - 1, [slab, xpT])
                i += 1

    def v4(P, n):
        return P.h[:, 0:4 * n].rearrange("p (a t) -> p a t", t=n)

    def wkv_pass(n, c0, seq_slot, PY, ntot, load_q):
        m = MX
        nl = int(np.log2(n))
        cols = slice(c0, c0 + n)
        Ml = [m["M%d" % l] for l in range(7)]
        Nl = [m["N0"], m["N1"]]
        if load_q is not None:
            kb.op("dve", lambda g: g.memset(Hf[:, :, :], 0.0), w=[Hf])
            kb.dma("sp", Hf[0:64, :, 0:64], st_wkv[load_q, 0:64, :, :], w=[Hf])
            kb.dma("sp", Hf[64:128, :, 64:128], st_wkv[load_q, 64:128, :, :], w=[Hf])
        kb.cp("act", Hb[:, :, :], Hf[:, :, :], [Hf], [Hb])
        if c0 == 0:
            for nm in ("AT", "BT", "RT"):
                for par in range(2):
                    kb.ts("pool", m[nm + "z%d" % par][:, :, :ntot], m[nm][:, :, :ntot], pm[:, par:par + 1], None, ALU.mult, None,
                          [m[nm], pm], [m[nm + "z%d" % par]])
        for (src, dst) in ((m["BT"], m["Btok"]), (m["KT"], m["Ktok"]), (m["VT"], m["Vtok"])):
            for fb in range(4):
                kb.tr(PB, PB[:n, fb * 128:(fb + 1) * 128], src[:, fb, cols], ident_b[:, :], [src, ident_b])
            kb.cp("dve", dst[:n, :], PB[:n, 0:512], [PB], [dst])
        for par in range(2):
            kb.tt("pool", m["Vz%d" % par][:n, :], m["Vtok"][:n, :], cm[par][:n, :], ALU.mult, [m["Vtok"], cm[par]], [m["Vz%d" % par]])

        def pair_products(lT, rname, mask, dst):
            for half in range(2):
                P = nextq()
                for hh in range(4):
                    h = half * 4 + hh
                    b, par = h // 2, h % 2
                    rz = m[rname + "z%d" % par]
                    kb.mm(P, P[:n, hh * n:(hh + 1) * n], lT[:, b, cols], rz[:, b, cols], True, True, [lT, rz])
                kb.tt("dve", dst[:n, half * 4:half * 4 + 4, :n], v4(P, n)[:n], bc(mask[:n, :n], 1, [n, 4, n]), ALU.mult, [P, mask], [dst])

        pair_products(m["BT"], "AT", m_su, Ml[0])
        pair_products(m["AT"], "BT", m_sl, Nl[0])
        pair_products(m["KT"], "AT", m_su, m["AkT"])
        pair_products(m["BT"], "RT", m_ui, m["LrbT"])
        pair_products(m["KT"], "RT", m_ui, m["LrkT"])
        for l in range(nl - 1):
            Mc, Nc = Ml[l], Nl[l % 2]
            for half in range(2):
                P = nextq()
                for hh in range(4):
                    h = half * 4 + hh
                    kb.mm(P, P[:n, hh * n:(hh + 1) * n], Nc[:n, h, :n], Mc[:n, h, :n], True, True, [Nc, Mc])
                kb.cp("act" if half else "dve", Ml[l + 1][:n, half * 4:half * 4 + 4, :n], v4(P, n)[:n], [P], [Ml[l + 1]])
            if l + 1 < nl - 1:
                Nn = Nl[(l + 1) % 2]
                for half in range(2):
                    P = nextq()
                    for hh in range(4):
                        h = half * 4 + hh
                        kb.mm(P, P[:n, hh * n:(hh + 1) * n], Mc[:n, h, :n], Nc[:n, h, :n], True, True, [Nc, Mc])
                    kb.cp("act" if half else "dve", Nn[:n, half * 4:half * 4 + 4, :n], v4(P, n)[:n], [P], [Nn])
        P = nextq()
        for h in range(8):
            b, par = h // 2, h % 2
            hc = slice(h * 64, (h + 1) * 64)
            kb.mm(P, P[:n, hc], m["AT"][:, b, cols], Hb[:, b, par * 64:(par + 1) * 64], True, False, [m["AT"], Hb])
            kb.mm(P, P[:n, hc], m["AkT"][:n, h, :n], m["Vtok"][:n, hc], False, True, [m["AkT"], m["Vtok"]])
        Xc = m["Xb0"]
        kb.cp("dve", Xc[:n, :], P[:n, :], [P], [Xc])
        for l in range(nl):
            P = nextq()
            for h in range(8):
                hc = slice(h * 64, (h + 1) * 64)
                kb.mm(P, P[:n, hc], ident_b[:n, :n], Xc[:n, hc], True, False, [ident_b, Xc])
                kb.mm(P, P[:n, hc], Ml[l][:n, h, :n], Xc[:n, hc], False, True, [Ml[l], Xc])
            Xn = m["Xb1"] if Xc is m["Xb0"] else m["Xb0"]
            kb.cp("dve", Xn[:n, :], P[:n, :], [P], [Xn])
            Xc = Xn
        U = Xc
        for par in range(2):
            kb.tt("pool", m["Uz%d" % par][:n, :], U[:n, :], cm[par][:n, :], ALU.mult, [U, cm[par]], [m["Uz%d" % par]])
        PYv = PY.h[:, 0:4 * ntot].rearrange("p (a t) -> p a t", t=ntot)
        for b in range(4):
            o = PYv[:, b, cols]
            bs = slice(b * 128, (b + 1) * 128)
            kb.mm(PY, o, Hb[:, b, :], m["RT"][:, b, cols], True, False, [Hb, m["RT"]])
            for par in range(2):
                h = 2 * b + par
                Uz, Vz = m["Uz%d" % par], m["Vz%d" % par]
                kb.mm(PY, o, Uz[:n, bs], m["LrbT"][:n, h, :n], False, False, [Uz, m["LrbT"]])
                kb.mm(PY, o, Vz[:n, bs], m["LrkT"][:n, h, :n], False, par == 1, [Vz, m["LrkT"]])
        P = nextq()
        Pv = P.h[:, 0:512].rearrange("p (a v) -> p a v", v=128)
        for b in range(4):
            bs = slice(b * 128, (b + 1) * 128)
            kb.mm(P, Pv[:, b, :], m["Btok"][:n, bs], U[:n, bs], True, False, [m["Btok"], U])
            kb.mm(P, Pv[:, b, :], m["Ktok"][:n, bs], m["Vtok"][:n, bs], False, True, [m["Ktok"], m["Vtok"]])
        kb.tt("dve", tmpH[:, :, :], Pv, Hf[:, :, :], ALU.add, [P, Hf], [tmpH])
        kb.tt("dve", tmpH[:, :, :], tmpH[:, :, :], m["eW"][:, :, c0 + n - 1:c0 + n].to_broadcast([128, 4, 128]), ALU.mult,
              [tmpH, m["eW"]], [tmpH])
        kb.tt("dve", Hf[:, :, :], tmpH[:, :, :], bc(bones[:, :], 1, [128, 4, 128]), ALU.mult, [tmpH, bones], [Hf])
        if seq_slot is not None:
            kb.dma("pool", wkv_o[seq_slot, 0:64, :, :], Hf[0:64, :, 0:64], r=[Hf], st=Hf)
            kb.dma("pool", wkv_o[seq_slot, 64:128, :, :], Hf[64:128, :, 64:128], r=[Hf], st=Hf)

    for ti in range(NT + 1):
        sample = ti == NT
        n = 64 if sample else 128
        L = 16 if sample else 128
        nseq = NS if sample else 1
        last_prompt = ti == NT - 1
        xin = xt[ti % 2]
        m = MX
        src = xs if sample else xp[ti * 128:(ti + 1) * 128, :]
        kb.dma("sp", xin[:n, :], src, w=[xin])
        rmsnorm(xin, g1b, n, xnf)
        kb.cp("act", xnb[:n, :], xnf[:n, :], [xnf], [xnb])
        if sample:
            for q in range(NS):
                kb.dma("pool", shift_o[1 + q:2 + q, :], xnf[16 * q + 15:16 * q + 16, :], r=[xnf], st=xnf)
        elif last_prompt:
            kb.dma("pool", shift_o[0:1, :], xnf[127:128, :], r=[xnf], st=xnf)
        to_featmajor(xnb, n, xnT)
        kb.cp("dve", xpT[:, :, 1:n], xnT[:, :, 0:n - 1], [xnT], [xpT])
        if sample:
            kb.cp("dve", xpT[:, :, 0:n].rearrange("p c (q l) -> p c q l", l=16)[:, :, :, 0],
                  stS[:, :, :].rearrange("p q c -> p c q"), [stS], [xpT])
        else:
            kb.cp("dve", xpT[:, :, 0:1], carryx[:, :, :], [carryx], [xpT])
            kb.cp("dve", carryx[:, :, :], xnT[:, :, n - 1:n], [xnT], [carryx])
        if stop == "A":
            kb.barrier()
            return nc
        ue = uexs if sample else uext
        if sample:
            ucur = ue[:, :, :, 2:18]
            uv = lambda a, b_: ue[:, :, :, a:b_]
            v3 = lambda t: t[:, :, :n].rearrange("p f (q l) -> p f q l", l=16)
        else:
            ucur = ue[:, :, 2:130]
            uv = lambda a, b_: ue[:, :, a:b_]
            v3 = lambda t: t[:, :, :n]
        sl = get_slab()
        Pc, Ph = nextq(), nextq()
        for fc in range(4):
            proj_block(Pc, v4(Pc, n)[:, fc, :], sl, fc * 1024, n)
        for fc in range(4):
            proj_block(Ph, v4(Ph, n)[:, fc, :], sl, (4 + fc) * 1024, n)
        done_slab()
        kb.cp("act", m["zh"][:, :, :n], v4(Ph, n), [Ph], [m["zh"]])
        kb.tt("dve", ucur, v3(v4(Pc, n)) if sample else v4(Pc, n), v3(m["zh"]), ALU.mult, [Pc, m["zh"]], [ue])
        cw = lambda j: bc(pf[:, PF["cw%d" % j]:PF["cw%d" % j] + 4], 2, [128, 4, n]) if not sample else \
            pf[:, PF["cw%d" % j]:PF["cw%d" % j] + 4].unsqueeze(2).unsqueeze(3).to_broadcast([128, 4, NS, 16])
        kb.tt("pool", v3(m["cc"]), uv(0, L), cw(0), ALU.mult, [ue, pf], [m["cc"]])
        kb.tt("pool", v3(m["t1"]), uv(1, L + 1), cw(1), ALU.mult, [ue, pf], [m["t1"]])
        kb.tt("pool", m["cc"][:, :, :n], m["cc"][:, :, :n], m["t1"][:, :, :n], ALU.add, [m["cc"], m["t1"]], [m["cc"]])
        kb.tt("pool", v3(m["t1"]), uv(2, L + 2), cw(2), ALU.mult, [ue, pf], [m["t1"]])
        kb.tt("pool", m["cc"][:, :, :n], m["cc"][:, :, :n], m["t1"][:, :, :n], ALU.add, [m["cc"], m["t1"]], [m["cc"]])
        if sample:
            for fc in range(4):
                kb.dma("pool", conv_o[:, fc, 1:1 + NS, :], ue[:, fc, :, 16:18], r=[ue], st=ue)
        else:
            if last_prompt:
                kb.dma("pool", conv_o[:, :, 0, :], ue[:, :, 128:130], r=[ue], st=ue)
            kb.cp("pool", ue[:, :, 0:2], ue[:, :, 128:130], [ue], [ue])
        if stop == "B1":
            kb.barrier()
            return nc
        sl = get_slab()
        Pz = nextq()
        for fc in range(4):
            proj_block(Pz, v4(Pz, n)[:, fc, :], sl, fc * 1024, n)
        kb.tt("dve", m["yaT"][:, :, :n], v4(Pz, n), m["cc"][:, :, :n], ALU.mult, [Pz, m["cc"]], [m["yaT"]])
        Pl = nextq()
        proj_block(Pl, Pl[:, 0:n], sl, 4 * 1024, n, prev_boff=6 * 1024)
        proj_block(Pl, Pl[:, n:2 * n], sl, 5 * 1024, n, prev_boff=7 * 1024)
        done_slab()
        kb.act(m["l01"][0:64, :n], Pl[0:64, 0:n], AF.Tanh, [Pl], [m["l01"]])
        kb.act(m["l01"][64:128, :n], Pl[64:128, 0:n], AF.Copy, [Pl], [m["l01"]])
        kb.act(m["lg"][:, :n], Pl[:, n:2 * n], AF.Sigmoid, [Pl], [m["lg"]])
        Pw, Pa, Pg = nextq(), nextq(), nextq()
        for fb in range(4):
            kb.mm(Pw, v4(Pw, n)[:, fb, :], w2a2[0:64, fb * 128:(fb + 1) * 128], m["l01"][0:64, :n], True, True, [w2a2, m["l01"]])
        for fb in range(4):
            kb.mm(Pa, v4(Pa, n)[:, fb, :], w2a2[64:128, fb * 128:(fb + 1) * 128], m["l01"][64:128, :n], True, True, [w2a2, m["l01"]])
        for fb in range(4):
            kb.mm(Pg, v4(Pg, n)[:, fb, :], g2_b[:, fb * 128:(fb + 1) * 128], m["lg"][:, :n], True, True, [g2_b, m["lg"]])
        for fb in range(4):
            kb.act(m["sg"][:, fb, :n], v4(Pw, n)[:, fb, :], AF.Sigmoid, [Pw, pf], [m["sg"]], bias=pfc("w0", fb))
            kb.act(m["asig"][:, fb, :n], v4(Pa, n)[:, fb, :], AF.Sigmoid, [Pa, pf], [m["asig"]], bias=pfc("a0", fb))
        kb.cp("act", m["gs"][:, :, :n], v4(Pg, n), [Pg], [m["gs"]])
        if stop == "B2":
            kb.barrier()
            return nc
        for q in range(nseq):
            for fb in range(4):
                cs = slice(q * L, (q + 1) * L)
                kb.op("dve", lambda g, fb=fb, cs=cs: g.tensor_tensor_scan(
                    out=m["cum"][:, fb, cs], data0=ones_f[:, 0:L], data1=m["sg"][:, fb, cs], initial=0.0,
                    op0=ALU.mult, op1=ALU.add), r=[ones_f, m["sg"]], w=[m["cum"]])
        kb.act(m["eW"][:, :, :n], m["cum"][:, :, :n], AF.Exp, [m["cum"]], [m["eW"]], scale=-C0)
        kb.act(m["eWi"][:, :, :n], m["cum"][:, :, :n], AF.Exp, [m["cum"]], [m["eWi"]], scale=C0)
        kb.tt("pool", m["t1"][:, :, :n], m["cum"][:, :, :n], m["sg"][:, :, :n], ALU.subtract, [m["cum"], m["sg"]], [m["t1"]])
        kb.act(m["eWm"][:, :, :n], m["t1"][:, :, :n], AF.Exp, [m["t1"]], [m["eWm"]], scale=-C0)
        if stop == "B3":
            kb.barrier()
            return nc
        sl = get_slab()
        Pr = nextq()
        for fb in range(4):
            proj_block(Pr, v4(Pr, n)[:, fb, :], sl, (2 * fb) * 1024, n, prev_boff=(2 * fb + 1) * 1024)
        done_slab()
        kb.cp("act", m["rs"][:, :, :n], v4(Pr, n), [Pr], [m["rs"]])
        kb.tt("pool", m["RT"][:, :, :n], m["rs"][:, :, :n], m["eW"][:, :, :n], ALU.mult, [m["rs"], m["eW"]], [m["RT"]])
        sl = get_slab()
        Pk = nextq()
        for fb in range(4):
            proj_block(Pk, v4(Pk, n)[:, fb, :], sl, (2 * fb) * 1024, n, prev_boff=(2 * fb + 1) * 1024)
        done_slab()
        kb.cp("act", m["ks"][:, :, :n], v4(Pk, n), [Pk], [m["ks"]])
        kb.tt("dve", m["kk"][:, :, :n], m["ks"][:, :, :n], bc(pf[:, PF["k_k"]:PF["k_k"] + 4], 2, [128, 4, n]), ALU.mult,
              [m["ks"], pf], [m["kk"]])
        kb.tt("pool", m["t2"][:, :, :n], m["kk"][:, :, :n], m["kk"][:, :, :n], ALU.mult, [m["kk"]], [m["t2"]])
        Pn = nextq()
        kb.mm(Pn, Pn[:, 0:4 * n], bones[:, :], m["t2"][:, :, :n], True, True, [bones, m["t2"]])
        kb.act(m["t2"][:, :, :n], v4(Pn, n), AF.Sqrt, [Pn], [m["t2"]])
        kb.ts("dve", m["t2"][:, :, :n], m["t2"][:, :, :n], 1e-12, None, ALU.max, None, [m["t2"]], [m["t2"]])
        kb.op("dve", lambda g: g.reciprocal(out=m["t2"][:, :, :n], in_=m["t2"][:, :, :n]), r=[m["t2"]], w=[m["t2"]])
        kb.tt("dve", m["kk"][:, :, :n], m["kk"][:, :, :n], m["t2"][:, :, :n], ALU.mult, [m["kk"], m["t2"]], [m["kk"]])
        kb.tt("pool", m["t1"][:, :, :n], m["asig"][:, :, :n], bc(pf[:, PF["k_a"]:PF["k_a"] + 4], 2, [128, 4, n]), ALU.mult,
              [m["asig"], pf], [m["t1"]])
        kb.tt("pool", m["t1"][:, :, :n], m["t1"][:, :, :n], bc(pf2[:, OM_KA:OM_KA + 4], 2, [128, 4, n]), ALU.add,
              [m["t1"], pf2], [m["t1"]])
        kb.tt("pool", m["kmod"][:, :, :n], m["ks"][:, :, :n], m["t1"][:, :, :n], ALU.mult, [m["ks"], m["t1"]], [m["kmod"]])
        kb.stt("dve", m["AT"][:, :, :n], m["kk"][:, :, :n], -1.0, m["eWm"][:, :, :n], ALU.mult, ALU.mult, [m["kk"], m["eWm"]], [m["AT"]])
        kb.tt("dve", m["t1"][:, :, :n], m["kk"][:, :, :n], m["asig"][:, :, :n], ALU.mult, [m["kk"], m["asig"]], [m["t1"]])
        kb.tt("dve", m["BT"][:, :, :n], m["t1"][:, :, :n], m["eWi"][:, :, :n], ALU.mult, [m["t1"], m["eWi"]], [m["BT"]])
        kb.tt("pool", m["KT"][:, :, :n], m["kmod"][:, :, :n], m["eWi"][:, :, :n], ALU.mult, [m["kmod"], m["eWi"]], [m["KT"]])
        sl = get_slab()
        Pv_ = nextq()
        for fb in range(4):
            proj_block(Pv_, v4(Pv_, n)[:, fb, :], sl, (2 * fb) * 1024, n, prev_boff=(2 * fb + 1) * 1024)
        done_slab()
        kb.cp("act", m["vs"][:, :, :n], v4(Pv_, n), [Pv_], [m["vs"]])
        kb.cp("pool", m["VT"][:, :, :n], m["vs"][:, :, :n], [m["vs"]], [m["VT"]])
        kb.tt("dve", m["t1"][:, :, :n], m["rs"][:, :, :n], m["kmod"][:, :, :n], ALU.mult, [m["rs"], m["kmod"]], [m["t1"]])
        kb.tt("dve", m["t1"][:, :, :n], m["t1"][:, :, :n], bc(pf[:, PF["r_k"]:PF["r_k"] + 4], 2, [128, 4, n]), ALU.mult,
              [m["t1"], pf], [m["t1"]])
        Pb_ = nextq()
        kb.mm(Pb_, Pb_[:, 0:4 * n], bones[:, :], m["t1"][:, :, :n], True, True, [bones, m["t1"]])
        kb.tt("dve", m["bonus"][:, :, :n], v4(Pb_, n), m["vs"][:, :, :n], ALU.mult, [Pb_, m["vs"]], [m["bonus"]])
        if stop == "B4":
            kb.barrier()
            return nc
        PY = Q[5]
        if sample:
            for q in range(NS):
                wkv_pass(16, 16 * q, 1 + q, PY, n, q)
        else:
            wkv_pass(128, 0, 0 if last_prompt else None, PY, n, None)
        if stop == "WKV":
            kb.barrier()
            return nc
        kb.cp("act", m["Ys"][:, :, :n], v4(PY, n), [PY], [m["Ys"]])
        Pm = nextq()
        kb.mm(Pm, Pm[:, 0:4 * n], bones[:, :], m["Ys"][:, :, :n], True, True, [bones, m["Ys"]])
        kb.stt("dve", m["cen"][:, :, :n], v4(Pm, n), -1.0 / 64, m["Ys"][:, :, :n], ALU.mult, ALU.add, [Pm, m["Ys"]], [m["cen"]])
        kb.tt("pool", m["t1"][:, :, :n], m["cen"][:, :, :n], m["cen"][:, :, :n], ALU.mult, [m["cen"]], [m["t1"]])
        Pv2 = nextq()
        kb.mm(Pv2, Pv2[:, 0:4 * n], bones[:, :], m["t1"][:, :, :n], True, True, [bones, m["t1"]])
        kb.ts("dve", m["t2"][:, :, :n], v4(Pv2, n), 1.0 / 64, GN_EPS, ALU.mult, ALU.add, [Pv2], [m["t2"]])
        kb.act(m["t2"][:, :, :n], m["t2"][:, :, :n], AF.Sqrt, [m["t2"]], [m["t2"]])
        kb.op("dve", lambda g: g.reciprocal(out=m["t2"][:, :, :n], in_=m["t2"][:, :, :n]), r=[m["t2"]], w=[m["t2"]])
        kb.tt("dve", m["cen"][:, :, :n], m["cen"][:, :, :n], m["t2"][:, :, :n], ALU.mult, [m["cen"], m["t2"]], [m["cen"]])
        kb.tt("pool", m["cen"][:, :, :n], m["cen"][:, :, :n], bc(pf[:, PF["gn_w"]:PF["gn_w"] + 4], 2, [128, 4, n]), ALU.mult,
              [m["cen"], pf], [m["cen"]])
        kb.tt("pool", m["cen"][:, :, :n], m["cen"][:, :, :n], bc(pf[:, PF["gn_b"]:PF["gn_b"] + 4], 2, [128, 4, n]), ALU.add,
              [m["cen"], pf], [m["cen"]])
        kb.tt("dve", m["cen"][:, :, :n], m["cen"][:, :, :n], m["bonus"][:, :, :n], ALU.add, [m["cen"], m["bonus"]], [m["cen"]])
        kb.tt("dve", m["ybT"][:, :, :n], m["cen"][:, :, :n], m["gs"][:, :, :n], ALU.mult, [m["cen"], m["gs"]], [m["ybT"]])
        if stop == "GN":
            kb.barrier()
            return nc
        for j, dst in enumerate((m["sga"], m["sgb"])):
            sl = get_slab()
            for half in range(2):
                P = nextq()
                for o in range(4):
                    proj_block(P, v4(P, n)[:, o, :], sl, (half * 4 + o) * 1024, n)
                kb.act(dst[:, half * 4:half * 4 + 4, :n], v4(P, n), AF.Sigmoid, [P], [dst])
            done_slab()
        sl = get_slab()
        for j, (srcT, gate) in enumerate(((m["yaT"], m["sga"]), (m["ybT"], m["sgb"]))):
            for half in range(2):
                P = nextq()
                for o in range(4):
                    ob = half * 4 + o
                    for kc in range(4):
                        off = j * 4096 + ob * 512 + kc * 128
                        kb.mm(P, v4(P, n)[:, o, :], sl[:, off:off + 128], srcT[:, kc, :n], kc == 0, kc == 3, [sl, srcT])
                if j == 0:
                    kb.tt("dve", m["m1"][:, half * 4:half * 4 + 4, :n], v4(P, n), gate[:, half * 4:half * 4 + 4, :n], ALU.mult,
                          [P, gate], [m["m1"]])
                else:
                    kb.tt("dve", m["cen"][:, :, :n], v4(P, n), gate[:, half * 4:half * 4 + 4, :n], ALU.mult, [P, gate], [m["cen"]])
                    kb.tt("pool", m["mT"][:, half * 4:half * 4 + 4, :n], m["cen"][:, :, :n], m["m1"][:, half * 4:half * 4 + 4, :n],
                          ALU.add, [m["cen"], m["m1"]], [m["mT"]])
        done_slab()
        sl = get_slab()
        for half in range(2):
            P = Q[5 + half]
            for kc in range(8):
                kb.mm(P, P[:n, :], m["mT"][:, kc, :n], sl[:, kc * 1024 + half * 512: kc * 1024 + half * 512 + 512], kc == 0, kc == 7,
                      [m["mT"], sl])
            kb.tt("dve", x1[:n, half * 512:(half + 1) * 512], P[:n, :], xin[:n, half * 512:(half + 1) * 512], ALU.add, [P, xin], [x1])
        done_slab()
        kb.barrier(engines=("pe", "act", "dve", "pool"))
        if stop == "MIX":
            kb.barrier()
            return nc
        p = PR
        rmsnorm(x1, g2b, n, xnf)
        kb.cp("act", xnb[:n, :], xnf[:n, :], [xnf], [xnb])
        to_featmajor(xnb, n, p["xn2T"])
        for j in range(2):
            sl = get_slab()
            for half in range(2):
                P = nextq()
                for o in range(4):
                    hp_l = half * 4 + o
                    for dc in range(8):
                        off = dc * 1024 + hp_l * 128
                        kb.mm(P, v4(P, n)[:, o, :], sl[:, off:off + 128], p["xn2T"][:, dc, :n], dc == 0, dc == 7, [sl, p["xn2T"]])
                hp0 = j * 8 + half * 4
                kb.cp("act" if half else "dve", p["qT"][:, hp0:hp0 + 4, :n], v4(P, n), [P], [p["qT"]])
            done_slab()
        for grp in range(4):
            P = nextq()
            for o in range(4):
                hp = grp * 4 + o
                kb.mm(P, P[:n, o * 128:(o + 1) * 128], p["qT"][:, hp, :n], keys_b[:, hp, :], True, True, [p["qT"], keys_b])
            kb.cp("act" if grp % 2 else "dve", p["S"][:n, grp * 4:grp * 4 + 4, :], P[:n, :].rearrange("t (a k) -> t a k", k=128), [P], [p["S"]])
        if stop == "C3":
            kb.barrier()
            return nc
        S, top, idx = p["S"], p["top"], p["idx"]
        for hp in range(16):
            kb.op("dve", lambda g, hp=hp: g.max(out=top[:n, hp, 0:8], in_=S[:n, hp, :]), r=[S], w=[top])
            kb.op("dve", lambda g, hp=hp: g.max_index(out=idx[:n, hp, 0:8], in_max=top[:n, hp, 0:8], in_values=S[:n, hp, :]), r=[S, top], w=[idx])
            kb.op("dve", lambda g, hp=hp: g.match_replace(out=S[:n, hp, :], in_to_replace=top[:n, hp, 0:8], in_values=S[:n, hp, :], imm_value=NEG),
                  r=[top], w=[S])
            kb.op("dve", lambda g, hp=hp: g.max(out=top[:n, hp, 8:16], in_=S[:n, hp, :]), r=[S], w=[top])
            kb.op("dve", lambda g, hp=hp: g.max_index(out=idx[:n, hp, 8:16], in_max=top[:n, hp, 8:16], in_values=S[:n, hp, :]), r=[S, top], w=[idx])
        kb.cp("dve", p["idxf"][:n, :, :], idx[:n, :, :], [idx], [p["idxf"]])
        top4 = top[:n, :, :].rearrange("t (h q) a -> t h q a", q=2)
        idx4 = p["idxf"][:n, :, :].rearrange("t (h q) a -> t h q a", q=2)
        cand = p["cand"]
        cand4 = cand[:n, :, :].rearrange("t h (a b) -> t h a b", b=16)
        kb.tt("dve", cand4, bc(top4[:, :, 0, :], 3, [n, 8, 16, 16]), bc(top4[:, :, 1, :], 2, [n, 8, 16, 16]), ALU.add, [top], [cand])
        sc, pos = p["sc"], p["pos"]
        for h in range(8):
            kb.op("dve", lambda g, h=h: g.max(out=sc[:n, h, 0:8], in_=cand[:n, h, :]), r=[cand], w=[sc])
            kb.op("dve", lambda g, h=h: g.max_index(out=pos[:n, h, 0:8], in_max=sc[:n, h, 0:8], in_values=cand[:n, h, :]), r=[cand, sc], w=[pos])
            kb.op("dve", lambda g, h=h: g.match_replace(out=cand[:n, h, :], in_to_replace=sc[:n, h, 0:8], in_values=cand[:n, h, :], imm_value=NEG),
                  r=[sc], w=[cand])
            kb.op("dve", lambda g, h=h: g.max(out=sc[:n, h, 8:16], in_=cand[:n, h, :]), r=[cand], w=[sc])
            kb.op("dve", lambda g, h=h: g.max_index(out=pos[:n, h, 8:16], in_max=sc[:n, h, 8:16], in_values=cand[:n, h, :]), r=[cand, sc], w=[pos])
        if stop == "C4":
            kb.barrier()
            return nc
        kb.tt("dve", p["ex"][:n, :, :], sc[:n, :, :], sc[:n, :, 0:1].to_broadcast([n, 8, 16]), ALU.subtract, [sc], [p["ex"]])
        kb.act(p["ex"][:n, :, :], p["ex"][:n, :, :], AF.Exp, [p["ex"]], [p["ex"]])
        kb.op("dve", lambda g: g.tensor_reduce(out=p["Z"][:n, :], in_=p["ex"][:n, :, :], axis=AX.X, op=ALU.add), r=[p["ex"]], w=[p["Z"]])
        kb.op("dve", lambda g: g.reciprocal(out=p["Z"][:n, :], in_=p["Z"][:n, :]), r=[p["Z"]], w=[p["Z"]])
        gate3 = p["gate"][:n, :].rearrange("t (h k) -> t h k", k=16)
        kb.tt("dve", gate3, p["ex"][:n, :, :], bc(p["Z"][:n, :], 2, [n, 8, 16]), ALU.mult, [p["ex"], p["Z"]], [p["gate"]])
        kb.op("dve", lambda g: g.tensor_single_scalar(out=p["pa"][:n, :, :], in_=pos[:n, :, :], scalar=4, op=ALU.logical_shift_right), r=[pos], w=[p["pa"]])
        kb.op("dve", lambda g: g.tensor_single_scalar(out=p["pb"][:n, :, :], in_=pos[:n, :, :], scalar=15, op=ALU.bitwise_and), r=[pos], w=[p["pb"]])
        kb.cp("dve", p["paf"][:n, :, :], p["pa"][:n, :, :], [p["pa"]], [p["paf"]])
        kb.cp("dve", p["pbf"][:n, :, :], p["pb"][:n, :, :], [p["pb"]], [p["pbf"]])
        oh = p["oh"]
        iota16 = iota_c[:n, 0:16].unsqueeze(1).unsqueeze(2).to_broadcast([n, 8, 16, 16])
        for (pf_, q_, dst) in ((p["paf"], 0, p["i_f"]), (p["pbf"], 1, p["j_f"])):
            kb.tt("dve", oh[:n], iota16, bc(pf_[:n, :, :], 3, [n, 8, 16, 16]), ALU.is_equal, [iota_c, pf_], [oh])
            kb.tt("dve", oh[:n], oh[:n], bc(idx4[:, :, q_, :], 2, [n, 8, 16, 16]), ALU.mult, [oh, p["idxf"]], [oh])
            kb.op("dve", lambda g, dst=dst: g.tensor_reduce(out=dst[:n, :].rearrange("t (h k) -> t h k", k=16), in_=oh[:n], axis=AX.X, op=ALU.add),
                  r=[oh], w=[dst])
        if stop == "C5":
            kb.barrier()
            return nc
        ijg = p["ijg"]
        kb.cp("dve", ijg[:n, 0, :], p["i_f"][:n, :], [p["i_f"]], [ijg])
        kb.cp("dve", ijg[:n, 1, :], p["j_f"][:n, :], [p["j_f"]], [ijg])
        kb.cp("dve", ijg[:n, 2, :], p["gate"][:n, :], [p["gate"]], [ijg])
        if stop == "C5b":
            kb.barrier()
            return nc
        for k3 in range(3):
            kb.tr(PB, PB[:, k3 * 128:k3 * 128 + n], ijg[:n, k3, :], ident_b[:n, :n], [ijg, ident_b])
        if stop == "C5c":
            kb.barrier()
            return nc
        ijgT = p["ijgT"]
        kb.cp("dve", ijgT[:, :, :n], PB[:, 0:384].rearrange("p (k t) -> p k t", t=128)[:, :, :n], [PB], [ijgT])
        if stop == "C6":
            kb.barrier()
            return nc
        A, Bm, CT = p["A"], p["Bm"], p["CT"]
        TG = 32
        for tg in range(n // TG):
            ts_ = slice(tg * TG, (tg + 1) * TG)
            iob = bc(iota_c[:, :], 1, [128, TG, 128])
            kb.tt("dve", A[:, :, :], iob, bc(ijgT[:, 0, ts_], 2, [128, TG, 128]), ALU.is_equal, [iota_c, ijgT], [A])
            kb.tt("pool", A[:, :, :], A[:, :, :], bc(ijgT[:, 2, ts_], 2, [128, TG, 128]), ALU.mult, [A, ijgT], [A])
            kb.tt("dve", Bm[:, :, :], iob, bc(ijgT[:, 1, ts_], 2, [128, TG, 128]), ALU.is_equal, [iota_c, ijgT], [Bm])
            for t4 in range(TG // 4):
                P = nextq()
                for tt_ in range(4):
                    tl = t4 * 4 + tt_
                    kb.mm(P, P[:, tt_ * 128:(tt_ + 1) * 128], Bm[:, tl, :], A[:, tl, :], True, True, [Bm, A])
                t0 = tg * TG + t4 * 4
                kb.cp("act", CT[:, :, t0:t0 + 4].rearrange("j i t -> j t i"),
                      P[:, :].rearrange("j (t i) -> j t i", i=128), [P], [CT])
        if stop == "C7":
            kb.barrier()
            return nc
        PO = (Q[5], Q[6])
        acts2 = (p["acts"], p["acts1"])
        hT2 = (p["hT"], p["hT1"])
        sl_of = {}

        def emit_U(st_):
            g, half = st_ // 2, st_ % 2
            if half == 0:
                sl_of[g] = (get_slab(), get_slab())
            slU = sl_of[g][0]
            P = nextq()
            for o in range(4):
                il = half * 4 + o
                for dc in range(8):
                    off = dc * 1024 + il * 128
                    kb.mm(P, v4(P, n)[:, o, :], slU[:, off:off + 128], p["xn2T"][:, dc, :n], dc == 0, dc == 7, [slU, p["xn2T"]])
            i0_ = g * 8 + half * 4
            a_, h_ = acts2[st_ % 2], hT2[st_ % 2]
            kb.act(a_[:, :, :n], v4(P, n), AF.Gelu_apprx_tanh, [P], [a_])
            kb.tt("dve", h_[:, :, :n], a_[:, :, :n], CT[:, i0_:i0_ + 4, :n], ALU.mult, [a_, CT], [h_])

        def emit_V(st_):
            g, half = st_ // 2, st_ % 2
            slV = sl_of[g][1]
            h_ = hT2[st_ % 2]
            for o in range(4):
                i = g * 8 + half * 4 + o
                il = half * 4 + o
                for hv in range(2):
                    kb.mm(PO[hv], PO[hv][:n, :], h_[:, o, :n], slV[:, il * 1024 + hv * 512: il * 1024 + hv * 512 + 512],
                          i == 0, i == 127, [h_, slV])
            if half == 1:
                done_slab()

        for g in range(NGRP):
            emit_U(2 * g)
            emit_U(2 * g + 1)
            done_slab()
            emit_V(2 * g)
            emit_V(2 * g + 1)
        if stop == "C8":
            kb.barrier()
            return nc
        x2 = p["x2"]
        for hv in range(2):
            kb.tt("dve", x2[:n, hv * 512:(hv + 1) * 512], PO[hv][:n, :], x1[:n, hv * 512:(hv + 1) * 512], ALU.add, [PO[hv], x1], [x2])
        rmsnorm(x2, gfb, n, xnf)
        dst = ys if sample else yp[ti * 128:(ti + 1) * 128, :]
        kb.dma("pool", dst, xnf[:n, :], r=[xnf], st=xnf)
        kb.barrier(engines=("pe", "act", "dve", "pool"))
    kb.barrier()
    return nc


def _fm(v, c):
    return np.ascontiguousarray(np.asarray(v, np.float32).reshape(c, 128).T)


def make_in_maps(inp, NT=32, n_cores=8):
    f = lambda k: np.asarray(inp[k], np.float32)
    pfa = np.concatenate([
        _fm(f("mu_rkv")[0], 12), _fm(f("w0")[0], 4), _fm(f("a0")[0], 4), _fm(f("k_k")[0], 4), _fm(f("k_a")[0], 4),
        _fm(f("r_k")[0].reshape(-1), 4), _fm(f("gn_w")[0], 4), _fm(f("gn_b")[0], 4),
        _fm(f("conv_w")[0, 0], 4), _fm(f("conv_w")[0, 1], 4), _fm(f("conv_w")[0, 2], 4),
        _fm(f("mu_wag")[0, 0], 8), _fm(f("mu_wag")[0, 1], 8), _fm(f("mu_wag")[0, 2], 8)], axis=1)
    assert pfa.shape == (128, NPF)
    shared = dict(
        w_in=np.ascontiguousarray(f("w_in")[0]), w1=np.ascontiguousarray(f("w1")[0]), a1=np.ascontiguousarray(f("a1")[0]),
        g1=np.ascontiguousarray(f("g1")[0]), w2=np.ascontiguousarray(f("w2")[0]), a2=np.ascontiguousarray(f("a2")[0]),
        g2=np.ascontiguousarray(f("g2")[0]), w_pa=np.ascontiguousarray(f("w_pa")[0]), w_pb=np.ascontiguousarray(f("w_pb")[0]),
        w_o=np.ascontiguousarray(f("w_o")[0]), wq=np.ascontiguousarray(f("peer_wq")[0]),
        keysT=np.ascontiguousarray(f("peer_keys")[0].reshape(16, 128, 128).transpose(2, 0, 1)),
        UT=np.ascontiguousarray(f("peer_u")[0].T), V=np.ascontiguousarray(f("peer_v")[0]),
        pf=np.ascontiguousarray(pfa), mu_rkv_row=np.ascontiguousarray(f("mu_rkv")[0].reshape(1, 1536)),
        n1g=np.ascontiguousarray(f("norm1_g")[0].reshape(1, D)), n2g=np.ascontiguousarray(f("norm2_g")[0].reshape(1, D)),
        nfg=np.ascontiguousarray(f("norm_f_g").reshape(1, D)),
    )
    maps = []
    for c in range(n_cores):
        sl = slice(4 * c, 4 * c + 4)
        m = dict(shared)
        m["xp"] = np.ascontiguousarray(f("x_prompt")[c, :NT * 128])
        m["xs"] = np.ascontiguousarray(f("x_sample")[sl].reshape(64, D))
        m["st_shift"] = np.ascontiguousarray(f("state_shift")[0, sl].reshape(4, 8, 128).transpose(0, 2, 1))
        m["st_conv"] = np.ascontiguousarray(f("state_conv")[0, sl].reshape(4, 2, 4, 128).transpose(3, 2, 0, 1))
        m["st_wkv"] = np.ascontiguousarray(f("state_wkv")[0, sl].reshape(4, 4, 2, 64, 64).transpose(0, 2, 4, 1, 3).reshape(4, 128, 4, 64))
        maps.append(m)
    return maps


def assemble(results, NT=32, n_cores=8):
    T = NT * 128
    y_prompt = np.stack([r["yp"] for r in results]).reshape(n_cores, T, D)
    y_sample = np.concatenate([r["ys"].reshape(4, 16, D) for r in results], axis=0)
    conv = np.stack([r["conv_o"] for r in results])
    conv = conv.transpose(0, 3, 4, 2, 1).reshape(n_cores, 5, 2, 512)
    shift = np.stack([r["shift_o"] for r in results])
    wkv = np.stack([r["wkv_o"] for r in results])
    wkv = wkv.reshape(n_cores, 5, 2, 64, 4, 64).transpose(0, 1, 4, 2, 5, 3).reshape(n_cores, 5, 8, 64, 64)
    f32 = np.float32
    return (y_prompt.astype(f32), y_sample.astype(f32),
            np.ascontiguousarray(conv[:, 0])[None].astype(f32), np.ascontiguousarray(shift[:, 0])[None].astype(f32),
            np.ascontiguousarray(wkv[:, 0])[None].astype(f32),
            np.ascontiguousarray(conv[:, 1:].reshape(n_cores * 4, 2, 512))[None].astype(f32),
            np.ascontiguousarray(shift[:, 1:].reshape(n_cores * 4, D))[None].astype(f32),
            np.ascontiguousarray(wkv[:, 1:].reshape(n_cores * 4, 8, 64, 64))[None].astype(f32))


def kernel(**inputs):
    nc = build(32)
    maps = make_in_maps(inputs, 32)
    res = run_bass_kernel_spmd(nc, maps, core_ids=list(range(8)))
    return assemble(res.results, 32)
```
